# Optimizing a Trainium2 kernel written in Bass

```python
import numpy as np
import jax
import jax.numpy as jnp
from jax import lax

D_MODEL = 1024
BATCH = 16
SEQ = 2048
DEPTH = 2

HEAD_DIM = 64
NSA_HEADS = 8
NSA_KV_HEADS = 2
NSA_GROUP = NSA_HEADS // NSA_KV_HEADS
SB_HEADS = 8
CMP_BLOCK = 32
CMP_STRIDE = 16
SLC_BLOCK = 64
SLC_TOPK = 16
WINDOW = 512
WIN_QBLOCK = 128
SLC_QCHUNK = 64
SB_QBLOCK = 128
D_FF = 2816
CONV_WIDTH = 3
ROPE_THETA = 10000.0
RMS_EPS = 1e-6
NEG_INF = -1e30
FORCE_BONUS = 1e4

NSA_WIDTH = NSA_HEADS * HEAD_DIM
KV_WIDTH = NSA_KV_HEADS * HEAD_DIM
SB_WIDTH = SB_HEADS * HEAD_DIM
IN_SPLITS = (NSA_WIDTH, 6 * KV_WIDTH, 3 * NSA_HEADS, 3 * SB_WIDTH, 2 * D_MODEL)
IN_WIDTH = sum(IN_SPLITS)

kernel_name = 'nsa_stickbreak_gated_hybrid'


def rms_norm(x, gain):
    xf = x.astype(jnp.float32)
    y = xf * lax.rsqrt(jnp.mean(xf * xf, axis=-1, keepdims=True) + RMS_EPS)
    return y.astype(x.dtype) * gain


def rope(x, positions):
    half = x.shape[-1] // 2
    inv_freq = ROPE_THETA ** (-jnp.arange(half, dtype=jnp.float32) / half)
    ang = positions.astype(jnp.float32)[:, :, None] * inv_freq
    cos = jnp.cos(ang)[:, :, None, :]
    sin = jnp.sin(ang)[:, :, None, :]
    xf = x.astype(jnp.float32)
    x1, x2 = xf[..., :half], xf[..., half:]
    return jnp.concatenate([x1 * cos - x2 * sin, x2 * cos + x1 * sin], axis=-1).astype(x.dtype)


def compress_blocks(k, pe, w1, b1, w2):
    B, T, Hkv, hd = k.shape
    n_cmp = (T - CMP_BLOCK) // CMP_STRIDE + 1
    idx = np.arange(n_cmp)[:, None] * CMP_STRIDE + np.arange(CMP_BLOCK)[None, :]
    blocks = k[:, idx] + pe[None, None, :, None, :]
    flat = jnp.transpose(blocks, (0, 1, 3, 2, 4)).reshape(B, n_cmp, Hkv, CMP_BLOCK * hd)
    return jax.nn.gelu(flat @ w1 + b1, approximate=True) @ w2


def nsa_attention(q, k_cmp, v_cmp, k_slc, v_slc, k_win, v_win, gate_logits,
                  ck_pe, ck_w1, ck_b1, ck_w2, cv_pe, cv_w1, cv_b1, cv_w2):
    B, T, H, hd = q.shape
    scale = hd ** -0.5
    qg = q.reshape(B, T, NSA_KV_HEADS, NSA_GROUP, hd)
    t_idx = jnp.arange(T)

    kc = compress_blocks(k_cmp, ck_pe, ck_w1, ck_b1, ck_w2)
    vc = compress_blocks(v_cmp, cv_pe, cv_w1, cv_b1, cv_w2)
    n_cmp = kc.shape[1]
    cmp_end = jnp.arange(n_cmp) * CMP_STRIDE + CMP_BLOCK - 1
    cmp_mask = cmp_end[None, :] <= t_idx[:, None]
    s = jnp.einsum('bthgd,bnhd->bhgtn', qg, kc).astype(jnp.float32) * scale
    p_cmp = jax.nn.softmax(jnp.where(cmp_mask, s, NEG_INF), axis=-1) * cmp_mask
    o_cmp = jnp.einsum('bhgtn,bnhd->bthgd', p_cmp.astype(vc.dtype), vc)

    n_slc = T // SLC_BLOCK
    top_n = min(SLC_TOPK, n_slc)
    ci = np.arange(n_cmp)[:, None] * CMP_STRIDE
    sj = np.arange(n_slc)[None, :] * SLC_BLOCK
    overlap = ((ci < sj + SLC_BLOCK) & (sj < ci + CMP_BLOCK)).astype(np.float32)
    p_slc = jnp.einsum('bhtn,nj->bhtj', p_cmp.sum(axis=2), jnp.asarray(overlap))
    blk = jnp.arange(n_slc)
    cur = t_idx // SLC_BLOCK
    blk_valid = blk[None, :] * SLC_BLOCK <= t_idx[:, None]
    forced = (blk[None, :] == 0) | (blk[None, :] == cur[:, None]) | (blk[None, :] == cur[:, None] - 1)
    score = jnp.where(blk_valid, p_slc + jnp.where(forced, FORCE_BONUS, 0.0), -1.0)
    top_score, top_idx = lax.top_k(score, top_n)
    top_valid = top_score >= 0.0

    ks_blk = jnp.transpose(k_slc.reshape(B, n_slc, SLC_BLOCK, NSA_KV_HEADS, hd), (0, 3, 1, 2, 4))
    vs_blk = jnp.transpose(v_slc.reshape(B, n_slc, SLC_BLOCK, NSA_KV_HEADS, hd), (0, 3, 1, 2, 4))
    b_ix = jnp.arange(B)[:, None, None, None]
    h_ix = jnp.arange(NSA_KV_HEADS)[None, :, None, None]
    n_chunk = T // SLC_QCHUNK

    def slc_chunk(args):
        qc, idx_c, ok_c, t_c = args
        kg = ks_blk[b_ix, h_ix, idx_c]
        vg = vs_blk[b_ix, h_ix, idx_c]
        tok = idx_c[..., None] * SLC_BLOCK + jnp.arange(SLC_BLOCK)
        mask = ok_c[..., None] & (tok <= t_c[None, None, :, None, None])
        sc = jnp.einsum('bqhgd,bhqnld->bhgqnl', qc, kg).astype(jnp.float32) * scale
        sc = jnp.where(mask[:, :, None], sc, NEG_INF)
        p = jax.nn.softmax(sc.reshape(*sc.shape[:4], -1), axis=-1).reshape(sc.shape)
        return jnp.einsum('bhgqnl,bhqnld->bqhgd', p.astype(vg.dtype), vg)

    xs = (jnp.swapaxes(qg.reshape(B, n_chunk, SLC_QCHUNK, NSA_KV_HEADS, NSA_GROUP, hd), 0, 1),
          jnp.moveaxis(top_idx.reshape(B, NSA_KV_HEADS, n_chunk, SLC_QCHUNK, top_n), 2, 0),
          jnp.moveaxis(top_valid.reshape(B, NSA_KV_HEADS, n_chunk, SLC_QCHUNK, top_n), 2, 0),
          t_idx.reshape(n_chunk, SLC_QCHUNK))
    o_slc = jnp.swapaxes(lax.map(slc_chunk, xs), 0, 1).reshape(B, T, NSA_KV_HEADS, NSA_GROUP, hd)

    n_wblk = T // WIN_QBLOCK
    span = WINDOW + WIN_QBLOCK
    kw_pad = jnp.pad(k_win, ((0, 0), (WINDOW, 0), (0, 0), (0, 0)))
    vw_pad = jnp.pad(v_win, ((0, 0), (WINDOW, 0), (0, 0), (0, 0)))

    def win_block(i):
        start = i * WIN_QBLOCK
        qb = lax.dynamic_slice_in_dim(qg, start, WIN_QBLOCK, axis=1)
        kb = lax.dynamic_slice_in_dim(kw_pad, start, span, axis=1)
        vb = lax.dynamic_slice_in_dim(vw_pad, start, span, axis=1)
        q_pos = start + jnp.arange(WIN_QBLOCK)
        k_pos = start - WINDOW + jnp.arange(span)
        diff = q_pos[:, None] - k_pos[None, :]
        mask = (diff >= 0) & (diff < WINDOW) & (k_pos[None, :] >= 0)
        sc = jnp.einsum('bqhgd,bkhd->bhgqk', qb, kb).astype(jnp.float32) * scale
        p = jax.nn.softmax(jnp.where(mask, sc, NEG_INF), axis=-1)
        return jnp.einsum('bhgqk,bkhd->bqhgd', p.astype(vb.dtype), vb)

    o_win = jnp.swapaxes(lax.map(win_block, jnp.arange(n_wblk)), 0, 1).reshape(B, T, NSA_KV_HEADS, NSA_GROUP, hd)

    g = jax.nn.sigmoid(gate_logits.reshape(B, T, NSA_KV_HEADS, NSA_GROUP, 3))
    o = g[..., 0:1] * o_cmp + g[..., 1:2] * o_slc + g[..., 2:3] * o_win
    return o.reshape(B, T, H * hd)


def stick_breaking_attention(q, k, v):
    B, T, H, hd = q.shape
    scale = hd ** -0.5
    outs = []
    for i in range(T // SB_QBLOCK):
        q0, q1 = i * SB_QBLOCK, (i + 1) * SB_QBLOCK
        z = jnp.einsum('bqhd,bkhd->bhqk', q[:, q0:q1], k[:, :q1]).astype(jnp.float32) * scale
        q_pos = jnp.arange(q0, q1)
        k_pos = jnp.arange(q1)
        strict = k_pos[None, :] < q_pos[:, None]
        log_1m = jnp.where(strict, -jax.nn.softplus(z), 0.0)
        later = lax.cumsum(log_1m, axis=3, reverse=True) - log_1m
        w = jnp.where(strict, jnp.exp(jax.nn.log_sigmoid(z) + later), 0.0)
        outs.append(jnp.einsum('bhqk,bkhd->bqhd', w.astype(v.dtype), v[:, :q1]))
    return jnp.concatenate(outs, axis=1)


def hybrid_mixer(h, positions, w_in, ck_pe, ck_w1, ck_b1, ck_w2, cv_pe, cv_w1, cv_b1, cv_w2,
                 w_proj_nsa, w_proj_sb, w_out):
    B, T, _ = h.shape
    cuts = np.cumsum(IN_SPLITS)[:-1].tolist()
    q_nsa, kv_nsa, g_nsa, qkv_sb, g_merge = jnp.split(h @ w_in, cuts, axis=-1)
    q_nsa = rope(q_nsa.reshape(B, T, NSA_HEADS, HEAD_DIM), positions)
    kv = kv_nsa.reshape(B, T, 6, NSA_KV_HEADS, HEAD_DIM)
    k_cmp = rope(kv[:, :, 0], positions)
    v_cmp = kv[:, :, 1]
    k_slc = rope(kv[:, :, 2], positions)
    v_slc = kv[:, :, 3]
    k_win = rope(kv[:, :, 4], positions)
    v_win = kv[:, :, 5]
    o_nsa = nsa_attention(q_nsa, k_cmp, v_cmp, k_slc, v_slc, k_win, v_win, g_nsa,
                          ck_pe, ck_w1, ck_b1, ck_w2, cv_pe, cv_w1, cv_b1, cv_w2)
    qkv = qkv_sb.reshape(B, T, 3, SB_HEADS, HEAD_DIM)
    o_sb = stick_breaking_attention(qkv[:, :, 0], qkv[:, :, 1], qkv[:, :, 2]).reshape(B, T, SB_WIDTH)
    g_a, g_b = jnp.split(jax.nn.sigmoid(g_merge), 2, axis=-1)
    y = g_a * (o_nsa @ w_proj_nsa) + g_b * (o_sb @ w_proj_sb)
    return y @ w_out


def conv_ffn(h, w_up, conv_w, conv_b, w_down):
    u = h @ w_up
    c = u.shape[-1]
    u = lax.conv_general_dilated(u, conv_w[:, None, :], window_strides=(1,),
                                 padding=[(CONV_WIDTH - 1, 0)],
                                 dimension_numbers=('NWC', 'WIO', 'NWC'),
                                 feature_group_count=c) + conv_b
    gate, up = jnp.split(u, 2, axis=-1)
    return (jax.nn.gelu(gate, approximate=True) * up) @ w_down


def setup_inputs(seed: int = 0) -> dict:
    key = jax.random.key(seed)
    ks = jax.random.split(key, 24)
    f32 = jnp.float32

    def nrm(k, shape, scale):
        return jax.random.normal(k, shape, f32) * scale

    def gain(k):
        return 1.0 + 0.02 * jax.random.normal(k, (DEPTH, D_MODEL), f32)

    cmp_in = CMP_BLOCK * HEAD_DIM
    return {
        'x': nrm(ks[0], (BATCH, SEQ, D_MODEL), 1.0),
        'positions': jnp.broadcast_to(jnp.arange(SEQ, dtype=jnp.int32), (BATCH, SEQ)),
        'g_pre_mix': gain(ks[1]),
        'w_in': nrm(ks[2], (DEPTH, D_MODEL, IN_WIDTH), D_MODEL ** -0.5),
        'ck_pe': nrm(ks[3], (DEPTH, CMP_BLOCK, HEAD_DIM), 0.02),
        'ck_w1': nrm(ks[4], (DEPTH, cmp_in, HEAD_DIM), cmp_in ** -0.5),
        'ck_b1': nrm(ks[5], (DEPTH, HEAD_DIM), 0.01),
        'ck_w2': nrm(ks[6], (DEPTH, HEAD_DIM, HEAD_DIM), HEAD_DIM ** -0.5),
        'cv_pe': nrm(ks[7], (DEPTH, CMP_BLOCK, HEAD_DIM), 0.02),
        'cv_w1': nrm(ks[8], (DEPTH, cmp_in, HEAD_DIM), cmp_in ** -0.5),
        'cv_b1': nrm(ks[9], (DEPTH, HEAD_DIM), 0.01),
        'cv_w2': nrm(ks[10], (DEPTH, HEAD_DIM, HEAD_DIM), HEAD_DIM ** -0.5),
        'w_proj_nsa': nrm(ks[11], (DEPTH, NSA_WIDTH, D_MODEL), NSA_WIDTH ** -0.5),
        'w_proj_sb': nrm(ks[12], (DEPTH, SB_WIDTH, D_MODEL), SB_WIDTH ** -0.5),
        'w_out': nrm(ks[13], (DEPTH, D_MODEL, D_MODEL), D_MODEL ** -0.5),
        'g_post_mix': gain(ks[14]),
        'g_pre_ffn': gain(ks[15]),
        'w_up': nrm(ks[16], (DEPTH, D_MODEL, 2 * D_FF), D_MODEL ** -0.5),
        'conv_w': nrm(ks[17], (DEPTH, CONV_WIDTH, 2 * D_FF), CONV_WIDTH ** -0.5),
        'conv_b': nrm(ks[18], (DEPTH, 2 * D_FF), 0.01),
        'w_down': nrm(ks[19], (DEPTH, D_FF, D_MODEL), D_FF ** -0.5),
        'g_post_ffn': gain(ks[20]),
    }


def reference(x, positions, g_pre_mix, w_in, ck_pe, ck_w1, ck_b1, ck_w2, cv_pe, cv_w1, cv_b1, cv_w2,
              w_proj_nsa, w_proj_sb, w_out, g_post_mix, g_pre_ffn, w_up, conv_w, conv_b, w_down, g_post_ffn):
    for l in range(DEPTH):
        h = rms_norm(x, g_pre_mix[l])
        m = hybrid_mixer(h, positions, w_in[l], ck_pe[l], ck_w1[l], ck_b1[l], ck_w2[l],
                         cv_pe[l], cv_w1[l], cv_b1[l], cv_w2[l], w_proj_nsa[l], w_proj_sb[l], w_out[l])
        x = x + rms_norm(m, g_post_mix[l])
        h = rms_norm(x, g_pre_ffn[l])
        f = conv_ffn(h, w_up[l], conv_w[l], conv_b[l], w_down[l])
        x = x + rms_norm(f, g_post_ffn[l])
    return x
```

```python
import numpy as np
from contextlib import ExitStack
import concourse.bass as bass
import concourse.mybir as mybir
from concourse.bass_utils import run_bass_kernel_spmd

F32 = mybir.dt.float32
BF16 = mybir.dt.bfloat16
I32 = mybir.dt.int32
ALU = mybir.AluOpType
AF = mybir.ActivationFunctionType
AX = mybir.AxisListType

T = 2048
D = 1024
NT = 16
HD = 64
DFF = 2816
NCP = 22
SCALE = 0.125
EPS = 1e-6
NEG = -30000.0
N_CORES = 8
NB = 2
DEPTH = 2
NCMP = 127
PI = 3.14159265358979

CB_IDENT, CB_TRI, CB_ONES, CB_MSTRICT, CB_MCAUSAL, CB_MWINLOW = 0, 128, 256, 384, 512, 640
CB_E = 768
CB_CMPM = CB_E + 2048
CB_N = CB_CMPM + 2048
CF_OV = 0
CF_BV = 32
CF_VALID = CF_BV + 512
CF_INVF = CF_VALID + 512
CF_SGN = CF_INVF + 1
CF_N = CF_SGN + 1


class Buf:
    __slots__ = ("name", "w", "r")

    def __init__(self, name=""):
        self.name = name
        self.w = None
        self.r = {}


def bufs(n, name=""):
    return [Buf(f"{name}{i}") for i in range(n)]


class KB:
    NDMA = 32

    def __init__(self, nc, es):
        self.nc = nc
        self.engs = {"pe": nc.tensor, "act": nc.scalar, "dve": nc.vector, "pool": nc.gpsimd, "sp": nc.sync}
        self.sems = {}
        for e in ("pe", "act", "dve", "pool"):
            self.sems[e] = es.enter_context(nc.semaphore("sem_" + e))
        self.cnt = {e: 0 for e in ("pe", "act", "dve", "pool")}
        self.seen = {e: {} for e in self.engs}
        for i in range(self.NDMA):
            self.sems[("d", i)] = es.enter_context(nc.semaphore(f"semd{i}"))
        self.dma_val = [0] * self.NDMA
        self.dma_next = 0
        self.n_inst = 0
        self.n_wait = 0

    def _wait(self, e, deps):
        eng = self.engs[e]
        seen = self.seen[e]
        for k, v in deps:
            if seen.get(k, 0) < v:
                eng.wait_ge(self.sems[k], v)
                seen[k] = v
                self.n_wait += 1

    def op(self, e, fn, reads=(), writes=()):
        deps = []
        for b in reads:
            if b.w is not None:
                deps.append(b.w)
        for b in writes:
            if b.w is not None and b.w[0] != e:
                deps.append(b.w)
            for k, v in b.r.items():
                if k != e:
                    deps.append((k, v))
        self._wait(e, deps)
        ins = fn()
        self.cnt[e] += 1
        t = self.cnt[e]
        ins.then_inc(self.sems[e], 1)
        self.n_inst += 1
        for b in writes:
            b.w = (e, t)
            b.r = {}
        for b in reads:
            if b.r.get(e, 0) < t:
                b.r[e] = t
        return ins

    def dma(self, q, out, in_, reads=(), writes=()):
        i = self.dma_next
        self.dma_next = (i + 1) % self.NDMA
        key = ("d", i)
        deps = []
        if self.dma_val[i] > 0:
            deps.append((key, self.dma_val[i]))
        for b in reads:
            if b.w is not None:
                deps.append(b.w)
        for b in writes:
            if b.w is not None:
                deps.append(b.w)
            deps.extend(b.r.items())
        self._wait(q, deps)
        ins = self.engs[q].dma_start(out=out, in_=in_)
        self.dma_val[i] += 16
        v = self.dma_val[i]
        ins.then_inc(self.sems[key], 16)
        self.n_inst += 1
        for b in writes:
            b.w = (key, v)
            b.r = {}
        for b in reads:
            b.r[key] = v

    def barrier(self):
        deps = [(e, c) for e, c in self.cnt.items() if c > 0]
        deps += [(("d", i), v) for i, v in enumerate(self.dma_val) if v > 0]
        for e in self.engs:
            self._wait(e, [d for d in deps if d[0] != e])

    def mm(self, out, lhsT, rhs, start, stop, reads, writes):
        return self.op("pe", lambda: self.nc.tensor.matmul(out, lhsT=lhsT, rhs=rhs, start=start, stop=stop),
                       reads, writes)

    def tr(self, out, in_, ident, reads, writes):
        return self.op("pe", lambda: self.nc.tensor.transpose(out, in_, ident), reads, writes)

    def act(self, out, in_, func, reads, writes, **kw):
        return self.op("act", lambda: self.nc.scalar.activation(out=out, in_=in_, func=func, **kw), reads, writes)

    def tt(self, e, out, in0, in1, op, reads, writes):
        return self.op(e, lambda: self.engs[e].tensor_tensor(out=out, in0=in0, in1=in1, op=op), reads, writes)

    def ts(self, e, out, in0, s1, s2, op0, op1, reads, writes):
        if op1 is None:
            return self.op(e, lambda: self.engs[e].tensor_scalar(out=out, in0=in0, scalar1=s1, scalar2=None, op0=op0),
                           reads, writes)
        return self.op(e, lambda: self.engs[e].tensor_scalar(out=out, in0=in0, scalar1=s1, scalar2=s2, op0=op0, op1=op1),
                       reads, writes)

    def stt(self, e, out, in0, scalar, in1, op0, op1, reads, writes):
        return self.op(e, lambda: self.engs[e].scalar_tensor_tensor(out=out, in0=in0, scalar=scalar, in1=in1,
                                                                      op0=op0, op1=op1), reads, writes)

    def copy(self, e, out, in_, reads, writes):
        if e == "act":
            return self.op(e, lambda: self.nc.scalar.copy(out=out, in_=in_), reads, writes)
        return self.op(e, lambda: self.engs[e].tensor_copy(out=out, in_=in_), reads, writes)

    def memset(self, e, ap, val, writes):
        return self.op(e, lambda: self.engs[e].memset(ap, val), (), writes)


def run_pipelined(gens, depth):
    active = []
    gens = iter(gens)
    exhausted = False
    while True:
        while not exhausted and len(active) < depth:
            try:
                active.append(next(gens))
            except StopIteration:
                exhausted = True
        if not active:
            break
        for g in list(active):
            try:
                next(g)
            except StopIteration:
                active.remove(g)


def _swap_halves(cols):
    return cols.reshape(-1, 2, 32)[:, ::-1, :].reshape(-1)


def w_in_columns():
    def qn(h):
        return np.arange(64 * h, 64 * h + 64)

    def kv(i, h):
        return 512 + i * 128 + h * 64 + np.arange(64)

    def sb(j, h):
        return 1304 + j * 512 + h * 64 + np.arange(64)

    fm = []
    for p in range(4):
        c = np.concatenate([qn(2 * p), qn(2 * p + 1)])
        fm += [c, _swap_halves(c)]
    for i in (0, 2, 4):
        c = np.concatenate([kv(i, 0), kv(i, 1)])
        fm += [c, _swap_halves(c)]
    fm.append(np.concatenate([kv(1, 0), kv(1, 1)]))
    for j in (0, 1):
        for p in range(4):
            fm.append(np.concatenate([sb(j, 2 * p), sb(j, 2 * p + 1)]))
    fm.append(np.full(128, -1))
    assert len(fm) == 24
    cols = np.concatenate(fm)
    g0 = np.concatenate([kv(3, 0), kv(3, 1), kv(5, 0), kv(5, 1), 1280 + np.arange(24), np.full(512 - 280, -1)])
    g1 = 1304 + 2 * 512 + np.arange(512)
    g25 = 2840 + np.arange(2048)
    cols = np.concatenate([cols, g0, g1, g25])
    assert cols.shape[0] == 12 * 512
    return cols


def prep_weights(inp):
    L = DEPTH
    f = np.float32
    out = {}
    cols = w_in_columns()
    valid = cols >= 0
    win = np.zeros((L, 1024, 12 * 512), f)
    for l in range(L):
        win[l][:, valid] = inp["w_in"][l][:, cols[valid]]
    out["win"] = np.ascontiguousarray(win.reshape(L, 8, 128, 12, 512).transpose(0, 3, 2, 1, 4))
    for nm, src in (("wck1", "ck_w1"), ("wcv1", "cv_w1")):
        out[nm] = np.ascontiguousarray(inp[src].reshape(L, 32, 64, 64).transpose(0, 2, 1, 3).reshape(L, 64, 2048))
    out["peT"] = np.ascontiguousarray(np.stack([inp["ck_pe"], inp["cv_pe"]], 1).transpose(0, 1, 3, 2))
    out["w2"] = np.ascontiguousarray(np.stack([inp["ck_w2"], inp["cv_w2"]], 1))
    out["b1"] = np.ascontiguousarray(np.stack([inp["ck_b1"], inp["cv_b1"]], 2))
    out["wpn"] = np.ascontiguousarray(inp["w_proj_nsa"].reshape(L, 4, 128, 1024).transpose(0, 2, 1, 3))
    out["wps"] = np.ascontiguousarray(inp["w_proj_sb"].reshape(L, 4, 128, 1024).transpose(0, 2, 1, 3))
    out["wout"] = np.ascontiguousarray(inp["w_out"].reshape(L, 8, 128, 1024).transpose(0, 2, 1, 3))
    wup = np.zeros((L, NCP, 128, 8, 256), f)
    for cp in range(NCP):
        c = np.concatenate([cp * 128 + np.arange(128), DFF + cp * 128 + np.arange(128)])
        wup[:, cp] = inp["w_up"][:, :, c].reshape(L, 8, 128, 256).transpose(0, 2, 1, 3)
    out["wup"] = wup
    out["wdn"] = np.ascontiguousarray(inp["w_down"].reshape(L, NCP, 128, 1024).transpose(0, 2, 1, 3))
    out["cw"] = np.ascontiguousarray(inp["conv_w"].transpose(0, 2, 1).reshape(L, 44, 128, 3).transpose(0, 2, 1, 3))
    out["cb"] = np.ascontiguousarray(inp["conv_b"].reshape(L, 44, 128).transpose(0, 2, 1))
    out["gfm"] = np.ascontiguousarray(
        np.stack([inp["g_pre_mix"], inp["g_pre_ffn"]], 1).reshape(L, 2, 8, 128).transpose(0, 1, 3, 2))
    out["gbc"] = np.ascontiguousarray(np.stack([inp["g_post_mix"], inp["g_post_ffn"]], 1))
    return {k: np.ascontiguousarray(v, dtype=f) for k, v in out.items()}


def const_tables():
    p = np.arange(128)
    cb = np.zeros((128, CB_N), np.float32)
    cb[:, CB_IDENT:CB_IDENT + 128] = np.eye(128)
    cb[:, CB_TRI:CB_TRI + 128] = -1.0 * (p[:, None] >= p[None, :])
    cb[:, CB_ONES:CB_ONES + 128] = -1.0
    cb[:, CB_MSTRICT:CB_MSTRICT + 128] = p[:, None] < p[None, :]
    cb[:, CB_MCAUSAL:CB_MCAUSAL + 128] = p[:, None] <= p[None, :]
    cb[:, CB_MWINLOW:CB_MWINLOW + 128] = p[:, None] > p[None, :]
    s = np.arange(T)
    cb[:32, CB_E:CB_E + T] = (s[None, :] // 64) == np.arange(32)[:, None]
    n = np.arange(NCMP)
    cb[:NCMP, CB_CMPM:CB_CMPM + T] = (16 * n[:, None] + 31) <= s[None, :]
    cf = np.zeros((128, CF_N), np.float32)
    ci = n[:, None] * 16
    sj = np.arange(32)[None, :] * 64
    cf[:NCMP, CF_OV:CF_OV + 32] = (ci < sj + 64) & (sj < ci + 32)
    t = (np.arange(NT)[None, :, None] * 128 + p[:, None, None])
    j = np.arange(32)[None, None, :]
    valid = (j * 64 <= t)
    cur = t // 64
    forced = (j == 0) | (j == cur) | (j == cur - 1)
    bv = np.where(forced, 1e4, 0.0) + (valid.astype(np.float32) - 1.0)
    cf[:, CF_BV:CF_BV + 512] = bv.reshape(128, 512)
    cf[:, CF_VALID:CF_VALID + 512] = valid.reshape(128, 512)
    half = 32
    inv_freq = (10000.0 ** (-np.arange(half, dtype=np.float32) / half)).astype(np.float32)
    cf[:, CF_INVF] = inv_freq[p % 32]
    cf[:, CF_SGN] = np.where((p % 64) < 32, -1.0, 1.0)
    return cb, cf


class Prog:
    def __init__(self, nb=NB, layers=(0, 1), stop_after=None, dbg=False):
        self.nb = nb
        self.layers = layers
        self.stop_after = stop_after
        self.dbg = dbg
        self.nc = bass.Bass("TRN2", target_bir_lowering=False)
        self.uid = 0
        self.norm_tmp = {}
        self.build()

    def S(self, name, shape, dt):
        self.uid += 1
        return self.nc.sbuf_tensor(f"{name}_u{self.uid}", list(shape), dt)

    def P(self, name, shape, dt):
        self.uid += 1
        return self.nc.psum_tensor(f"{name}_u{self.uid}", list(shape), dt)

    def dram(self, name, shape, dt, kind="ExternalInput"):
        return self.nc.dram_tensor(name, list(shape), dt, kind=kind).ap()

    def build(self):
        nc = self.nc
        L = DEPTH
        nb = self.nb
        skind = "ExternalOutput" if self.dbg else "Internal"
        d = {}
        d["x"] = self.dram("x", [nb, T, D], F32)
        d["pos"] = self.dram("pos", [nb, T], I32)
        d["win"] = self.dram("win", [L, 12, 128, 8, 512], F32)
        d["wck1"] = self.dram("wck1", [L, 64, 2048], F32)
        d["wcv1"] = self.dram("wcv1", [L, 64, 2048], F32)
        d["peT"] = self.dram("peT", [L, 2, 64, 32], F32)
        d["w2"] = self.dram("w2", [L, 2, 64, 64], F32)
        d["b1"] = self.dram("b1", [L, 64, 2], F32)
        d["wpn"] = self.dram("wpn", [L, 128, 4, 1024], F32)
        d["wps"] = self.dram("wps", [L, 128, 4, 1024], F32)
        d["wout"] = self.dram("wout", [L, 128, 8, 1024], F32)
        d["wup"] = self.dram("wup", [L, NCP, 128, 8, 256], F32)
        d["wdn"] = self.dram("wdn", [L, 128, NCP, 1024], F32)
        d["cw"] = self.dram("cw", [L, 128, 44, 3], F32)
        d["cb"] = self.dram("cb", [L, 128, 44], F32)
        d["gfm"] = self.dram("gfm", [L, 2, 128, 8], F32)
        d["gbc"] = self.dram("gbc", [L, 2, 1024], F32)
        d["cstb"] = self.dram("cstb", [128, CB_N], F32)
        d["cstf"] = self.dram("cstf", [128, CF_N], F32)
        d["y"] = self.dram("y", [nb, T, D], F32, kind="ExternalOutput")
        d["qTn"] = self.dram("s_qTn", [8, 64, T], BF16, kind=skind)
        d["kTc"] = self.dram("s_kTc", [2, 64, T], BF16, kind=skind)
        d["vTc"] = self.dram("s_vTc", [2, 64, T], BF16, kind=skind)
        d["kTs"] = self.dram("s_kTs", [2, 64, T], BF16, kind=skind)
        d["kTw"] = self.dram("s_kTw", [2, 64, T], BF16, kind=skind)
        d["qTsb"] = self.dram("s_qTsb", [8, 64, T], BF16, kind=skind)
        d["kTsb"] = self.dram("s_kTsb", [8, 64, T], BF16, kind=skind)
        d["vS"] = self.dram("s_vS", [T, 128], BF16, kind=skind)
        d["vW"] = self.dram("s_vW", [T, 128], BF16, kind=skind)
        d["vSB"] = self.dram("s_vSB", [T, 512], BF16, kind=skind)
        d["gm"] = self.dram("s_gm", [T, 2048], BF16, kind=skind)
        self.d = d
        sb = {}
        sb["qTn"] = bufs(8, "qTn")
        for k in ("kTc", "vTc", "kTs", "kTw"):
            sb[k] = bufs(2, k)
        sb["qTsb"] = bufs(8, "qTsb")
        sb["kTsb"] = bufs(8, "kTsb")
        for k in ("vS", "vW", "vSB"):
            sb[k] = bufs(NT, k)
        sb["gm"] = [bufs(4, f"gm{i}_") for i in range(NT)]
        self.sbuf = sb
        if self.dbg:
            d["dbg_hT"] = self.dram("dbg_hT", [128, 8, T], BF16, kind="ExternalOutput")
            d["dbg_gn"] = self.dram("dbg_gn", [128, NT, 24], F32, kind="ExternalOutput")
            d["dbg_onsa"] = self.dram("dbg_onsa", [128, NT, 512], F32, kind="ExternalOutput")
            d["dbg_osb"] = self.dram("dbg_osb", [128, NT, 512], BF16, kind="ExternalOutput")
            d["dbg_selT"] = self.dram("dbg_selT", [32, 2, T], BF16, kind="ExternalOutput")

        with ExitStack() as es:
            self.es = es
            kb = KB(nc, es)
            self.kb = kb
            self.x = es.enter_context(self.S("x_sb", [128, NT, D], F32))
            self.xb = bufs(NT, "x")
            self.cstb = es.enter_context(self.S("cstb_sb", [128, CB_E], BF16))
            self.cstf = es.enter_context(self.S("cstf_sb", [128, CF_N], F32))
            self.cb_buf = Buf("cstb")
            self.cf_buf = Buf("cstf")
            kb.dma("pool", self.cstb[:], d["cstb"][:, 0:CB_E], (), [self.cb_buf])
            kb.dma("sp", self.cstf[:], d["cstf"][:, :], (), [self.cf_buf])
            self.epsc = es.enter_context(self.S("epsc", [128, 1], F32))
            kb.memset("pool", self.epsc[:], EPS, [Buf("eps")])
            for b in range(nb):
                for i in range(NT):
                    kb.dma("sp", self.x[:, i, :], d["x"][b, i * 128:(i + 1) * 128, :], (), [self.xb[i]])
                for l in self.layers:
                    self.layer(b, l)
                for i in range(NT):
                    kb.dma("sp", d["y"][b, i * 128:(i + 1) * 128, :], self.x[:, i, :], [self.xb[i]], ())
            kb.barrier()

    def cbs(self, off, n=128, rows=128):
        return self.cstb[0:rows, off:off + n]

    def stop(self, name):
        return self.stop_after == name

    def layer(self, b, l):
        kb = self.kb
        nc = self.nc
        done = False
        with ExitStack() as ls:
            self.gn = ls.enter_context(self.S("gn", [128, NT, 24], F32))
            self.gnb = bufs(NT, "gn")
            self.phase_proj(b, l)
            if self.dbg:
                kb.dma("sp", self.d["dbg_gn"][:, :, :], self.gn[:], self.gnb, ())
            if not self.stop("proj"):
                self.phase_attn(b, l, ls)
                if not self.stop("attn") and not self.stop("nsa"):
                    self.phase_merge(b, l)
                    done = not self.stop("merge")
            kb.barrier()
        if done:
            self.phase_ffn(b, l)

    def norm_tiles(self, ps, tiles, gfm_ap, hT, hT_bufs, psum_t, psum_bufs, tag, joff=0):
        kb = self.kb
        nc = self.nc
        n = len(tiles)
        if tag not in self.norm_tmp:
            self.norm_tmp[tag] = dict(
                ss=ps.enter_context(self.S(f"ss_{tag}", [128, NT], F32)),
                rstd=ps.enter_context(self.S(f"rstd_{tag}", [128, NT], F32)),
                junk=ps.enter_context(self.S(f"junk_{tag}", [128, D], BF16)),
                hb=ps.enter_context(self.S(f"hb_{tag}", [128, 2, D], BF16)),
                gfm=ps.enter_context(self.S(f"gfm_{tag}", [128, 8], F32)),
                bufs=(Buf(), Buf(), Buf(), Buf(), bufs(2)))
        tm = self.norm_tmp[tag]
        ss, rstd, junk, hb, gfm = tm["ss"], tm["rstd"], tm["junk"], tm["hb"], tm["gfm"]
        b_ss, b_rstd, b_junk, b_g, b_hb = tm["bufs"]
        kb.dma("sp", gfm[:], gfm_ap, (), [b_g])
        kb.memset("dve", ss[:], 0.0, [b_ss])
        for j, i in enumerate(tiles):
            kb.act(junk[:], self.x[:, i, :], AF.Square, [self.xb[i]], [b_junk, b_ss], accum_out=ss[:, j:j + 1])
        kb.ts("dve", rstd[:, 0:n], ss[:, 0:n], 1.0 / D, EPS, ALU.mult, ALU.add, [b_ss], [b_rstd])
        kb.act(rstd[:, 0:n], rstd[:, 0:n], AF.Sqrt, [b_rstd], [b_rstd])
        kb.op("dve", lambda: nc.vector.reciprocal(out=rstd[:, 0:n], in_=rstd[:, 0:n]), [b_rstd], [b_rstd])
        ident = self.cbs(CB_IDENT)
        for j, i in enumerate(tiles):
            s = j % 2
            kb.act(hb[:, s, :], self.x[:, i, :], AF.Copy, [self.xb[i], b_rstd], [b_hb[s]], scale=rstd[:, j:j + 1])
            pt = psum_t[s]
            for c in range(8):
                kb.tr(pt[:, c * 128:(c + 1) * 128], hb[:, s, c * 128:(c + 1) * 128], ident,
                      [b_hb[s], self.cb_buf], [psum_bufs[s]])
            kb.tt("dve", hT[:, :, (joff + j) * 128:(joff + j + 1) * 128], pt[:].rearrange("p (c t) -> p c t", c=8),
                  gfm[:].unsqueeze(2).broadcast_to([128, 8, 128]), ALU.mult,
                  [psum_bufs[s], b_g], [hT_bufs[joff + j]])

    def phase_proj(self, b, l):
        kb = self.kb
        nc = self.nc
        d = self.d
        sbf = self.sbuf
        with ExitStack() as ps:
            hT = ps.enter_context(self.S("hT", [128, 8, T], BF16))
            hTb = bufs(NT, "hT")
            cos2 = ps.enter_context(self.S("cos2", [128, T], F32))
            sinpm = ps.enter_context(self.S("sinpm", [128, T], F32))
            b_cos, b_sin = Buf("cos"), Buf("sin")
            pst = [ps.enter_context(self.P(f"pst{i}", [128, 1024], BF16)) for i in range(2)]
            pstb = bufs(2, "pst")
            psm = [ps.enter_context(self.P(f"psm{i}", [128, 512], F32)) for i in range(6)]
            psmb = bufs(6, "psm")
            with ExitStack() as ps2:
                posi = ps2.enter_context(self.S("posi", [128, T], I32))
                ang = ps2.enter_context(self.S("ang", [128, T], F32))
                tmpf = ps2.enter_context(self.S("tmpf", [128, T], F32))
                tmpi = ps2.enter_context(self.S("tmpi", [128, T], I32))
                b_posi, b_ang, b_tf, b_ti = Buf(), Buf(), Buf(), Buf()
                kb.dma("sp", posi[:], d["pos"][b:b + 1, :].partition_broadcast(128), (), [b_posi])
                kb.copy("dve", ang[:], posi[:], [b_posi], [b_ang])
                kb.ts("dve", ang[:], ang[:], self.cstf[:, CF_INVF:CF_INVF + 1], None, ALU.mult, None,
                      [b_ang, self.cf_buf], [b_ang])
                for which in ("sin", "cos"):
                    dst, db = (sinpm, b_sin) if which == "sin" else (cos2, b_cos)
                    if which == "cos":
                        kb.ts("dve", ang[:], ang[:], PI / 2, None, ALU.add, None, [b_ang], [b_ang])
                    kb.ts("dve", tmpi[:], ang[:], 1.0 / (2 * PI), None, ALU.mult, None, [b_ang], [b_ti])
                    kb.copy("dve", tmpf[:], tmpi[:], [b_ti], [b_tf])
                    kb.stt("dve", tmpf[:], tmpf[:], -2 * PI, ang[:], ALU.mult, ALU.add, [b_tf, b_ang], [b_tf])
                    kb.ts("dve", tmpf[:], tmpf[:], -3.1415925, 3.1415925, ALU.max, ALU.min, [b_tf], [b_tf])
                    kb.act(dst[:], tmpf[:], AF.Sin, [b_tf], [db])
                kb.ts("dve", sinpm[:], sinpm[:], self.cstf[:, CF_SGN:CF_SGN + 1], None, ALU.mult, None,
                      [b_sin, self.cf_buf], [b_sin])
                kb.barrier()
            ntag = f"p{self.uid}"
            wbuf = [ps.enter_context(self.S(f"wbuf{i}", [128, 8, 512], BF16)) for i in range(3)]
            wb = bufs(3, "wbuf")
            stg = [ps.enter_context(self.S(f"stg{i}", [128, T], BF16)) for i in range(3)]
            stgb = bufs(3, "stg")
            stt_ = [ps.enter_context(self.S(f"stt{i}", [128, 512], BF16)) for i in range(4)]
            sttb = bufs(4, "stt")
            rt = [ps.enter_context(self.S(f"rt{i}", [128, 512], F32)) for i in range(4)]
            rtb = bufs(4, "rt")
            def fm_dest(ch):
                if ch < 8:
                    p = ch // 2
                    return [(d["qTn"][2 * p], sbf["qTn"][2 * p]), (d["qTn"][2 * p + 1], sbf["qTn"][2 * p + 1])]
                if ch < 14:
                    nm = ("kTc", "kTs", "kTw")[(ch - 8) // 2]
                    return [(d[nm][0], sbf[nm][0]), (d[nm][1], sbf[nm][1])]
                if ch == 14:
                    return [(d["vTc"][0], sbf["vTc"][0]), (d["vTc"][1], sbf["vTc"][1])]
                if ch < 19:
                    p = ch - 15
                    return [(d["qTsb"][2 * p], sbf["qTsb"][2 * p]), (d["qTsb"][2 * p + 1], sbf["qTsb"][2 * p + 1])]
                p = ch - 19
                return [(d["kTsb"][2 * p], sbf["kTsb"][2 * p]), (d["kTsb"][2 * p + 1], sbf["kTsb"][2 * p + 1])]

            pi = 0
            si = 0
            ri = 0
            def issue_w(g):
                if g < 12:
                    kb.dma("pool", wbuf[g % 3][:], d["win"][l, g], (), [wb[g % 3]])
            issue_w(0)
            issue_w(1)
            for g in range(6):
                w = wbuf[g % 3]
                wbb = wb[g % 3]
                issue_w(g + 2)
                chunks = [g * 4 + c for c in range(4) if g * 4 + c < 23]
                ci = 0
                while ci < len(chunks):
                    ch = chunks[ci]
                    rope = ch < 14
                    st = stg[si % 3]
                    stb = stgb[si % 3]
                    si += 1
                    for tc in range(4):
                        if g == 0 and ci == 0:
                            self.norm_tiles(ps, [4 * tc + k for k in range(4)], d["gfm"][l, 0], hT, hTb, pst, pstb,
                                            ntag, joff=4 * tc)
                        rd = [hTb[4 * tc + k] for k in range(4)] + [wbb]
                        pa, pab = psm[pi % 6], psmb[pi % 6]
                        pi += 1
                        for k in range(8):
                            kb.mm(pa[:], w[:, k, (ch % 4) * 128:(ch % 4 + 1) * 128], hT[:, k, tc * 512:(tc + 1) * 512],
                                  k == 0, k == 7, rd, [pab])
                        if rope:
                            pb_, pbb = psm[pi % 6], psmb[pi % 6]
                            pi += 1
                            for k in range(8):
                                kb.mm(pb_[:], w[:, k, (ch % 4 + 1) * 128:(ch % 4 + 2) * 128],
                                      hT[:, k, tc * 512:(tc + 1) * 512], k == 0, k == 7, rd, [pbb])
                            r1, r1b = rt[ri % 4], rtb[ri % 4]
                            r2, r2b = rt[(ri + 1) % 4], rtb[(ri + 1) % 4]
                            ri += 2
                            kb.tt("dve", r1[:], pa[:], cos2[:, tc * 512:(tc + 1) * 512], ALU.mult, [pab, b_cos], [r1b])
                            kb.tt("dve", r2[:], pb_[:], sinpm[:, tc * 512:(tc + 1) * 512], ALU.mult, [pbb, b_sin], [r2b])
                            kb.tt("pool", st[:, tc * 512:(tc + 1) * 512], r1[:], r2[:], ALU.add, [r1b, r2b], [stb])
                        elif 15 <= ch < 19:
                            kb.act(st[:, tc * 512:(tc + 1) * 512], pa[:], AF.Copy, [pab], [stb], scale=SCALE)
                        else:
                            kb.copy("act", st[:, tc * 512:(tc + 1) * 512], pa[:], [pab], [stb])
                    for half, (dap, dbuf) in enumerate(fm_dest(ch)):
                        kb.dma("sp", dap, st[half * 64:(half + 1) * 64, :], [stb], [dbuf])
                    ci += 2 if rope else 1
            for g in range(6, 12):
                w = wbuf[g % 3]
                wbb = wb[g % 3]
                issue_w(g + 2)
                for i in range(NT):
                    pa, pab = psm[pi % 6], psmb[pi % 6]
                    pi += 1
                    for k in range(8):
                        kb.mm(pa[:], hT[:, k, i * 128:(i + 1) * 128], w[:, k, :], k == 0, k == 7, [hTb[i], wbb], [pab])
                    s4 = (g * NT + i) % 4
                    st, stb = stt_[s4], sttb[s4]
                    rows = slice(i * 128, (i + 1) * 128)
                    if g == 6:
                        kb.copy("dve", st[:, 0:256], pa[:, 0:256], [pab], [stb])
                        kb.act(self.gn[:, i, :], pa[:, 256:280], AF.Sigmoid, [pab], [self.gnb[i]])
                        kb.dma("sp", d["vS"][rows, :], st[:, 0:128], [stb], [sbf["vS"][i]])
                        kb.dma("sp", d["vW"][rows, :], st[:, 128:256], [stb], [sbf["vW"][i]])
                    elif g == 7:
                        kb.copy("dve", st[:], pa[:], [pab], [stb])
                        kb.dma("sp", d["vSB"][rows, :], st[:], [stb], [sbf["vSB"][i]])
                    else:
                        kb.act(st[:], pa[:], AF.Sigmoid, [pab], [stb])
                        kb.dma("sp", d["gm"][rows, (g - 8) * 512:(g - 7) * 512], st[:], [stb], [sbf["gm"][i][g - 8]])
            kb.barrier()

    def phase_attn(self, b, l, ls):
        kb = self.kb
        nc = self.nc
        d = self.d
        sbf = self.sbuf
        self.o_acc = ls.enter_context(self.S("o_acc", [128, NT, 512], F32))
        self.o_accb = [bufs(8, f"oacc{i}_") for i in range(NT)]
        self.o_sb = ls.enter_context(self.S("o_sb", [128, NT, 512], BF16))
        self.o_sbb = [bufs(8, f"osb{i}_") for i in range(NT)]
        self.nsa(b, l)
        if self.dbg:
            kb.dma("sp", d["dbg_onsa"][:, :, :], self.o_acc[:], [x for r in self.o_accb for x in r], ())
        if self.stop("nsa"):
            return
        mw = {}
        for nm, shp in (("wpn", [128, 4, 1024]), ("wps", [128, 4, 1024]), ("wout", [128, 8, 1024])):
            t_ = ls.enter_context(self.S("mw_" + nm, shp, BF16))
            b_ = Buf(nm)
            kb.dma("pool", t_[:], d[nm][l], (), [b_])
            mw[nm] = (t_, b_)
        self.mw = mw
        self.sbattn(b, l)
        if self.dbg:
            kb.dma("sp", d["dbg_osb"][:, :, :], self.o_sb[:], [x for r in self.o_sbb for x in r], ())

    def nsa(self, b, l):
        kb = self.kb
        nc = self.nc
        d = self.d
        sbf = self.sbuf
        cstb, cstf = self.cstb, self.cstf
        CB, CFb = self.cb_buf, self.cf_buf
        with ExitStack() as ps:
            def sbt(name, shape, dt):
                return ps.enter_context(self.S(name, shape, dt))
            qT = sbt("qT", [96, 4, T], BF16); qTb = bufs(4, "qT")
            kbuf = sbt("kbuf", [64, 4, T], BF16); kbb = bufs(4, "kbuf")
            ksel = sbt("ksel", [96, T], BF16); kselb = Buf("ksel"); kselEb = Buf("kselE")
            cmpm = sbt("cmpm", [128, T], BF16); cmpmb = Buf("cmpm")
            kb.dma("pool", cmpm[:], d["cstb"][:, CB_CMPM:CB_CMPM + T], (), [cmpmb])
            kb.dma("pool", ksel[64:96, :], d["cstb"][0:32, CB_E:CB_E + T], (), [kselEb])
            Vs = sbt("Vs", [128, NT, 65], BF16); Vsb = Buf("Vs")
            Vw = sbt("Vw", [128, NT, 65], BF16); Vwb = Buf("Vw")
            selTb = bufs(NT, "selT")
            w1 = [sbt("w1k", [64, 2048], BF16), sbt("w1v", [64, 2048], BF16)]; w1b = bufs(2, "w1")
            peT = sbt("peT", [64, 2, 32], BF16); peTb = Buf()
            w2 = sbt("w2", [64, 2, 64], BF16); w2b = Buf()
            b1 = sbt("b1", [64, 2], F32); b1b = Buf()
            biasb = sbt("biasb", [64, 2], F32); biasbb = Buf()
            hc = sbt("hc", [64, 128], BF16); hcb = Buf()
            kcT = sbt("kcT", [64, 128], BF16); kcTb = Buf()
            vcx = sbt("vcx", [128, 97], F32); vcxb = Buf()
            em = [sbt(f"em{i}", [128, 512], F32) for i in range(2)]; emb = bufs(2, "em")
            rden = sbt("rden", [128, 8], F32); rdenb = Buf()
            scg = sbt("scg", [128, 8], F32); scgb = Buf()
            sc = [sbt(f"sc{i}", [128, 32], F32) for i in range(2)]; scb = bufs(2, "sc")
            big = sbt("big", [128, 1024], F32); bigb = Buf()
            rank = sbt("rank", [128, 32], F32); rankb = Buf()
            selb = sbt("selb", [128, 32], BF16); selbb = Buf()
            Pt = [sbt(f"Pt{i}", [128, 512], BF16) for i in range(6)]; Ptb = bufs(6, "Pt")
            rden2 = [sbt(f"rden2_{i}", [128, 4], F32) for i in range(4)]; rden2b = bufs(4, "rden2")
            pa = [ps.enter_context(self.P(f"pa{i}", [128, 512], F32)) for i in range(7)]
            pab = bufs(7, "pa")
            pst = ps.enter_context(self.P("pstn", [128, 1024], BF16)); pstb = Buf("pstn")

            kb.dma("pool", w1[0][:], d["wck1"][l], (), [w1b[0]])
            kb.dma("pool", w1[1][:], d["wcv1"][l], (), [w1b[1]])
            kb.dma("pool", peT[:], d["peT"][l].rearrange("k d l -> d k l"), (), [peTb])
            kb.dma("pool", w2[:], d["w2"][l].rearrange("k a b -> a k b"), (), [w2b])
            kb.dma("sp", b1[:], d["b1"][l], (), [b1b])
            kb.memset("pool", Vs[:, :, 64:65], 1.0, [Vsb])
            kb.memset("pool", Vw[:, :, 64:65], 1.0, [Vwb])
            kb.memset("dve", vcx[:], 0.0, [vcxb])
            kb.memset("dve", vcx[:, 64:65], 1.0, [vcxb])
            kb.copy("dve", vcx[:, 65:97], cstf[:, CF_OV:CF_OV + 32], [CFb], [vcxb])
            ident = self.cbs(CB_IDENT)
            mcausal = self.cbs(CB_MCAUSAL)
            mwinlow = self.cbs(CB_MWINLOW)

            for h in range(2):
                for g in range(4):
                    kb.dma("sp", qT[0:64, g, :], d["qTn"][4 * h + g], [sbf["qTn"][4 * h + g]], [qTb[g]])
                for slot, nm in enumerate(("kTc", "vTc", "kTs", "kTw")):
                    if slot == 2:
                        kb.dma("sp", ksel[0:64, :], d[nm][h], [sbf[nm][h]], [kselb])
                    else:
                        kb.dma("sp", kbuf[:, slot, :], d[nm][h], [sbf[nm][h]], [kbb[slot]])
                kb.dma("sp", Vs[:, :, 0:64], d["vS"].rearrange("(i p) c -> p i c", p=128)[:, :, h * 64:(h + 1) * 64],
                       sbf["vS"], [Vsb])
                kb.dma("sp", Vw[:, :, 0:64], d["vW"].rearrange("(i p) c -> p i c", p=128)[:, :, h * 64:(h + 1) * 64],
                       sbf["vW"], [Vwb])
                for kv in range(2):
                    A = pa[0][0:64, 0:NCMP]
                    for li in range(32):
                        kb.mm(A, w1[kv][:, li * 64:(li + 1) * 64], kbuf[:, kv, li:li + 2017:16], li == 0, li == 31,
                              [w1b[kv], kbb[kv]], [pab[0]])
                    Bv = pa[1][0:64, 0:1]
                    for li in range(32):
                        kb.mm(Bv, w1[kv][:, li * 64:(li + 1) * 64], peT[:, kv, li:li + 1], li == 0, li == 31,
                              [w1b[kv], peTb], [pab[1]])
                    kb.tt("dve", biasb[:, kv:kv + 1], Bv, b1[:, kv:kv + 1], ALU.add, [pab[1], b1b], [biasbb])
                    kb.act(hc[:, 0:NCMP], A, AF.Gelu_apprx_tanh, [pab[0], biasbb], [hcb], bias=biasb[:, kv:kv + 1])
                    if kv == 0:
                        o2 = pa[1][0:64, 128:128 + NCMP]
                        kb.mm(o2, w2[:, 0, :], hc[:, 0:NCMP], True, True, [w2b, hcb], [pab[1]])
                        kb.copy("dve", kcT[:, 0:NCMP], o2, [pab[1]], [kcTb])
                    else:
                        o2 = pa[1][0:NCMP, 256:320]
                        kb.mm(o2, hc[:, 0:NCMP], w2[:, 1, :], True, True, [w2b, hcb], [pab[1]])
                        kb.copy("dve", vcx[0:NCMP, 0:64], o2, [pab[1]], [vcxb])
                state = {"s": 0, "p": 0, "r": 0}
                SB_ = [0, 1, 2, 6]

                def next_S():
                    j = SB_[state["s"] % 4]
                    state["s"] += 1
                    return pa[j], pab[j]

                def cmp_tile(i):
                    tsl = slice(i * 128, (i + 1) * 128)
                    S, Sb = next_S()
                    S3 = S[0:NCMP, :].rearrange("p (g t) -> p g t", g=4)
                    kb.mm(S3, kcT[:, 0:NCMP], qT[0:64, :, tsl], True, True, [kcTb] + qTb, [Sb])
                    yield
                    e_, eb_ = em[i % 2], emb[i % 2]
                    kb.act(e_[0:NCMP, :], S[0:NCMP, :], AF.Exp, [Sb], [eb_], scale=SCALE)
                    e3 = e_[0:NCMP, :].rearrange("p (g t) -> p g t", g=4)
                    kb.tt("dve", e3, e3, cmpm[0:NCMP, i * 128:(i + 1) * 128].unsqueeze(1).broadcast_to([NCMP, 4, 128]),
                          ALU.mult, [eb_, cmpmb], [eb_])
                    yield
                    PV, PVb = pa[3], pab[3]
                    PV3 = PV[:, 0:388].rearrange("p (g c) -> p g c", g=4)
                    for g in range(4):
                        kb.mm(PV3[:, g, :], e_[0:NCMP, g * 128:(g + 1) * 128], vcx[0:NCMP, :], g == 0, g == 3,
                              [eb_, vcxb], [PVb])
                    yield
                    r4 = rden[:, 0:4]
                    kb.ts("dve", r4, PV3[:, :, 64], 1e-30, None, ALU.max, None, [PVb], [rdenb])
                    yield
                    kb.op("dve", lambda: nc.vector.reciprocal(out=r4, in_=r4), [rdenb], [rdenb])
                    yield
                    prev = cstf[:, CF_BV + i * 32:CF_BV + (i + 1) * 32]
                    prevb = CFb
                    for g in range(4):
                        hh = 4 * h + g
                        s_, sb_ = sc[g % 2], scb[g % 2]
                        kb.stt("dve", s_[:], PV3[:, g, 65:97], rden[:, g:g + 1], prev, ALU.mult, ALU.add,
                               [PVb, rdenb, prevb], [sb_])
                        prev, prevb = s_[:], sb_
                        kb.stt("dve", self.o_acc[:, i, hh * 64:(hh + 1) * 64], PV3[:, g, 0:64], rden[:, g:g + 1],
                               self.gn[:, i, 3 * hh:3 * hh + 1].broadcast_to([128, 64]), ALU.mult, ALU.mult,
                               [PVb, rdenb, self.gnb[i]], [self.o_accb[i][hh]])
                    yield
                    big3 = big[:].rearrange("p (a c) -> p a c", a=32)
                    kb.tt("dve", big3, prev.unsqueeze(1).broadcast_to([128, 32, 32]),
                          prev.unsqueeze(2).broadcast_to([128, 32, 32]), ALU.is_gt, [prevb], [bigb])
                    yield
                    kb.op("dve", lambda: nc.vector.reduce_sum(out=rank[:], in_=big3, axis=AX.X), [bigb], [rankb])
                    yield
                    kb.ts("dve", selb[:], rank[:], 16.0, NEG, ALU.is_ge, ALU.mult, [rankb], [selbb])
                    yield
                    kb.tr(pst[0:32, 0:128], selb[:], ident, [selbb, CB], [pstb])
                    kb.copy("act", qT[64:96, :, tsl], pst[0:32, 0:128].unsqueeze(1).broadcast_to([32, 4, 128]),
                            [pstb], [selTb[i]])

                def branch_tile(i, ks, kslot, V, Vb, O, Ob, first, last, masks, gate, use_sel):
                    tsl = slice(i * 128, (i + 1) * 128)
                    S, Sb = next_S()
                    S3 = S[:].rearrange("p (g t) -> p g t", g=4)
                    if use_sel:
                        kb.mm(S3, ksel[:, ks * 128:(ks + 1) * 128], qT[:, :, tsl], True, True,
                              [kselb, kselEb, selTb[i]] + qTb, [Sb])
                    else:
                        kb.mm(S3, kbuf[:, kslot, ks * 128:(ks + 1) * 128], qT[0:64, :, tsl], True, True,
                              [kbb[kslot]] + qTb, [Sb])
                    yield
                    P, Pb = Pt[state["p"] % 6], Ptb[state["p"] % 6]
                    state["p"] += 1
                    kb.act(P[:], S[:], AF.Exp, [Sb], [Pb], scale=SCALE)
                    P3 = P[:].rearrange("p (g t) -> p g t", g=4)
                    for m in masks:
                        kb.tt("pool", P3, P3, m.unsqueeze(1).broadcast_to([128, 4, 128]), ALU.mult, [Pb, CB], [Pb])
                    yield
                    O3 = O[:, 0:260].rearrange("p (g c) -> p g c", g=4)
                    for g in range(4):
                        kb.mm(O3[:, g, :], P[:, g * 128:(g + 1) * 128], V[:, ks, :], first and g == 0, last and g == 3,
                              [Pb, Vb], [Ob])
                    if last:
                        yield
                        r_, rb_ = rden2[state["r"] % 4], rden2b[state["r"] % 4]
                        state["r"] += 1
                        kb.op("dve", lambda: nc.vector.reciprocal(out=r_[:], in_=O3[:, :, 64]), [Ob], [rb_])
                        yield
                        kb.tt("dve", r_[:], r_[:], self.gn[:, i, 12 * h + gate:12 * h + 12:3], ALU.mult,
                              [rb_, self.gnb[i]], [rb_])
                        yield
                        for g in range(4):
                            hh = 4 * h + g
                            oa = self.o_acc[:, i, hh * 64:(hh + 1) * 64]
                            kb.stt("dve", oa, O3[:, g, 0:64], r_[:, g:g + 1], oa, ALU.mult, ALU.add,
                                   [Ob, rb_, self.o_accb[i][hh]], [self.o_accb[i][hh]])

                def group_tiles(i):
                    Os, Osb = pa[4], pab[4]
                    Ow, Owb = pa[5], pab[5]
                    for ks in range(i + 1):
                        yield branch_tile(i, ks, 2, Vs, Vsb, Os, Osb, ks == 0, ks == i,
                                          [mcausal] if ks == i else [], 1, True)
                    lo = max(0, i - 4)
                    for ks in range(lo, i + 1):
                        masks = []
                        if ks == i:
                            masks.append(mcausal)
                        if ks == i - 4:
                            masks.append(mwinlow)
                        yield branch_tile(i, ks, 3, Vw, Vwb, Ow, Owb, ks == lo, ks == i, masks, 2, False)

                DEPTH_ = 4
                active = []

                def step_all():
                    for g_ in list(active):
                        try:
                            next(g_)
                        except StopIteration:
                            active.remove(g_)

                def start(g_):
                    while len(active) >= DEPTH_:
                        step_all()
                    active.append(g_)

                cmp_g = {i: cmp_tile(i) for i in range(NT)}
                start(cmp_g[0])
                for i in range(NT):
                    while any(g_ is cmp_g[i] for g_ in active):
                        step_all()
                    if i + 1 < NT:
                        start(cmp_g[i + 1])
                    for g_ in group_tiles(i):
                        start(g_)
                while active:
                    step_all()
            kb.barrier()

    def sbattn(self, b, l):
        kb = self.kb
        nc = self.nc
        d = self.d
        sbf = self.sbuf
        cstb = self.cstb
        CB = self.cb_buf
        NCH = 4
        with ExitStack() as ps:
            def sbt(name, shape, dt):
                return ps.enter_context(self.S(name, shape, dt))
            qs = [sbt(f"qs{i}", [128, T], BF16) for i in range(2)]; qsb = [bufs(2, f"qs{i}_") for i in range(2)]
            ksb_ = [sbt(f"ks{i}", [128, T], BF16) for i in range(2)]; ksbb = [bufs(2, f"ks{i}_") for i in range(2)]
            vsb = [sbt(f"vsb{i}", [128, NT, 128], BF16) for i in range(2)]; vsbb = bufs(2, "vsb")
            et = [sbt(f"e{i}", [128, 512], F32) for i in range(NCH)]; etb = bufs(NCH, "e")
            Lt = [sbt(f"L{i}", [128, 512], BF16) for i in range(NCH)]; Ltb = bufs(NCH, "L")
            Ls = [sbt(f"Ls{i}", [128, 512], BF16) for i in range(NCH)]; Lsb = bufs(NCH, "Ls")
            wt = [sbt(f"w{i}", [128, 512], BF16) for i in range(NCH)]; wtb = bufs(NCH, "w")
            Ob_ = [ps.enter_context(self.P(f"sbO{i}", [128, 512], F32)) for i in range(NCH)]; Obb = bufs(NCH, "sbO")
            zb_ = [ps.enter_context(self.P(f"sbz{i}", [128, 512], F32)) for i in range(NCH)]
            zbb = bufs(NCH, "sbz")
            tri = self.cbs(CB_TRI)
            onesn = self.cbs(CB_ONES)
            mstrict = self.cbs(CB_MSTRICT)
            state = {"z": 0}

            def load_pair(hp):
                s = hp % 2
                for a in range(2):
                    kb.dma("sp", qs[s][a * 64:(a + 1) * 64, :], d["qTsb"][2 * hp + a], [sbf["qTsb"][2 * hp + a]], [qsb[s][a]])
                    kb.dma("sp", ksb_[s][a * 64:(a + 1) * 64, :], d["kTsb"][2 * hp + a], [sbf["kTsb"][2 * hp + a]], [ksbb[s][a]])
                kb.dma("sp", vsb[s][:], d["vSB"].rearrange("(i p) c -> p i c", p=128)[:, :, hp * 128:(hp + 1) * 128],
                       sbf["vSB"], [vsbb[s]])

            def chain(slot, hp, a, tq):
                s = hp % 2
                base = a * 64
                head = 2 * hp + a
                O, Ob = Ob_[slot], Obb[slot]
                O3 = O[:, 0:256].rearrange("p (c e) -> p c e", c=4)
                Lsum, Lsumb = Ls[slot], Lsb[slot]
                kb.memset("pool", Lsum[:], 0.0, [Lsumb])
                nks = 4 * tq + 4
                firstO = True
                for ks in range(nks - 1, -1, -1):
                    c0 = max(0, ks - 4 * tq) * 128
                    N = 512 - c0
                    diag = ks >= 4 * tq
                    first = ks == nks - 1
                    z, zb = zb_[slot], zbb[slot]
                    kb.mm(z[:, 0:N], ksb_[s][base:base + 64, ks * 128:(ks + 1) * 128],
                          qs[s][base:base + 64, tq * 512 + c0:(tq + 1) * 512], True, False,
                          [ksbb[s][a], qsb[s][a]], [zb])
                    yield
                    e, eb = et[slot], etb[slot]
                    kb.act(e[:, 0:N], z[:, 0:N], AF.Exp, [zb], [eb])
                    if diag:
                        kb.tt("dve", e[:, 0:128], e[:, 0:128], mstrict, ALU.mult, [eb, CB], [eb])
                    yield
                    Lx, Lxb = Lt[slot], Ltb[slot]
                    kb.act(Lx[:, 0:N], e[:, 0:N], AF.Ln, [eb], [Lxb], bias=1.0)
                    yield
                    kb.mm(z[:, 0:N], tri, Lx[:, 0:N], False, first, [CB, Lxb], [zb])
                    if not first:
                        kb.mm(z[:, 0:N], onesn, Lsum[:, c0:512], False, True, [CB, Lsumb], [zb])
                    if ks > 0:
                        kb.tt("dve", Lsum[:, c0:512], Lsum[:, c0:512], Lx[:, 0:N], ALU.add, [Lsumb, Lxb], [Lsumb])
                    yield
                    w, wb = wt[slot], wtb[slot]
                    kb.act(w[:, 0:N], z[:, 0:N], AF.Exp, [zb], [wb])
                    if diag:
                        kb.tt("dve", w[:, 0:128], w[:, 0:128], mstrict, ALU.mult, [wb, CB], [wb])
                    yield
                    for c in range(c0 // 128, 4):
                        kb.mm(O3[:, c, :], w[:, c * 128 - c0:(c + 1) * 128 - c0], vsb[s][:, ks, a * 64:(a + 1) * 64],
                              firstO, ks == 0 and c == 3, [wb, vsbb[s]], [Ob])
                        firstO = False
                    yield
                kb.copy("dve", self.o_sb[:, 4 * tq:4 * tq + 4, head * 64:(head + 1) * 64], O3,
                        [Ob], [self.o_sbb[4 * tq + c][head] for c in range(4)])

            def all_chains():
                for hp in range(4):
                    for tq in (3, 2, 1, 0):
                        for a in range(2):
                            yield (hp, a, tq)

            load_pair(0)
            load_pair(1)
            remaining = {hp: 8 for hp in range(4)}
            free = list(range(NCH))
            active = []
            gen = all_chains()
            done = False
            while True:
                while not done and free:
                    try:
                        hp, a, tq = next(gen)
                    except StopIteration:
                        done = True
                        break
                    sl = free.pop(0)
                    active.append((sl, hp, chain(sl, hp, a, tq)))
                if not active:
                    break
                for item in list(active):
                    sl, hp, g = item
                    try:
                        next(g)
                    except StopIteration:
                        active.remove(item)
                        free.append(sl)
                        remaining[hp] -= 1
                        if remaining[hp] == 0 and hp + 2 < 4:
                            load_pair(hp + 2)
            kb.barrier()

    def post_norm_add(self, i, m, mb, gbc, gbcb, tmp, add_eng="pool"):
        kb = self.kb
        nc = self.nc
        ssq, ssqb, rs, rsb, junk, junkb, tn, tnb = tmp
        kb.memset("dve", ssq[:], 0.0, [ssqb])
        for hf in range(2):
            kb.act(junk[:], m[hf][:], AF.Square, [mb[hf]], [junkb, ssqb], accum_out=ssq[:, hf:hf + 1],
                   scale=float(D) ** -0.5)
        kb.tt("dve", rs[:], ssq[:, 0:1], ssq[:, 1:2], ALU.add, [ssqb], [rsb])
        kb.act(rs[:], rs[:], AF.Sqrt, [rsb], [rsb], bias=self.epsc[:, 0:1])
        kb.op("dve", lambda: nc.vector.reciprocal(out=rs[:], in_=rs[:]), [rsb], [rsb])
        for hf in range(2):
            kb.stt("dve", tn[:, hf * 512:(hf + 1) * 512], m[hf][:], rs[:, 0:1], gbc[:, hf * 512:(hf + 1) * 512],
                   ALU.mult, ALU.mult, [mb[hf], rsb, gbcb], [tnb])
        kb.tt(add_eng, self.x[:, i, :], self.x[:, i, :], tn[:], ALU.add, [self.xb[i], tnb], [self.xb[i]])

    def phase_merge(self, b, l):
        kb = self.kb
        nc = self.nc
        d = self.d
        sbf = self.sbuf
        CB = self.cb_buf
        ident = self.cbs(CB_IDENT)
        with ExitStack() as ps:
            def sbt(name, shape, dt):
                return ps.enter_context(self.S(name, shape, dt))
            wpn, wpnb = self.mw["wpn"]
            wps, wpsb = self.mw["wps"]
            wout, woutb = self.mw["wout"]
            gbc = sbt("m_gbc", [128, 1024], F32); gbcb = Buf()
            kb.dma("sp", gbc[:], d["gbc"][l, 0:1, :].partition_broadcast(128), (), [gbcb])
            gmt = [sbt(f"m_gmt{j}", [128, 2048], BF16) for j in range(2)]; gmtb = bufs(2)
            onb = [sbt(f"m_onb{j}", [128, 512], BF16) for j in range(2)]; onbb = bufs(2)
            oT = [sbt(f"m_oT{j}", [128, 8, 128], BF16) for j in range(2)]; oTb = bufs(2)
            t1 = [sbt(f"m_t1{j}", [128, 1024], F32) for j in range(2)]; t1b = bufs(2)
            t2 = [sbt(f"m_t2{j}", [128, 1024], F32) for j in range(2)]; t2b = bufs(2)
            yb = [sbt(f"m_yb{j}", [128, 1024], BF16) for j in range(2)]; ybb = bufs(2)
            yT = [sbt(f"m_yT{j}", [128, 8, 128], BF16) for j in range(2)]; yTb = bufs(2)
            tmp = (sbt("m_ssq", [128, 2], F32), Buf(), sbt("m_rs", [128, 1], F32), Buf(),
                   sbt("m_junk", [128, 512], BF16), Buf(), sbt("m_tn", [128, 1024], F32), Buf())
            pT = [ps.enter_context(self.P(f"m_pT{j}", [128, 1024], BF16)) for j in range(2)]; pTb = bufs(2)
            yn = [ps.enter_context(self.P(f"m_yn{j}", [128, 512], F32)) for j in range(2)]; ynb = bufs(2)
            ys = [ps.enter_context(self.P(f"m_ys{j}", [128, 512], F32)) for j in range(2)]; ysb = bufs(2)
            mm_ = [ps.enter_context(self.P(f"m_m{j}", [128, 512], F32)) for j in range(2)]; mmb = bufs(2)
            def merge_tile(i):
                j = i % 2
                pTj, pTjb = pT[j], pTb[j]
                kb.dma("sp", gmt[j][:], d["gm"][i * 128:(i + 1) * 128, :], sbf["gm"][i], [gmtb[j]])
                kb.copy("act", onb[j][:], self.o_acc[:, i, :], self.o_accb[i], [onbb[j]])
                for c in range(4):
                    kb.tr(pTj[:, c * 128:(c + 1) * 128], onb[j][:, c * 128:(c + 1) * 128], ident, [onbb[j], CB], [pTjb])
                for c in range(4):
                    kb.tr(pTj[:, (4 + c) * 128:(5 + c) * 128], self.o_sb[:, i, c * 128:(c + 1) * 128], ident,
                          self.o_sbb[i] + [CB], [pTjb])
                yield
                kb.copy("dve", oT[j][:].rearrange("p c t -> p (c t)"), pTj[:], [pTjb], [oTb[j]])
                yield
                for hf in range(2):
                    for k in range(4):
                        kb.mm(yn[hf][:], oT[j][:, k, :], wpn[:, k, hf * 512:(hf + 1) * 512], k == 0, k == 3,
                              [oTb[j], wpnb], [ynb[hf]])
                    for k in range(4):
                        kb.mm(ys[hf][:], oT[j][:, 4 + k, :], wps[:, k, hf * 512:(hf + 1) * 512], k == 0, k == 3,
                              [oTb[j], wpsb], [ysb[hf]])
                    sl = slice(hf * 512, (hf + 1) * 512)
                    kb.tt("dve", t1[j][:, sl], yn[hf][:], gmt[j][:, hf * 512:(hf + 1) * 512], ALU.mult, [ynb[hf], gmtb[j]], [t1b[j]])
                    kb.tt("dve", t2[j][:, sl], ys[hf][:], gmt[j][:, 1024 + hf * 512:1024 + (hf + 1) * 512], ALU.mult,
                          [ysb[hf], gmtb[j]], [t2b[j]])
                    yield
                kb.tt("pool", yb[j][:], t1[j][:], t2[j][:], ALU.add, [t1b[j], t2b[j]], [ybb[j]])
                yield
                for c in range(8):
                    kb.tr(pTj[:, c * 128:(c + 1) * 128], yb[j][:, c * 128:(c + 1) * 128], ident, [ybb[j], CB], [pTjb])
                yield
                kb.copy("act", yT[j][:].rearrange("p c t -> p (c t)"), pTj[:], [pTjb], [yTb[j]])
                yield
                for hf in range(2):
                    for k in range(8):
                        kb.mm(mm_[hf][:], yT[j][:, k, :], wout[:, k, hf * 512:(hf + 1) * 512], k == 0, k == 7,
                              [yTb[j], woutb], [mmb[hf]])
                self.post_norm_add(i, mm_, mmb, gbc, gbcb, tmp)

            run_pipelined((merge_tile(i) for i in range(NT)), 2)
            kb.barrier()

    def phase_ffn(self, b, l):
        kb = self.kb
        nc = self.nc
        d = self.d
        CB = self.cb_buf
        with ExitStack() as ps:
            def sbt(name, shape, dt):
                return ps.enter_context(self.S(name, shape, dt))
            wd = sbt("f_wd", [128, NCP, 1024], BF16); wdb = bufs(2)
            kb.dma("pool", wd[:, 0:11, :], d["wdn"][l][:, 0:11, :], (), [wdb[0]])
            kb.dma("pool", wd[:, 11:22, :], d["wdn"][l][:, 11:22, :], (), [wdb[1]])
            cw = sbt("f_cw", [128, 44, 3], F32); cwb = Buf()
            cbs_ = sbt("f_cb", [128, 44], F32); cbb = Buf()
            gbc = sbt("f_gbc", [128, 1024], F32); gbcb = Buf()
            kb.dma("sp", cw[:], d["cw"][l], (), [cwb])
            kb.dma("sp", cbs_[:], d["cb"][l], (), [cbb])
            kb.dma("sp", gbc[:], d["gbc"][l, 1:2, :].partition_broadcast(128), (), [gbcb])
            Xe = [sbt(f"f_Xe{j}", [128, 44, 4], F32) for j in range(2)]; Xeb = bufs(2, "Xe")
            kb.memset("pool", Xe[0][:], 0.0, [Xeb[0]])
            et = [sbt(f"f_et{j}", [128, 44, 2], F32) for j in range(3)]; etb = bufs(3, "et")
            hT2 = sbt("f_hT2", [128, 8, 512], BF16); hT2b = bufs(4)
            aT = sbt("f_aT", [128, NCP, 512], BF16); aTb = bufs(NCP)
            wub = [sbt(f"f_wub{j}", [128, 8, 256], BF16) for j in range(3)]; wubb = bufs(3)
            cv = [[sbt(f"f_c{p}{j}", [128, 512], F32) for j in range(3)] for p in range(2)]
            cvb = [bufs(3), bufs(3)]
            gl = [sbt(f"f_gl{j}", [128, 512], F32) for j in range(2)]; glb = bufs(2)
            tmp = (sbt("f_ssq", [128, 2], F32), Buf(), sbt("f_rs", [128, 1], F32), Buf(),
                   sbt("f_junk", [128, 512], BF16), Buf(), sbt("f_tn", [128, 1024], F32), Buf())
            pst = [ps.enter_context(self.P(f"f_pst{j}", [128, 1024], BF16)) for j in range(2)]; pstb = bufs(2)
            pu = [ps.enter_context(self.P(f"f_pu{j}", [128, 512], F32)) for j in range(4)]; pub = bufs(4)
            pf = [ps.enter_context(self.P(f"f_pf{j}", [128, 512], F32)) for j in range(2)]; pfb = bufs(2)
            tagn = f"f{self.uid}"
            pi = 0
            nld = 4 * NCP

            def issue_wu(j):
                if j < nld:
                    kb.dma("pool", wub[j % 3][:], d["wup"][l, j % NCP], (), [wubb[j % 3]])
            issue_wu(0)
            issue_wu(1)
            self.norm_tiles(ps, [0, 1, 2, 3], d["gfm"][l, 1], hT2, hT2b, pst, pstb, tagn)
            for qt in range(4):
                for cp in range(NCP):
                    jj = qt * NCP + cp
                    issue_wu(jj + 2)
                    w, wb = wub[jj % 3], wubb[jj % 3]
                    parts = []
                    for part in range(2):
                        ci = part * NCP + cp
                        p_, pb_ = pu[pi % 4], pub[pi % 4]
                        pi += 1
                        for k in range(8):
                            kb.mm(p_[:], w[:, k, part * 128:(part + 1) * 128], hT2[:, k, :], k == 0, k == 7,
                                  hT2b + [wb], [pb_])
                        Xc, Xn = Xe[qt % 2], Xe[(qt + 1) % 2]
                        Xcb, Xnb = Xeb[qt % 2], Xeb[(qt + 1) % 2]
                        c, cb_ = cv[part][cp % 3], cvb[part][cp % 3]
                        kb.act(c[:], p_[:], AF.Identity, [pb_, cwb, cbb], [cb_], scale=cw[:, ci, 2:3],
                               bias=cbs_[:, ci:ci + 1])
                        kb.copy("act", Xc[:, ci, 2:4], p_[:, 0:2], [pb_], [Xcb])
                        kb.copy("act", Xn[:, ci, 0:2], p_[:, 510:512], [pb_], [Xnb])
                        parts.append((ci, p_, pb_, c, cb_))
                    for (ci, p_, pb_, c, cb_) in parts:
                        kb.stt("dve", c[:, 2:512], p_[:, 1:511], cw[:, ci, 1:2], c[:, 2:512], ALU.mult, ALU.add,
                               [pb_, cwb, cb_], [cb_])
                    for (ci, p_, pb_, c, cb_) in parts:
                        kb.stt("dve", c[:, 2:512], p_[:, 0:510], cw[:, ci, 0:1], c[:, 2:512], ALU.mult, ALU.add,
                               [pb_, cwb, cb_], [cb_])

                    def finish(cq):
                        g_, gb_ = gl[cq % 2], glb[cq % 2]
                        kb.act(g_[:, 2:512], cv[0][cq % 3][:, 2:512], AF.Gelu_apprx_tanh, [cvb[0][cq % 3]], [gb_])
                        kb.tt("dve", aT[:, cq, 2:512], g_[:, 2:512], cv[1][cq % 3][:, 2:512], ALU.mult,
                              [gb_, cvb[1][cq % 3]], [aTb[cq]])
                    if cp > 0:
                        finish(cp - 1)
                    if cp == NCP - 1:
                        finish(cp)
                Xc, Xcb = Xe[qt % 2], Xeb[qt % 2]
                kb.tt("dve", et[0][:], Xc[:, :, 2:4], cw[:, :, 2:3].broadcast_to([128, 44, 2]), ALU.mult, [Xcb, cwb], [etb[0]])
                kb.tt("dve", et[1][:], Xc[:, :, 1:3], cw[:, :, 1:2].broadcast_to([128, 44, 2]), ALU.mult, [Xcb, cwb], [etb[1]])
                kb.tt("dve", et[2][:], Xc[:, :, 0:2], cw[:, :, 0:1].broadcast_to([128, 44, 2]), ALU.mult, [Xcb, cwb], [etb[2]])
                kb.tt("dve", et[0][:], et[0][:], et[1][:], ALU.add, [etb[0], etb[1]], [etb[0]])
                kb.tt("dve", et[2][:], et[2][:], cbs_[:].unsqueeze(2).broadcast_to([128, 44, 2]), ALU.add, [etb[2], cbb], [etb[2]])
                kb.tt("dve", et[0][:], et[0][:], et[2][:], ALU.add, [etb[0], etb[2]], [etb[0]])
                kb.act(et[1][:, 0:NCP, :], et[0][:, 0:NCP, :], AF.Gelu_apprx_tanh, [etb[0]], [etb[1]])
                kb.tt("dve", aT[:, :, 0:2], et[1][:, 0:NCP, :], et[0][:, NCP:2 * NCP, :], ALU.mult, [etb[0], etb[1]], aTb)
                if qt + 1 < 4:
                    self.norm_tiles(ps, [4 * (qt + 1) + j for j in range(4)], d["gfm"][l, 1], hT2, hT2b, pst, pstb, tagn)
                for tt_ in range(4):
                    for hf in range(2):
                        for cp in range(NCP):
                            kb.mm(pf[hf][:], aT[:, cp, tt_ * 128:(tt_ + 1) * 128], wd[:, cp, hf * 512:(hf + 1) * 512],
                                  cp == 0, cp == NCP - 1, [aTb[cp], wdb[cp // 11]], [pfb[hf]])
                    self.post_norm_add(4 * qt + tt_, pf, pfb, gbc, gbcb, tmp, "dve")
            kb.barrier()


_PROG_CACHE = {}


def kernel(**inputs):
    inp = {k: np.asarray(v) for k, v in inputs.items()}
    W = prep_weights(inp)
    cb, cf = const_tables()
    if "prog" not in _PROG_CACHE:
        _PROG_CACHE["prog"] = Prog(nb=NB, layers=tuple(range(DEPTH)))
    prog = _PROG_CACHE["prog"]
    x = np.ascontiguousarray(inp["x"], dtype=np.float32)
    pos = np.ascontiguousarray(inp["positions"]).astype(np.int32)
    in_maps = []
    for c in range(N_CORES):
        m = dict(W)
        m["cstb"] = cb
        m["cstf"] = cf
        m["x"] = np.ascontiguousarray(x[c * NB:(c + 1) * NB])
        m["pos"] = np.ascontiguousarray(pos[c * NB:(c + 1) * NB])
        in_maps.append(m)
    res = run_bass_kernel_spmd(prog.nc, in_maps, core_ids=list(range(N_CORES)))
    out = np.concatenate([np.asarray(r["y"]) for r in res.results], axis=0)
    return out.astype(np.float32)
```

```python
import numpy as np
from contextlib import ExitStack
import concourse.bass as bass
import concourse.mybir as mybir
from concourse.bass_utils import run_bass_kernel_spmd

F32 = mybir.dt.float32
BF16 = mybir.dt.bfloat16
I32 = mybir.dt.int32
ALU = mybir.AluOpType
AF = mybir.ActivationFunctionType
AX = mybir.AxisListType

T = 2048
D = 1024
NT = 16
HD = 64
DFF = 2816
NCP = 22
SCALE = 0.125
EPS = 1e-6
NEG = -30000.0
N_CORES = 8
NB = 2
DEPTH = 2
NCMP = 127
PI = 3.14159265358979

CB_IDENT, CB_TRI, CB_ONES, CB_MSTRICT, CB_MCAUSAL, CB_MWINLOW = 0, 128, 256, 384, 512, 640
CB_E = 768
CB_CMPM = CB_E + 2048
CB_N = CB_CMPM + 2048
CF_OV = 0
CF_BV = 32
CF_VALID = CF_BV + 512
CF_INVF = CF_VALID + 512
CF_SGN = CF_INVF + 1
CF_N = CF_SGN + 1


class Buf:
    __slots__ = ("name", "w", "r")

    def __init__(self, name=""):
        self.name = name
        self.w = None
        self.r = {}


def bufs(n, name=""):
    return [Buf(f"{name}{i}") for i in range(n)]


class KB:
    NDMA = 32

    def __init__(self, nc, es):
        self.nc = nc
        self.engs = {"pe": nc.tensor, "act": nc.scalar, "dve": nc.vector, "pool": nc.gpsimd, "sp": nc.sync}
        self.sems = {}
        for e in ("pe", "act", "dve", "pool"):
            self.sems[e] = es.enter_context(nc.semaphore("sem_" + e))
        self.cnt = {e: 0 for e in ("pe", "act", "dve", "pool")}
        self.seen = {e: {} for e in self.engs}
        for i in range(self.NDMA):
            self.sems[("d", i)] = es.enter_context(nc.semaphore(f"semd{i}"))
        self.dma_val = [0] * self.NDMA
        self.dma_next = 0
        self.n_inst = 0
        self.n_wait = 0

    def _wait(self, e, deps):
        eng = self.engs[e]
        seen = self.seen[e]
        for k, v in deps:
            if seen.get(k, 0) < v:
                eng.wait_ge(self.sems[k], v)
                seen[k] = v
                self.n_wait += 1

    def op(self, e, fn, reads=(), writes=()):
        deps = []
        for b in reads:
            if b.w is not None:
                deps.append(b.w)
        for b in writes:
            if b.w is not None and b.w[0] != e:
                deps.append(b.w)
            for k, v in b.r.items():
                if k != e:
                    deps.append((k, v))
        self._wait(e, deps)
        ins = fn()
        self.cnt[e] += 1
        t = self.cnt[e]
        ins.then_inc(self.sems[e], 1)
        self.n_inst += 1
        for b in writes:
            b.w = (e, t)
            b.r = {}
        for b in reads:
            if b.r.get(e, 0) < t:
                b.r[e] = t
        return ins

    def dma(self, q, out, in_, reads=(), writes=()):
        i = self.dma_next
        self.dma_next = (i + 1) % self.NDMA
        key = ("d", i)
        deps = []
        if self.dma_val[i] > 0:
            deps.append((key, self.dma_val[i]))
        for b in reads:
            if b.w is not None:
                deps.append(b.w)
        for b in writes:
            if b.w is not None:
                deps.append(b.w)
            deps.extend(b.r.items())
        self._wait(q, deps)
        ins = self.engs[q].dma_start(out=out, in_=in_)
        self.dma_val[i] += 16
        v = self.dma_val[i]
        ins.then_inc(self.sems[key], 16)
        self.n_inst += 1
        for b in writes:
            b.w = (key, v)
            b.r = {}
        for b in reads:
            b.r[key] = v

    def barrier(self):
        deps = [(e, c) for e, c in self.cnt.items() if c > 0]
        deps += [(("d", i), v) for i, v in enumerate(self.dma_val) if v > 0]
        for e in self.engs:
            self._wait(e, [d for d in deps if d[0] != e])

    def mm(self, out, lhsT, rhs, start, stop, reads, writes):
        return self.op("pe", lambda: self.nc.tensor.matmul(out, lhsT=lhsT, rhs=rhs, start=start, stop=stop),
                       reads, writes)

    def tr(self, out, in_, ident, reads, writes):
        return self.op("pe", lambda: self.nc.tensor.transpose(out, in_, ident), reads, writes)

    def act(self, out, in_, func, reads, writes, **kw):
        return self.op("act", lambda: self.nc.scalar.activation(out=out, in_=in_, func=func, **kw), reads, writes)

    def tt(self, e, out, in0, in1, op, reads, writes):
        return self.op(e, lambda: self.engs[e].tensor_tensor(out=out, in0=in0, in1=in1, op=op), reads, writes)

    def ts(self, e, out, in0, s1, s2, op0, op1, reads, writes):
        if op1 is None:
            return self.op(e, lambda: self.engs[e].tensor_scalar(out=out, in0=in0, scalar1=s1, scalar2=None, op0=op0),
                           reads, writes)
        return self.op(e, lambda: self.engs[e].tensor_scalar(out=out, in0=in0, scalar1=s1, scalar2=s2, op0=op0, op1=op1),
                       reads, writes)

    def stt(self, e, out, in0, scalar, in1, op0, op1, reads, writes):
        return self.op(e, lambda: self.engs[e].scalar_tensor_tensor(out=out, in0=in0, scalar=scalar, in1=in1,
                                                                      op0=op0, op1=op1), reads, writes)

    def copy(self, e, out, in_, reads, writes):
        if e == "act":
            return self.op(e, lambda: self.nc.scalar.copy(out=out, in_=in_), reads, writes)
        return self.op(e, lambda: self.engs[e].tensor_copy(out=out, in_=in_), reads, writes)

    def memset(self, e, ap, val, writes):
        return self.op(e, lambda: self.engs[e].memset(ap, val), (), writes)


def run_pipelined(gens, depth):
    active = []
    gens = iter(gens)
    exhausted = False
    while True:
        while not exhausted and len(active) < depth:
            try:
                active.append(next(gens))
            except StopIteration:
                exhausted = True
        if not active:
            break
        for g in list(active):
            try:
                next(g)
            except StopIteration:
                active.remove(g)


def _swap_halves(cols):
    return cols.reshape(-1, 2, 32)[:, ::-1, :].reshape(-1)


def w_in_columns():
    def qn(h):
        return np.arange(64 * h, 64 * h + 64)

    def kv(i, h):
        return 512 + i * 128 + h * 64 + np.arange(64)

    def sb(j, h):
        return 1304 + j * 512 + h * 64 + np.arange(64)

    fm = []
    for p in range(4):
        c = np.concatenate([qn(2 * p), qn(2 * p + 1)])
        fm += [c, _swap_halves(c)]
    for i in (0, 2, 4):
        c = np.concatenate([kv(i, 0), kv(i, 1)])
        fm += [c, _swap_halves(c)]
    fm.append(np.concatenate([kv(1, 0), kv(1, 1)]))
    for j in (0, 1):
        for p in range(4):
            fm.append(np.concatenate([sb(j, 2 * p), sb(j, 2 * p + 1)]))
    fm.append(np.full(128, -1))
    assert len(fm) == 24
    cols = np.concatenate(fm)
    g0 = np.concatenate([kv(3, 0), kv(3, 1), kv(5, 0), kv(5, 1), 1280 + np.arange(24), np.full(512 - 280, -1)])
    g1 = 1304 + 2 * 512 + np.arange(512)
    g25 = 2840 + np.arange(2048)
    cols = np.concatenate([cols, g0, g1, g25])
    assert cols.shape[0] == 12 * 512
    return cols


def prep_weights(inp):
    L = DEPTH
    f = np.float32
    out = {}
    cols = w_in_columns()
    valid = cols >= 0
    win = np.zeros((L, 1024, 12 * 512), f)
    for l in range(L):
        win[l][:, valid] = inp["w_in"][l][:, cols[valid]]
    out["win"] = np.ascontiguousarray(win.reshape(L, 8, 128, 12, 512).transpose(0, 3, 2, 1, 4))
    for nm, src in (("wck1", "ck_w1"), ("wcv1", "cv_w1")):
        out[nm] = np.ascontiguousarray(inp[src].reshape(L, 32, 64, 64).transpose(0, 2, 1, 3).reshape(L, 64, 2048))
    out["peT"] = np.ascontiguousarray(np.stack([inp["ck_pe"], inp["cv_pe"]], 1).transpose(0, 1, 3, 2))
    out["w2"] = np.ascontiguousarray(np.stack([inp["ck_w2"], inp["cv_w2"]], 1))
    out["b1"] = np.ascontiguousarray(np.stack([inp["ck_b1"], inp["cv_b1"]], 2))
    out["wpn"] = np.ascontiguousarray(inp["w_proj_nsa"].reshape(L, 4, 128, 1024).transpose(0, 2, 1, 3))
    out["wps"] = np.ascontiguousarray(inp["w_proj_sb"].reshape(L, 4, 128, 1024).transpose(0, 2, 1, 3))
    out["wout"] = np.ascontiguousarray(inp["w_out"].reshape(L, 8, 128, 1024).transpose(0, 2, 1, 3))
    wup = np.zeros((L, NCP, 128, 8, 256), f)
    for cp in range(NCP):
        c = np.concatenate([cp * 128 + np.arange(128), DFF + cp * 128 + np.arange(128)])
        wup[:, cp] = inp["w_up"][:, :, c].reshape(L, 8, 128, 256).transpose(0, 2, 1, 3)
    out["wup"] = wup
    out["wdn"] = np.ascontiguousarray(inp["w_down"].reshape(L, NCP, 128, 1024).transpose(0, 2, 1, 3))
    out["cw"] = np.ascontiguousarray(inp["conv_w"].transpose(0, 2, 1).reshape(L, 44, 128, 3).transpose(0, 2, 1, 3))
    out["cb"] = np.ascontiguousarray(inp["conv_b"].reshape(L, 44, 128).transpose(0, 2, 1))
    out["gfm"] = np.ascontiguousarray(
        np.stack([inp["g_pre_mix"], inp["g_pre_ffn"]], 1).reshape(L, 2, 8, 128).transpose(0, 1, 3, 2))
    out["gbc"] = np.ascontiguousarray(np.stack([inp["g_post_mix"], inp["g_post_ffn"]], 1))
    return {k: np.ascontiguousarray(v, dtype=f) for k, v in out.items()}


def const_tables():
    p = np.arange(128)
    cb = np.zeros((128, CB_N), np.float32)
    cb[:, CB_IDENT:CB_IDENT + 128] = np.eye(128)
    cb[:, CB_TRI:CB_TRI + 128] = -1.0 * (p[:, None] >= p[None, :])
    cb[:, CB_ONES:CB_ONES + 128] = -1.0
    cb[:, CB_MSTRICT:CB_MSTRICT + 128] = p[:, None] < p[None, :]
    cb[:, CB_MCAUSAL:CB_MCAUSAL + 128] = p[:, None] <= p[None, :]
    cb[:, CB_MWINLOW:CB_MWINLOW + 128] = p[:, None] > p[None, :]
    s = np.arange(T)
    cb[:32, CB_E:CB_E + T] = (s[None, :] // 64) == np.arange(32)[:, None]
    n = np.arange(NCMP)
    cb[:NCMP, CB_CMPM:CB_CMPM + T] = (16 * n[:, None] + 31) <= s[None, :]
    cf = np.zeros((128, CF_N), np.float32)
    ci = n[:, None] * 16
    sj = np.arange(32)[None, :] * 64
    cf[:NCMP, CF_OV:CF_OV + 32] = (ci < sj + 64) & (sj < ci + 32)
    t = (np.arange(NT)[None, :, None] * 128 + p[:, None, None])
    j = np.arange(32)[None, None, :]
    valid = (j * 64 <= t)
    cur = t // 64
    forced = (j == 0) | (j == cur) | (j == cur - 1)
    bv = np.where(forced, 1e4, 0.0) + (valid.astype(np.float32) - 1.0)
    cf[:, CF_BV:CF_BV + 512] = bv.reshape(128, 512)
    cf[:, CF_VALID:CF_VALID + 512] = valid.reshape(128, 512)
    half = 32
    inv_freq = (10000.0 ** (-np.arange(half, dtype=np.float32) / half)).astype(np.float32)
    cf[:, CF_INVF] = inv_freq[p % 32]
    cf[:, CF_SGN] = np.where((p % 64) < 32, -1.0, 1.0)
    return cb, cf


class Prog:
    def __init__(self, nb=NB, layers=(0, 1), stop_after=None, dbg=False):
        self.nb = nb
        self.layers = layers
        self.stop_after = stop_after
        self.dbg = dbg
        self.nc = bass.Bass("TRN2", target_bir_lowering=False)
        self.uid = 0
        self.norm_tmp = {}
        self.build()

    def S(self, name, shape, dt):
        self.uid += 1
        return self.nc.sbuf_tensor(f"{name}_u{self.uid}", list(shape), dt)

    def P(self, name, shape, dt):
        self.uid += 1
        return self.nc.psum_tensor(f"{name}_u{self.uid}", list(shape), dt)

    def dram(self, name, shape, dt, kind="ExternalInput"):
        return self.nc.dram_tensor(name, list(shape), dt, kind=kind).ap()

    def build(self):
        nc = self.nc
        L = DEPTH
        nb = self.nb
        skind = "ExternalOutput" if self.dbg else "Internal"
        d = {}
        d["x"] = self.dram("x", [nb, T, D], F32)
        d["pos"] = self.dram("pos", [nb, T], I32)
        d["win"] = self.dram("win", [L, 12, 128, 8, 512], F32)
        d["wck1"] = self.dram("wck1", [L, 64, 2048], F32)
        d["wcv1"] = self.dram("wcv1", [L, 64, 2048], F32)
        d["peT"] = self.dram("peT", [L, 2, 64, 32], F32)
        d["w2"] = self.dram("w2", [L, 2, 64, 64], F32)
        d["b1"] = self.dram("b1", [L, 64, 2], F32)
        d["wpn"] = self.dram("wpn", [L, 128, 4, 1024], F32)
        d["wps"] = self.dram("wps", [L, 128, 4, 1024], F32)
        d["wout"] = self.dram("wout", [L, 128, 8, 1024], F32)
        d["wup"] = self.dram("wup", [L, NCP, 128, 8, 256], F32)
        d["wdn"] = self.dram("wdn", [L, 128, NCP, 1024], F32)
        d["cw"] = self.dram("cw", [L, 128, 44, 3], F32)
        d["cb"] = self.dram("cb", [L, 128, 44], F32)
        d["gfm"] = self.dram("gfm", [L, 2, 128, 8], F32)
        d["gbc"] = self.dram("gbc", [L, 2, 1024], F32)
        d["cstb"] = self.dram("cstb", [128, CB_N], F32)
        d["cstf"] = self.dram("cstf", [128, CF_N], F32)
        d["y"] = self.dram("y", [nb, T, D], F32, kind="ExternalOutput")
        d["qTn"] = self.dram("s_qTn", [8, 64, T], BF16, kind=skind)
        d["kTc"] = self.dram("s_kTc", [2, 64, T], BF16, kind=skind)
        d["vTc"] = self.dram("s_vTc", [2, 64, T], BF16, kind=skind)
        d["kTs"] = self.dram("s_kTs", [2, 64, T], BF16, kind=skind)
        d["kTw"] = self.dram("s_kTw", [2, 64, T], BF16, kind=skind)
        d["qTsb"] = self.dram("s_qTsb", [8, 64, T], BF16, kind=skind)
        d["kTsb"] = self.dram("s_kTsb", [8, 64, T], BF16, kind=skind)
        d["vS"] = self.dram("s_vS", [T, 128], BF16, kind=skind)
        d["vW"] = self.dram("s_vW", [T, 128], BF16, kind=skind)
        d["vSB"] = self.dram("s_vSB", [T, 512], BF16, kind=skind)
        d["gm"] = self.dram("s_gm", [T, 2048], BF16, kind=skind)
        self.d = d
        sb = {}
        sb["qTn"] = bufs(8, "qTn")
        for k in ("kTc", "vTc", "kTs", "kTw"):
            sb[k] = bufs(2, k)
        sb["qTsb"] = bufs(8, "qTsb")
        sb["kTsb"] = bufs(8, "kTsb")
        for k in ("vS", "vW", "vSB"):
            sb[k] = bufs(NT, k)
        sb["gm"] = [bufs(4, f"gm{i}_") for i in range(NT)]
        self.sbuf = sb
        if self.dbg:
            d["dbg_hT"] = self.dram("dbg_hT", [128, 8, T], BF16, kind="ExternalOutput")
            d["dbg_gn"] = self.dram("dbg_gn", [128, NT, 24], F32, kind="ExternalOutput")
            d["dbg_onsa"] = self.dram("dbg_onsa", [128, NT, 512], F32, kind="ExternalOutput")
            d["dbg_osb"] = self.dram("dbg_osb", [128, NT, 512], BF16, kind="ExternalOutput")
            d["dbg_selT"] = self.dram("dbg_selT", [32, 2, T], BF16, kind="ExternalOutput")

        with ExitStack() as es:
            self.es = es
            kb = KB(nc, es)
            self.kb = kb
            self.x = es.enter_context(self.S("x_sb", [128, NT, D], F32))
            self.xb = bufs(NT, "x")
            self.cstb = es.enter_context(self.S("cstb_sb", [128, CB_E], BF16))
            self.cstf = es.enter_context(self.S("cstf_sb", [128, CF_N], F32))
            self.cb_buf = Buf("cstb")
            self.cf_buf = Buf("cstf")
            kb.dma("pool", self.cstb[:], d["cstb"][:, 0:CB_E], (), [self.cb_buf])
            kb.dma("sp", self.cstf[:], d["cstf"][:, :], (), [self.cf_buf])
            self.epsc = es.enter_context(self.S("epsc", [128, 1], F32))
            kb.memset("pool", self.epsc[:], EPS, [Buf("eps")])
            for b in range(nb):
                for i in range(NT):
                    kb.dma("sp", self.x[:, i, :], d["x"][b, i * 128:(i + 1) * 128, :], (), [self.xb[i]])
                for l in self.layers:
                    self.layer(b, l)
                for i in range(NT):
                    kb.dma("sp", d["y"][b, i * 128:(i + 1) * 128, :], self.x[:, i, :], [self.xb[i]], ())
            kb.barrier()

    def cbs(self, off, n=128, rows=128):
        return self.cstb[0:rows, off:off + n]

    def stop(self, name):
        return self.stop_after == name

    def layer(self, b, l):
        kb = self.kb
        nc = self.nc
        done = False
        with ExitStack() as ls:
            self.gn = ls.enter_context(self.S("gn", [128, NT, 24], F32))
            self.gnb = bufs(NT, "gn")
            self.phase_proj(b, l)
            if self.dbg:
                kb.dma("sp", self.d["dbg_gn"][:, :, :], self.gn[:], self.gnb, ())
            if not self.stop("proj"):
                self.phase_attn(b, l, ls)
                if not self.stop("attn") and not self.stop("nsa"):
                    self.phase_merge(b, l)
                    done = not self.stop("merge")
            kb.barrier()
        if done:
            self.phase_ffn(b, l)

    def norm_tiles(self, ps, tiles, gfm_ap, hT, hT_bufs, psum_t, psum_bufs, tag, joff=0):
        kb = self.kb
        nc = self.nc
        n = len(tiles)
        if tag not in self.norm_tmp:
            self.norm_tmp[tag] = dict(
                ss=ps.enter_context(self.S(f"ss_{tag}", [128, NT], F32)),
                rstd=ps.enter_context(self.S(f"rstd_{tag}", [128, NT], F32)),
                junk=ps.enter_context(self.S(f"junk_{tag}", [128, D], BF16)),
                hb=ps.enter_context(self.S(f"hb_{tag}", [128, 2, D], BF16)),
                gfm=ps.enter_context(self.S(f"gfm_{tag}", [128, 8], F32)),
                bufs=(Buf(), Buf(), Buf(), Buf(), bufs(2)))
        tm = self.norm_tmp[tag]
        ss, rstd, junk, hb, gfm = tm["ss"], tm["rstd"], tm["junk"], tm["hb"], tm["gfm"]
        b_ss, b_rstd, b_junk, b_g, b_hb = tm["bufs"]
        kb.dma("sp", gfm[:], gfm_ap, (), [b_g])
        kb.memset("dve", ss[:], 0.0, [b_ss])
        for j, i in enumerate(tiles):
            kb.act(junk[:], self.x[:, i, :], AF.Square, [self.xb[i]], [b_junk, b_ss], accum_out=ss[:, j:j + 1])
        kb.ts("dve", rstd[:, 0:n], ss[:, 0:n], 1.0 / D, EPS, ALU.mult, ALU.add, [b_ss], [b_rstd])
        kb.act(rstd[:, 0:n], rstd[:, 0:n], AF.Sqrt, [b_rstd], [b_rstd])
        kb.op("dve", lambda: nc.vector.reciprocal(out=rstd[:, 0:n], in_=rstd[:, 0:n]), [b_rstd], [b_rstd])
        ident = self.cbs(CB_IDENT)
        for j, i in enumerate(tiles):
            s = j % 2
            kb.act(hb[:, s, :], self.x[:, i, :], AF.Copy, [self.xb[i], b_rstd], [b_hb[s]], scale=rstd[:, j:j + 1])
            pt = psum_t[s]
            for c in range(8):
                kb.tr(pt[:, c * 128:(c + 1) * 128], hb[:, s, c * 128:(c + 1) * 128], ident,
                      [b_hb[s], self.cb_buf], [psum_bufs[s]])
            kb.tt("dve", hT[:, :, (joff + j) * 128:(joff + j + 1) * 128], pt[:].rearrange("p (c t) -> p c t", c=8),
                  gfm[:].unsqueeze(2).broadcast_to([128, 8, 128]), ALU.mult,
                  [psum_bufs[s], b_g], [hT_bufs[joff + j]])

    def phase_proj(self, b, l):
        kb = self.kb
        nc = self.nc
        d = self.d
        sbf = self.sbuf
        with ExitStack() as ps:
            hT = ps.enter_context(self.S("hT", [128, 8, T], BF16))
            hTb = bufs(NT, "hT")
            cos2 = ps.enter_context(self.S("cos2", [128, T], F32))
            sinpm = ps.enter_context(self.S("sinpm", [128, T], F32))
            b_cos, b_sin = Buf("cos"), Buf("sin")
            pst = [ps.enter_context(self.P(f"pst{i}", [128, 1024], BF16)) for i in range(2)]
            pstb = bufs(2, "pst")
            psm = [ps.enter_context(self.P(f"psm{i}", [128, 512], F32)) for i in range(6)]
            psmb = bufs(6, "psm")
            with ExitStack() as ps2:
                posi = ps2.enter_context(self.S("posi", [128, T], I32))
                ang = ps2.enter_context(self.S("ang", [128, T], F32))
                tmpf = ps2.enter_context(self.S("tmpf", [128, T], F32))
                tmpi = ps2.enter_context(self.S("tmpi", [128, T], I32))
                b_posi, b_ang, b_tf, b_ti = Buf(), Buf(), Buf(), Buf()
                kb.dma("sp", posi[:], d["pos"][b:b + 1, :].partition_broadcast(128), (), [b_posi])
                kb.copy("dve", ang[:], posi[:], [b_posi], [b_ang])
                kb.ts("dve", ang[:], ang[:], self.cstf[:, CF_INVF:CF_INVF + 1], None, ALU.mult, None,
                      [b_ang, self.cf_buf], [b_ang])
                for which in ("sin", "cos"):
                    dst, db = (sinpm, b_sin) if which == "sin" else (cos2, b_cos)
                    if which == "cos":
                        kb.ts("dve", ang[:], ang[:], PI / 2, None, ALU.add, None, [b_ang], [b_ang])
                    kb.ts("dve", tmpi[:], ang[:], 1.0 / (2 * PI), None, ALU.mult, None, [b_ang], [b_ti])
                    kb.copy("dve", tmpf[:], tmpi[:], [b_ti], [b_tf])
                    kb.stt("dve", tmpf[:], tmpf[:], -2 * PI, ang[:], ALU.mult, ALU.add, [b_tf, b_ang], [b_tf])
                    kb.ts("dve", tmpf[:], tmpf[:], -3.1415925, 3.1415925, ALU.max, ALU.min, [b_tf], [b_tf])
                    kb.act(dst[:], tmpf[:], AF.Sin, [b_tf], [db])
                kb.ts("dve", sinpm[:], sinpm[:], self.cstf[:, CF_SGN:CF_SGN + 1], None, ALU.mult, None,
                      [b_sin, self.cf_buf], [b_sin])
                kb.barrier()
            ntag = f"p{self.uid}"
            wbuf = [ps.enter_context(self.S(f"wbuf{i}", [128, 8, 512], BF16)) for i in range(3)]
            wb = bufs(3, "wbuf")
            stg = [ps.enter_context(self.S(f"stg{i}", [128, T], BF16)) for i in range(3)]
            stgb = bufs(3, "stg")
            stt_ = [ps.enter_context(self.S(f"stt{i}", [128, 512], BF16)) for i in range(4)]
            sttb = bufs(4, "stt")
            rt = [ps.enter_context(self.S(f"rt{i}", [128, 512], F32)) for i in range(4)]
            rtb = bufs(4, "rt")
            def fm_dest(ch):
                if ch < 8:
                    p = ch // 2
                    return [(d["qTn"][2 * p], sbf["qTn"][2 * p]), (d["qTn"][2 * p + 1], sbf["qTn"][2 * p + 1])]
                if ch < 14:
                    nm = ("kTc", "kTs", "kTw")[(ch - 8) // 2]
                    return [(d[nm][0], sbf[nm][0]), (d[nm][1], sbf[nm][1])]
                if ch == 14:
                    return [(d["vTc"][0], sbf["vTc"][0]), (d["vTc"][1], sbf["vTc"][1])]
                if ch < 19:
                    p = ch - 15
                    return [(d["qTsb"][2 * p], sbf["qTsb"][2 * p]), (d["qTsb"][2 * p + 1], sbf["qTsb"][2 * p + 1])]
                p = ch - 19
                return [(d["kTsb"][2 * p], sbf["kTsb"][2 * p]), (d["kTsb"][2 * p + 1], sbf["kTsb"][2 * p + 1])]

            pi = 0
            si = 0
            ri = 0
            def issue_w(g):
                if g < 12:
                    kb.dma("pool", wbuf[g % 3][:], d["win"][l, g], (), [wb[g % 3]])
            issue_w(0)
            issue_w(1)
            for g in range(6):
                w = wbuf[g % 3]
                wbb = wb[g % 3]
                issue_w(g + 2)
                chunks = [g * 4 + c for c in range(4) if g * 4 + c < 23]
                ci = 0
                while ci < len(chunks):
                    ch = chunks[ci]
                    rope = ch < 14
                    st = stg[si % 3]
                    stb = stgb[si % 3]
                    si += 1
                    for tc in range(4):
                        if g == 0 and ci == 0:
                            self.norm_tiles(ps, [4 * tc + k for k in range(4)], d["gfm"][l, 0], hT, hTb, pst, pstb,
                                            ntag, joff=4 * tc)
                        rd = [hTb[4 * tc + k] for k in range(4)] + [wbb]
                        pa, pab = psm[pi % 6], psmb[pi % 6]
                        pi += 1
                        for k in range(8):
                            kb.mm(pa[:], w[:, k, (ch % 4) * 128:(ch % 4 + 1) * 128], hT[:, k, tc * 512:(tc + 1) * 512],
                                  k == 0, k == 7, rd, [pab])
                        if rope:
                            pb_, pbb = psm[pi % 6], psmb[pi % 6]
                            pi += 1
                            for k in range(8):
                                kb.mm(pb_[:], w[:, k, (ch % 4 + 1) * 128:(ch % 4 + 2) * 128],
                                      hT[:, k, tc * 512:(tc + 1) * 512], k == 0, k == 7, rd, [pbb])
                            r1, r1b = rt[ri % 4], rtb[ri % 4]
                            r2, r2b = rt[(ri + 1) % 4], rtb[(ri + 1) % 4]
                            ri += 2
                            kb.tt("dve", r1[:], pa[:], cos2[:, tc * 512:(tc + 1) * 512], ALU.mult, [pab, b_cos], [r1b])
                            kb.tt("dve", r2[:], pb_[:], sinpm[:, tc * 512:(tc + 1) * 512], ALU.mult, [pbb, b_sin], [r2b])
                            kb.tt("pool", st[:, tc * 512:(tc + 1) * 512], r1[:], r2[:], ALU.add, [r1b, r2b], [stb])
                        elif 15 <= ch < 19:
                            kb.act(st[:, tc * 512:(tc + 1) * 512], pa[:], AF.Copy, [pab], [stb], scale=SCALE)
                        else:
                            kb.copy("act", st[:, tc * 512:(tc + 1) * 512], pa[:], [pab], [stb])
                    for half, (dap, dbuf) in enumerate(fm_dest(ch)):
                        kb.dma("sp", dap, st[half * 64:(half + 1) * 64, :], [stb], [dbuf])
                    ci += 2 if rope else 1
            for g in range(6, 12):
                w = wbuf[g % 3]
                wbb = wb[g % 3]
                issue_w(g + 2)
                for i in range(NT):
                    pa, pab = psm[pi % 6], psmb[pi % 6]
                    pi += 1
                    for k in range(8):
                        kb.mm(pa[:], hT[:, k, i * 128:(i + 1) * 128], w[:, k, :], k == 0, k == 7, [hTb[i], wbb], [pab])
                    s4 = (g * NT + i) % 4
                    st, stb = stt_[s4], sttb[s4]
                    rows = slice(i * 128, (i + 1) * 128)
                    if g == 6:
                        kb.copy("dve", st[:, 0:256], pa[:, 0:256], [pab], [stb])
                        kb.act(self.gn[:, i, :], pa[:, 256:280], AF.Sigmoid, [pab], [self.gnb[i]])
                        kb.dma("sp", d["vS"][rows, :], st[:, 0:128], [stb], [sbf["vS"][i]])
                        kb.dma("sp", d["vW"][rows, :], st[:, 128:256], [stb], [sbf["vW"][i]])
                    elif g == 7:
                        kb.copy("dve", st[:], pa[:], [pab], [stb])
                        kb.dma("sp", d["vSB"][rows, :], st[:], [stb], [sbf["vSB"][i]])
                    else:
                        kb.act(st[:], pa[:], AF.Sigmoid, [pab], [stb])
                        kb.dma("sp", d["gm"][rows, (g - 8) * 512:(g - 7) * 512], st[:], [stb], [sbf["gm"][i][g - 8]])
            kb.barrier()

    def phase_attn(self, b, l, ls):
        kb = self.kb
        nc = self.nc
        d = self.d
        sbf = self.sbuf
        self.o_acc = ls.enter_context(self.S("o_acc", [128, NT, 512], F32))
        self.o_accb = [bufs(8, f"oacc{i}_") for i in range(NT)]
        self.o_sb = ls.enter_context(self.S("o_sb", [128, NT, 512], BF16))
        self.o_sbb = [bufs(8, f"osb{i}_") for i in range(NT)]
        self.nsa(b, l)
        if self.dbg:
            kb.dma("sp", d["dbg_onsa"][:, :, :], self.o_acc[:], [x for r in self.o_accb for x in r], ())
        if self.stop("nsa"):
            return
        mw = {}
        for nm, shp in (("wpn", [128, 4, 1024]), ("wps", [128, 4, 1024]), ("wout", [128, 8, 1024])):
            t_ = ls.enter_context(self.S("mw_" + nm, shp, BF16))
            b_ = Buf(nm)
            kb.dma("pool", t_[:], d[nm][l], (), [b_])
            mw[nm] = (t_, b_)
        self.mw = mw
        self.sbattn(b, l)
        if self.dbg:
            kb.dma("sp", d["dbg_osb"][:, :, :], self.o_sb[:], [x for r in self.o_sbb for x in r], ())

    def nsa(self, b, l):
        kb = self.kb
        nc = self.nc
        d = self.d
        sbf = self.sbuf
        cstb, cstf = self.cstb, self.cstf
        CB, CFb = self.cb_buf, self.cf_buf
        with ExitStack() as ps:
            def sbt(name, shape, dt):
                return ps.enter_context(self.S(name, shape, dt))
            qT = sbt("qT", [96, NT, 4, 128], BF16); qTb = bufs(4, "qT")
            kbuf = sbt("kbuf", [64, 4, T], BF16); kbb = bufs(4, "kbuf")
            ksel = sbt("ksel", [96, T], BF16); kselb = Buf("ksel"); kselEb = Buf("kselE")
            cmpm = sbt("cmpm", [128, T], BF16); cmpmb = Buf("cmpm")
            kb.dma("pool", cmpm[:], d["cstb"][:, CB_CMPM:CB_CMPM + T], (), [cmpmb])
            kb.dma("pool", ksel[64:96, :], d["cstb"][0:32, CB_E:CB_E + T], (), [kselEb])
            Vs = sbt("Vs", [128, NT, 65], BF16); Vsb = Buf("Vs")
            Vw = sbt("Vw", [128, NT, 65], BF16); Vwb = Buf("Vw")
            selTb = bufs(NT, "selT")
            w1 = [sbt("w1k", [64, 2048], BF16), sbt("w1v", [64, 2048], BF16)]; w1b = bufs(2, "w1")
            peT = sbt("peT", [64, 2, 32], BF16); peTb = Buf()
            w2 = sbt("w2", [64, 2, 64], BF16); w2b = Buf()
            b1 = sbt("b1", [64, 2], F32); b1b = Buf()
            biasb = sbt("biasb", [64, 2], F32); biasbb = Buf()
            hc = sbt("hc", [64, 128], BF16); hcb = Buf()
            kcT = sbt("kcT", [64, 128], BF16); kcTb = Buf()
            vcx = sbt("vcx", [128, 97], F32); vcxb = Buf()
            em = [sbt(f"em{i}", [128, 512], F32) for i in range(2)]; emb = bufs(2, "em")
            rden = sbt("rden", [128, 8], F32); rdenb = Buf()
            scg = sbt("scg", [128, 8], F32); scgb = Buf()
            sc = [sbt(f"sc{i}", [128, 32], F32) for i in range(2)]; scb = bufs(2, "sc")
            big = sbt("big", [128, 1024], F32); bigb = Buf()
            rank = sbt("rank", [128, 32], F32); rankb = Buf()
            selb = sbt("selb", [128, 32], BF16); selbb = Buf()
            Pt = [sbt(f"Pt{i}", [128, 512], BF16) for i in range(7)]; Ptb = bufs(7, "Pt")
            rden2 = [sbt(f"rden2_{i}", [128, 4], F32) for i in range(6)]; rden2b = bufs(6, "rden2")
            pa = [ps.enter_context(self.P(f"pa{i}", [128, 512], F32)) for i in range(8)]
            pab = bufs(8, "pa")
            pstv = pa[3][0:32, 400:464].bitcast(BF16)
            pstb = pab[3]

            kb.dma("pool", w1[0][:], d["wck1"][l], (), [w1b[0]])
            kb.dma("pool", w1[1][:], d["wcv1"][l], (), [w1b[1]])
            kb.dma("pool", peT[:], d["peT"][l].rearrange("k d l -> d k l"), (), [peTb])
            kb.dma("pool", w2[:], d["w2"][l].rearrange("k a b -> a k b"), (), [w2b])
            kb.dma("sp", b1[:], d["b1"][l], (), [b1b])
            kb.memset("pool", Vs[:, :, 64:65], 1.0, [Vsb])
            kb.memset("pool", Vw[:, :, 64:65], 1.0, [Vwb])
            kb.memset("dve", vcx[:], 0.0, [vcxb])
            kb.memset("dve", vcx[:, 64:65], 1.0, [vcxb])
            kb.copy("dve", vcx[:, 65:97], cstf[:, CF_OV:CF_OV + 32], [CFb], [vcxb])
            ident = self.cbs(CB_IDENT)
            mcausal = self.cbs(CB_MCAUSAL)
            mwinlow = self.cbs(CB_MWINLOW)

            for h in range(2):
                for g in range(4):
                    kb.dma("sp", qT[0:64, :, g, :], d["qTn"][4 * h + g].rearrange("d (i t) -> d i t", t=128),
                           [sbf["qTn"][4 * h + g]], [qTb[g]])
                for slot, nm in enumerate(("kTc", "vTc", "kTs", "kTw")):
                    if slot == 2:
                        kb.dma("sp", ksel[0:64, :], d[nm][h], [sbf[nm][h]], [kselb])
                    else:
                        kb.dma("sp", kbuf[:, slot, :], d[nm][h], [sbf[nm][h]], [kbb[slot]])
                kb.dma("sp", Vs[:, :, 0:64], d["vS"].rearrange("(i p) c -> p i c", p=128)[:, :, h * 64:(h + 1) * 64],
                       sbf["vS"], [Vsb])
                kb.dma("sp", Vw[:, :, 0:64], d["vW"].rearrange("(i p) c -> p i c", p=128)[:, :, h * 64:(h + 1) * 64],
                       sbf["vW"], [Vwb])
                for kv in range(2):
                    A = pa[0][0:64, 0:NCMP]
                    for li in range(32):
                        kb.mm(A, w1[kv][:, li * 64:(li + 1) * 64], kbuf[:, kv, li:li + 2017:16], li == 0, li == 31,
                              [w1b[kv], kbb[kv]], [pab[0]])
                    Bv = pa[1][0:64, 0:1]
                    for li in range(32):
                        kb.mm(Bv, w1[kv][:, li * 64:(li + 1) * 64], peT[:, kv, li:li + 1], li == 0, li == 31,
                              [w1b[kv], peTb], [pab[1]])
                    kb.tt("dve", biasb[:, kv:kv + 1], Bv, b1[:, kv:kv + 1], ALU.add, [pab[1], b1b], [biasbb])
                    kb.act(hc[:, 0:NCMP], A, AF.Gelu_apprx_tanh, [pab[0], biasbb], [hcb], bias=biasb[:, kv:kv + 1])
                    if kv == 0:
                        o2 = pa[1][0:64, 128:128 + NCMP]
                        kb.mm(o2, w2[:, 0, :], hc[:, 0:NCMP], True, True, [w2b, hcb], [pab[1]])
                        kb.copy("dve", kcT[:, 0:NCMP], o2, [pab[1]], [kcTb])
                    else:
                        o2 = pa[1][0:NCMP, 256:320]
                        kb.mm(o2, hc[:, 0:NCMP], w2[:, 1, :], True, True, [w2b, hcb], [pab[1]])
                        kb.copy("dve", vcx[0:NCMP, 0:64], o2, [pab[1]], [vcxb])
                state = {"s": 0, "p": 0, "r": 0}
                SB_ = [0, 1, 2, 6, 7]

                def next_S():
                    j = SB_[state["s"] % 5]
                    state["s"] += 1
                    return pa[j], pab[j]

                def cmp_tile(i):
                    tsl = slice(i * 128, (i + 1) * 128)
                    S, Sb = next_S()
                    S3 = S[0:NCMP, :].rearrange("p (g t) -> p g t", g=4)
                    kb.mm(S3, kcT[:, 0:NCMP], qT[0:64, i, :, :], True, True, [kcTb] + qTb, [Sb])
                    yield
                    e_, eb_ = em[i % 2], emb[i % 2]
                    kb.act(e_[0:NCMP, :], S[0:NCMP, :], AF.Exp, [Sb], [eb_], scale=SCALE)
                    e3 = e_[0:NCMP, :].rearrange("p (g t) -> p g t", g=4)
                    kb.tt("dve", e3, e3, cmpm[0:NCMP, i * 128:(i + 1) * 128].unsqueeze(1).broadcast_to([NCMP, 4, 128]),
                          ALU.mult, [eb_, cmpmb], [eb_])
                    yield
                    PV, PVb = pa[3], pab[3]
                    PV3 = PV[:, 0:388].rearrange("p (g c) -> p g c", g=4)
                    for g in range(4):
                        kb.mm(PV3[:, g, :], e_[0:NCMP, g * 128:(g + 1) * 128], vcx[0:NCMP, :], g == 0, g == 3,
                              [eb_, vcxb], [PVb])
                    yield
                    r4 = rden[:, 0:4]
                    kb.ts("dve", r4, PV3[:, :, 64], 1e-30, None, ALU.max, None, [PVb], [rdenb])
                    yield
                    kb.op("dve", lambda: nc.vector.reciprocal(out=r4, in_=r4), [rdenb], [rdenb])
                    yield
                    prev = cstf[:, CF_BV + i * 32:CF_BV + (i + 1) * 32]
                    prevb = CFb
                    for g in range(4):
                        hh = 4 * h + g
                        s_, sb_ = sc[g % 2], scb[g % 2]
                        kb.stt("dve", s_[:], PV3[:, g, 65:97], rden[:, g:g + 1], prev, ALU.mult, ALU.add,
                               [PVb, rdenb, prevb], [sb_])
                        prev, prevb = s_[:], sb_
                        kb.stt("dve", self.o_acc[:, i, hh * 64:(hh + 1) * 64], PV3[:, g, 0:64], rden[:, g:g + 1],
                               self.gn[:, i, 3 * hh:3 * hh + 1].broadcast_to([128, 64]), ALU.mult, ALU.mult,
                               [PVb, rdenb, self.gnb[i]], [self.o_accb[i][hh]])
                    yield
                    big3 = big[:].rearrange("p (a c) -> p a c", a=32)
                    kb.tt("dve", big3, prev.unsqueeze(1).broadcast_to([128, 32, 32]),
                          prev.unsqueeze(2).broadcast_to([128, 32, 32]), ALU.is_gt, [prevb], [bigb])
                    yield
                    kb.op("dve", lambda: nc.vector.reduce_sum(out=rank[:], in_=big3, axis=AX.X), [bigb], [rankb])
                    yield
                    kb.ts("dve", selb[:], rank[:], 16.0, NEG, ALU.is_ge, ALU.mult, [rankb], [selbb])
                    yield
                    kb.tr(pstv, selb[:], ident, [selbb, CB], [pstb])
                    kb.copy("act", qT[64:96, i, :, :], pstv.unsqueeze(1).broadcast_to([32, 4, 128]),
                            [pstb], [selTb[i]])

                def branch_tile(i, ks, kslot, V, Vb, O, Ob, first, last, masks, gate, use_sel):
                    tsl = slice(i * 128, (i + 1) * 128)
                    S, Sb = next_S()
                    S3 = S[:].rearrange("p (g t) -> p g t", g=4)
                    if use_sel:
                        kb.mm(S3, ksel[:, ks * 128:(ks + 1) * 128], qT[:, i, :, :], True, True,
                              [kselb, kselEb, selTb[i]] + qTb, [Sb])
                    else:
                        kb.mm(S3, kbuf[:, kslot, ks * 128:(ks + 1) * 128], qT[0:64, i, :, :], True, True,
                              [kbb[kslot]] + qTb, [Sb])
                    yield
                    P, Pb = Pt[state["p"] % 7], Ptb[state["p"] % 7]
                    state["p"] += 1
                    kb.act(P[:], S[:], AF.Exp, [Sb], [Pb], scale=SCALE)
                    P3 = P[:].rearrange("p (g t) -> p g t", g=4)
                    for m in masks:
                        kb.tt("pool", P3, P3, m.unsqueeze(1).broadcast_to([128, 4, 128]), ALU.mult, [Pb, CB], [Pb])
                    yield
                    O3 = O[:, 0:260].rearrange("p (g c) -> p g c", g=4)
                    for g in range(4):
                        kb.mm(O3[:, g, :], P[:, g * 128:(g + 1) * 128], V[:, ks, :], first and g == 0, last and g == 3,
                              [Pb, Vb], [Ob])
                    if last:
                        yield
                        r_, rb_ = rden2[state["r"] % 6], rden2b[state["r"] % 6]
                        state["r"] += 1
                        kb.op("dve", lambda: nc.vector.reciprocal(out=r_[:], in_=O3[:, :, 64]), [Ob], [rb_])
                        yield
                        kb.tt("dve", r_[:], r_[:], self.gn[:, i, 12 * h + gate:12 * h + 12:3], ALU.mult,
                              [rb_, self.gnb[i]], [rb_])
                        yield
                        for g in range(4):
                            hh = 4 * h + g
                            oa = self.o_acc[:, i, hh * 64:(hh + 1) * 64]
                            kb.stt("dve", oa, O3[:, g, 0:64], r_[:, g:g + 1], oa, ALU.mult, ALU.add,
                                   [Ob, rb_, self.o_accb[i][hh]], [self.o_accb[i][hh]])

                def group_tiles(i):
                    Os, Osb = pa[4], pab[4]
                    Ow, Owb = pa[5], pab[5]
                    for ks in range(i + 1):
                        yield branch_tile(i, ks, 2, Vs, Vsb, Os, Osb, ks == 0, ks == i,
                                          [mcausal] if ks == i else [], 1, True)
                    lo = max(0, i - 4)
                    for ks in range(lo, i + 1):
                        masks = []
                        if ks == i:
                            masks.append(mcausal)
                        if ks == i - 4:
                            masks.append(mwinlow)
                        yield branch_tile(i, ks, 3, Vw, Vwb, Ow, Owb, ks == lo, ks == i, masks, 2, False)

                DEPTH_ = 5
                active = []

                def step_all():
                    for g_ in list(active):
                        try:
                            next(g_)
                        except StopIteration:
                            active.remove(g_)

                def start(g_):
                    while len(active) >= DEPTH_:
                        step_all()
                    active.append(g_)

                cmp_g = {i: cmp_tile(i) for i in range(NT)}
                start(cmp_g[0])
                for i in range(NT):
                    while any(g_ is cmp_g[i] for g_ in active):
                        step_all()
                    if i + 1 < NT:
                        start(cmp_g[i + 1])
                    for g_ in group_tiles(i):
                        start(g_)
                while active:
                    step_all()
            kb.barrier()

    def sbattn(self, b, l):
        kb = self.kb
        nc = self.nc
        d = self.d
        sbf = self.sbuf
        cstb = self.cstb
        CB = self.cb_buf
        NCH = 4
        with ExitStack() as ps:
            def sbt(name, shape, dt):
                return ps.enter_context(self.S(name, shape, dt))
            qs = [sbt(f"qs{i}", [128, T], BF16) for i in range(2)]; qsb = [bufs(2, f"qs{i}_") for i in range(2)]
            ksb_ = [sbt(f"ks{i}", [128, T], BF16) for i in range(2)]; ksbb = [bufs(2, f"ks{i}_") for i in range(2)]
            vsb = [sbt(f"vsb{i}", [128, NT, 128], BF16) for i in range(2)]; vsbb = bufs(2, "vsb")
            et = [sbt(f"e{i}", [128, 512], F32) for i in range(NCH)]; etb = bufs(NCH, "e")
            Lt = [sbt(f"L{i}", [128, 512], BF16) for i in range(NCH)]; Ltb = bufs(NCH, "L")
            Ls = [sbt(f"Ls{i}", [128, 512], BF16) for i in range(NCH)]; Lsb = bufs(NCH, "Ls")
            wt = [sbt(f"w{i}", [128, 512], BF16) for i in range(NCH)]; wtb = bufs(NCH, "w")
            Ob_ = [ps.enter_context(self.P(f"sbO{i}", [128, 512], F32)) for i in range(NCH)]; Obb = bufs(NCH, "sbO")
            zb_ = [ps.enter_context(self.P(f"sbz{i}", [128, 512], F32)) for i in range(NCH)]
            zbb = bufs(NCH, "sbz")
            tri = self.cbs(CB_TRI)
            onesn = self.cbs(CB_ONES)
            mstrict = self.cbs(CB_MSTRICT)
            state = {"z": 0}

            def load_pair(hp):
                s = hp % 2
                for a in range(2):
                    kb.dma("sp", qs[s][a * 64:(a + 1) * 64, :], d["qTsb"][2 * hp + a], [sbf["qTsb"][2 * hp + a]], [qsb[s][a]])
                    kb.dma("sp", ksb_[s][a * 64:(a + 1) * 64, :], d["kTsb"][2 * hp + a], [sbf["kTsb"][2 * hp + a]], [ksbb[s][a]])
                kb.dma("sp", vsb[s][:], d["vSB"].rearrange("(i p) c -> p i c", p=128)[:, :, hp * 128:(hp + 1) * 128],
                       sbf["vSB"], [vsbb[s]])

            def chain(slot, hp, a, tq):
                s = hp % 2
                base = a * 64
                head = 2 * hp + a
                O, Ob = Ob_[slot], Obb[slot]
                O3 = O[:, 0:256].rearrange("p (c e) -> p c e", c=4)
                Lsum, Lsumb = Ls[slot], Lsb[slot]
                kb.memset("pool", Lsum[:], 0.0, [Lsumb])
                nks = 4 * tq + 4
                firstO = True
                for ks in range(nks - 1, -1, -1):
                    c0 = max(0, ks - 4 * tq) * 128
                    N = 512 - c0
                    diag = ks >= 4 * tq
                    first = ks == nks - 1
                    z, zb = zb_[slot], zbb[slot]
                    kb.mm(z[:, 0:N], ksb_[s][base:base + 64, ks * 128:(ks + 1) * 128],
                          qs[s][base:base + 64, tq * 512 + c0:(tq + 1) * 512], True, False,
                          [ksbb[s][a], qsb[s][a]], [zb])
                    yield
                    e, eb = et[slot], etb[slot]
                    kb.act(e[:, 0:N], z[:, 0:N], AF.Exp, [zb], [eb])
                    if diag:
                        kb.tt("dve", e[:, 0:128], e[:, 0:128], mstrict, ALU.mult, [eb, CB], [eb])
                    yield
                    Lx, Lxb = Lt[slot], Ltb[slot]
                    kb.act(Lx[:, 0:N], e[:, 0:N], AF.Ln, [eb], [Lxb], bias=1.0)
                    yield
                    kb.mm(z[:, 0:N], tri, Lx[:, 0:N], False, first, [CB, Lxb], [zb])
                    if not first:
                        kb.mm(z[:, 0:N], onesn, Lsum[:, c0:512], False, True, [CB, Lsumb], [zb])
                    if ks > 0:
                        kb.tt("dve", Lsum[:, c0:512], Lsum[:, c0:512], Lx[:, 0:N], ALU.add, [Lsumb, Lxb], [Lsumb])
                    yield
                    w, wb = wt[slot], wtb[slot]
                    kb.act(w[:, 0:N], z[:, 0:N], AF.Exp, [zb], [wb])
                    if diag:
                        kb.tt("dve", w[:, 0:128], w[:, 0:128], mstrict, ALU.mult, [wb, CB], [wb])
                    yield
                    for c in range(c0 // 128, 4):
                        kb.mm(O3[:, c, :], w[:, c * 128 - c0:(c + 1) * 128 - c0], vsb[s][:, ks, a * 64:(a + 1) * 64],
                              firstO, ks == 0 and c == 3, [wb, vsbb[s]], [Ob])
                        firstO = False
                    yield
                kb.copy("dve", self.o_sb[:, 4 * tq:4 * tq + 4, head * 64:(head + 1) * 64], O3,
                        [Ob], [self.o_sbb[4 * tq + c][head] for c in range(4)])

            def all_chains():
                for hp in range(4):
                    for tq in (3, 2, 1, 0):
                        for a in range(2):
                            yield (hp, a, tq)

            load_pair(0)
            load_pair(1)
            remaining = {hp: 8 for hp in range(4)}
            free = list(range(NCH))
            active = []
            gen = all_chains()
            done = False
            while True:
                while not done and free:
                    try:
                        hp, a, tq = next(gen)
                    except StopIteration:
                        done = True
                        break
                    sl = free.pop(0)
                    active.append((sl, hp, chain(sl, hp, a, tq)))
                if not active:
                    break
                for item in list(active):
                    sl, hp, g = item
                    try:
                        next(g)
                    except StopIteration:
                        active.remove(item)
                        free.append(sl)
                        remaining[hp] -= 1
                        if remaining[hp] == 0 and hp + 2 < 4:
                            load_pair(hp + 2)
            kb.barrier()

    def post_norm_add(self, i, m, mb, gbc, gbcb, tmp, add_eng="pool"):
        kb = self.kb
        nc = self.nc
        ssq, ssqb, rs, rsb, junk, junkb, tn, tnb = tmp
        kb.memset("dve", ssq[:], 0.0, [ssqb])
        for hf in range(2):
            kb.act(junk[:], m[hf][:], AF.Square, [mb[hf]], [junkb, ssqb], accum_out=ssq[:, hf:hf + 1],
                   scale=float(D) ** -0.5)
        kb.tt("dve", rs[:], ssq[:, 0:1], ssq[:, 1:2], ALU.add, [ssqb], [rsb])
        kb.act(rs[:], rs[:], AF.Sqrt, [rsb], [rsb], bias=self.epsc[:, 0:1])
        kb.op("dve", lambda: nc.vector.reciprocal(out=rs[:], in_=rs[:]), [rsb], [rsb])
        for hf in range(2):
            kb.stt("dve", tn[:, hf * 512:(hf + 1) * 512], m[hf][:], rs[:, 0:1], gbc[:, hf * 512:(hf + 1) * 512],
                   ALU.mult, ALU.mult, [mb[hf], rsb, gbcb], [tnb])
        kb.tt(add_eng, self.x[:, i, :], self.x[:, i, :], tn[:], ALU.add, [self.xb[i], tnb], [self.xb[i]])

    def phase_merge(self, b, l):
        kb = self.kb
        nc = self.nc
        d = self.d
        sbf = self.sbuf
        CB = self.cb_buf
        ident = self.cbs(CB_IDENT)
        with ExitStack() as ps:
            def sbt(name, shape, dt):
                return ps.enter_context(self.S(name, shape, dt))
            wpn, wpnb = self.mw["wpn"]
            wps, wpsb = self.mw["wps"]
            wout, woutb = self.mw["wout"]
            gbc = sbt("m_gbc", [128, 1024], F32); gbcb = Buf()
            kb.dma("sp", gbc[:], d["gbc"][l, 0:1, :].partition_broadcast(128), (), [gbcb])
            gmt = [sbt(f"m_gmt{j}", [128, 2048], BF16) for j in range(2)]; gmtb = bufs(2)
            onb = [sbt(f"m_onb{j}", [128, 512], BF16) for j in range(2)]; onbb = bufs(2)
            oT = [sbt(f"m_oT{j}", [128, 8, 128], BF16) for j in range(2)]; oTb = bufs(2)
            t1 = [sbt(f"m_t1{j}", [128, 1024], F32) for j in range(2)]; t1b = bufs(2)
            t2 = [sbt(f"m_t2{j}", [128, 1024], F32) for j in range(2)]; t2b = bufs(2)
            yb = [sbt(f"m_yb{j}", [128, 1024], BF16) for j in range(2)]; ybb = bufs(2)
            yT = [sbt(f"m_yT{j}", [128, 8, 128], BF16) for j in range(2)]; yTb = bufs(2)
            tmp = (sbt("m_ssq", [128, 2], F32), Buf(), sbt("m_rs", [128, 1], F32), Buf(),
                   sbt("m_junk", [128, 512], BF16), Buf(), sbt("m_tn", [128, 1024], F32), Buf())
            pT = [ps.enter_context(self.P(f"m_pT{j}", [128, 1024], BF16)) for j in range(2)]; pTb = bufs(2)
            yn = [ps.enter_context(self.P(f"m_yn{j}", [128, 512], F32)) for j in range(2)]; ynb = bufs(2)
            ys = [ps.enter_context(self.P(f"m_ys{j}", [128, 512], F32)) for j in range(2)]; ysb = bufs(2)
            mm_ = [ps.enter_context(self.P(f"m_m{j}", [128, 512], F32)) for j in range(2)]; mmb = bufs(2)
            def merge_tile(i):
                j = i % 2
                pTj, pTjb = pT[j], pTb[j]
                kb.dma("sp", gmt[j][:], d["gm"][i * 128:(i + 1) * 128, :], sbf["gm"][i], [gmtb[j]])
                kb.copy("act", onb[j][:], self.o_acc[:, i, :], self.o_accb[i], [onbb[j]])
                for c in range(4):
                    kb.tr(pTj[:, c * 128:(c + 1) * 128], onb[j][:, c * 128:(c + 1) * 128], ident, [onbb[j], CB], [pTjb])
                for c in range(4):
                    kb.tr(pTj[:, (4 + c) * 128:(5 + c) * 128], self.o_sb[:, i, c * 128:(c + 1) * 128], ident,
                          self.o_sbb[i] + [CB], [pTjb])
                yield
                kb.copy("dve", oT[j][:].rearrange("p c t -> p (c t)"), pTj[:], [pTjb], [oTb[j]])
                yield
                for hf in range(2):
                    for k in range(4):
                        kb.mm(yn[hf][:], oT[j][:, k, :], wpn[:, k, hf * 512:(hf + 1) * 512], k == 0, k == 3,
                              [oTb[j], wpnb], [ynb[hf]])
                    for k in range(4):
                        kb.mm(ys[hf][:], oT[j][:, 4 + k, :], wps[:, k, hf * 512:(hf + 1) * 512], k == 0, k == 3,
                              [oTb[j], wpsb], [ysb[hf]])
                    sl = slice(hf * 512, (hf + 1) * 512)
                    kb.tt("dve", t1[j][:, sl], yn[hf][:], gmt[j][:, hf * 512:(hf + 1) * 512], ALU.mult, [ynb[hf], gmtb[j]], [t1b[j]])
                    kb.tt("dve", t2[j][:, sl], ys[hf][:], gmt[j][:, 1024 + hf * 512:1024 + (hf + 1) * 512], ALU.mult,
                          [ysb[hf], gmtb[j]], [t2b[j]])
                    yield
                kb.tt("pool", yb[j][:], t1[j][:], t2[j][:], ALU.add, [t1b[j], t2b[j]], [ybb[j]])
                yield
                for c in range(8):
                    kb.tr(pTj[:, c * 128:(c + 1) * 128], yb[j][:, c * 128:(c + 1) * 128], ident, [ybb[j], CB], [pTjb])
                yield
                kb.copy("act", yT[j][:].rearrange("p c t -> p (c t)"), pTj[:], [pTjb], [yTb[j]])
                yield
                for hf in range(2):
                    for k in range(8):
                        kb.mm(mm_[hf][:], yT[j][:, k, :], wout[:, k, hf * 512:(hf + 1) * 512], k == 0, k == 7,
                              [yTb[j], woutb], [mmb[hf]])
                self.post_norm_add(i, mm_, mmb, gbc, gbcb, tmp)

            run_pipelined((merge_tile(i) for i in range(NT)), 2)
            kb.barrier()

    def phase_ffn(self, b, l):
        kb = self.kb
        nc = self.nc
        d = self.d
        CB = self.cb_buf
        with ExitStack() as ps:
            def sbt(name, shape, dt):
                return ps.enter_context(self.S(name, shape, dt))
            wd = sbt("f_wd", [128, NCP, 1024], BF16); wdb = bufs(2)
            kb.dma("pool", wd[:, 0:11, :], d["wdn"][l][:, 0:11, :], (), [wdb[0]])
            kb.dma("pool", wd[:, 11:22, :], d["wdn"][l][:, 11:22, :], (), [wdb[1]])
            cw = sbt("f_cw", [128, 44, 3], F32); cwb = Buf()
            cbs_ = sbt("f_cb", [128, 44], F32); cbb = Buf()
            gbc = sbt("f_gbc", [128, 1024], F32); gbcb = Buf()
            kb.dma("sp", cw[:], d["cw"][l], (), [cwb])
            kb.dma("sp", cbs_[:], d["cb"][l], (), [cbb])
            kb.dma("sp", gbc[:], d["gbc"][l, 1:2, :].partition_broadcast(128), (), [gbcb])
            Xe = [sbt(f"f_Xe{j}", [128, 44, 4], F32) for j in range(2)]; Xeb = bufs(2, "Xe")
            kb.memset("pool", Xe[0][:], 0.0, [Xeb[0]])
            et = [sbt(f"f_et{j}", [128, 44, 2], F32) for j in range(3)]; etb = bufs(3, "et")
            hT2 = sbt("f_hT2", [128, 8, 512], BF16); hT2b = bufs(4)
            aT = sbt("f_aT", [128, NCP, 512], BF16); aTb = bufs(NCP)
            wub = [sbt(f"f_wub{j}", [128, 8, 256], BF16) for j in range(3)]; wubb = bufs(3)
            cv = [[sbt(f"f_c{p}{j}", [128, 512], F32) for j in range(3)] for p in range(2)]
            cvb = [bufs(3), bufs(3)]
            gl = [sbt(f"f_gl{j}", [128, 512], F32) for j in range(2)]; glb = bufs(2)
            tmp = (sbt("f_ssq", [128, 2], F32), Buf(), sbt("f_rs", [128, 1], F32), Buf(),
                   sbt("f_junk", [128, 512], BF16), Buf(), sbt("f_tn", [128, 1024], F32), Buf())
            pst = [ps.enter_context(self.P(f"f_pst{j}", [128, 1024], BF16)) for j in range(2)]; pstb = bufs(2)
            pu = [ps.enter_context(self.P(f"f_pu{j}", [128, 512], F32)) for j in range(4)]; pub = bufs(4)
            pf = [ps.enter_context(self.P(f"f_pf{j}", [128, 512], F32)) for j in range(2)]; pfb = bufs(2)
            tagn = f"f{self.uid}"
            pi = 0
            nld = 4 * NCP

            def issue_wu(j):
                if j < nld:
                    kb.dma("pool", wub[j % 3][:], d["wup"][l, j % NCP], (), [wubb[j % 3]])
            issue_wu(0)
            issue_wu(1)
            self.norm_tiles(ps, [0, 1, 2, 3], d["gfm"][l, 1], hT2, hT2b, pst, pstb, tagn)
            for qt in range(4):
                for cp in range(NCP):
                    jj = qt * NCP + cp
                    issue_wu(jj + 2)
                    w, wb = wub[jj % 3], wubb[jj % 3]
                    parts = []
                    for part in range(2):
                        ci = part * NCP + cp
                        p_, pb_ = pu[pi % 4], pub[pi % 4]
                        pi += 1
                        for k in range(8):
                            kb.mm(p_[:], w[:, k, part * 128:(part + 1) * 128], hT2[:, k, :], k == 0, k == 7,
                                  hT2b + [wb], [pb_])
                        Xc, Xn = Xe[qt % 2], Xe[(qt + 1) % 2]
                        Xcb, Xnb = Xeb[qt % 2], Xeb[(qt + 1) % 2]
                        c, cb_ = cv[part][cp % 3], cvb[part][cp % 3]
                        kb.act(c[:], p_[:], AF.Identity, [pb_, cwb, cbb], [cb_], scale=cw[:, ci, 2:3],
                               bias=cbs_[:, ci:ci + 1])
                        kb.copy("act", Xc[:, ci, 2:4], p_[:, 0:2], [pb_], [Xcb])
                        kb.copy("act", Xn[:, ci, 0:2], p_[:, 510:512], [pb_], [Xnb])
                        parts.append((ci, p_, pb_, c, cb_))
                    for (ci, p_, pb_, c, cb_) in parts:
                        kb.stt("dve", c[:, 2:512], p_[:, 1:511], cw[:, ci, 1:2], c[:, 2:512], ALU.mult, ALU.add,
                               [pb_, cwb, cb_], [cb_])
                    for (ci, p_, pb_, c, cb_) in parts:
                        kb.stt("dve", c[:, 2:512], p_[:, 0:510], cw[:, ci, 0:1], c[:, 2:512], ALU.mult, ALU.add,
                               [pb_, cwb, cb_], [cb_])

                    def finish(cq):
                        g_, gb_ = gl[cq % 2], glb[cq % 2]
                        kb.act(g_[:, 2:512], cv[0][cq % 3][:, 2:512], AF.Gelu_apprx_tanh, [cvb[0][cq % 3]], [gb_])
                        kb.tt("dve", aT[:, cq, 2:512], g_[:, 2:512], cv[1][cq % 3][:, 2:512], ALU.mult,
                              [gb_, cvb[1][cq % 3]], [aTb[cq]])
                    if cp > 0:
                        finish(cp - 1)
                    if cp == NCP - 1:
                        finish(cp)
                Xc, Xcb = Xe[qt % 2], Xeb[qt % 2]
                kb.tt("dve", et[0][:], Xc[:, :, 2:4], cw[:, :, 2:3].broadcast_to([128, 44, 2]), ALU.mult, [Xcb, cwb], [etb[0]])
                kb.tt("dve", et[1][:], Xc[:, :, 1:3], cw[:, :, 1:2].broadcast_to([128, 44, 2]), ALU.mult, [Xcb, cwb], [etb[1]])
                kb.tt("dve", et[2][:], Xc[:, :, 0:2], cw[:, :, 0:1].broadcast_to([128, 44, 2]), ALU.mult, [Xcb, cwb], [etb[2]])
                kb.tt("dve", et[0][:], et[0][:], et[1][:], ALU.add, [etb[0], etb[1]], [etb[0]])
                kb.tt("dve", et[2][:], et[2][:], cbs_[:].unsqueeze(2).broadcast_to([128, 44, 2]), ALU.add, [etb[2], cbb], [etb[2]])
                kb.tt("dve", et[0][:], et[0][:], et[2][:], ALU.add, [etb[0], etb[2]], [etb[0]])
                kb.act(et[1][:, 0:NCP, :], et[0][:, 0:NCP, :], AF.Gelu_apprx_tanh, [etb[0]], [etb[1]])
                kb.tt("dve", aT[:, :, 0:2], et[1][:, 0:NCP, :], et[0][:, NCP:2 * NCP, :], ALU.mult, [etb[0], etb[1]], aTb)
                if qt + 1 < 4:
                    self.norm_tiles(ps, [4 * (qt + 1) + j for j in range(4)], d["gfm"][l, 1], hT2, hT2b, pst, pstb, tagn)
                for tt_ in range(4):
                    for hf in range(2):
                        for cp in range(NCP):
                            kb.mm(pf[hf][:], aT[:, cp, tt_ * 128:(tt_ + 1) * 128], wd[:, cp, hf * 512:(hf + 1) * 512],
                                  cp == 0, cp == NCP - 1, [aTb[cp], wdb[cp // 11]], [pfb[hf]])
                    self.post_norm_add(4 * qt + tt_, pf, pfb, gbc, gbcb, tmp, "dve")
            kb.barrier()


_PROG_CACHE = {}


def kernel(**inputs):
    inp = {k: np.asarray(v) for k, v in inputs.items()}
    W = prep_weights(inp)
    cb, cf = const_tables()
    if "prog" not in _PROG_CACHE:
        _PROG_CACHE["prog"] = Prog(nb=NB, layers=tuple(range(DEPTH)))
    prog = _PROG_CACHE["prog"]
    x = np.ascontiguousarray(inp["x"], dtype=np.float32)
    pos = np.ascontiguousarray(inp["positions"]).astype(np.int32)
    in_maps = []
    for c in range(N_CORES):
        m = dict(W)
        m["cstb"] = cb
        m["cstf"] = cf
        m["x"] = np.ascontiguousarray(x[c * NB:(c + 1) * NB])
        m["pos"] = np.ascontiguousarray(pos[c * NB:(c + 1) * NB])
        in_maps.append(m)
    res = run_bass_kernel_spmd(prog.nc, in_maps, core_ids=list(range(N_CORES)))
    out = np.concatenate([np.asarray(r["y"]) for r in res.results], axis=0)
    return out.astype(np.float32)
```

```python
import numpy as np
from contextlib import ExitStack
import concourse.bass as bass
import concourse.mybir as mybir
from concourse.bass_utils import run_bass_kernel_spmd

F32 = mybir.dt.float32
BF16 = mybir.dt.bfloat16
I32 = mybir.dt.int32
ALU = mybir.AluOpType
AF = mybir.ActivationFunctionType
AX = mybir.AxisListType

T = 2048
D = 1024
NT = 16
HD = 64
DFF = 2816
NCP = 22
SCALE = 0.125
EPS = 1e-6
NEG = -30000.0
N_CORES = 8
NB = 2
DEPTH = 2
NCMP = 127
PI = 3.14159265358979

CB_IDENT, CB_TRI, CB_ONES, CB_MSTRICT, CB_MCAUSAL, CB_MWINLOW = 0, 128, 256, 384, 512, 640
CB_E = 768
CB_CMPM = CB_E + 2048
CB_N = CB_CMPM + 2048
CF_OV = 0
CF_BV = 32
CF_VALID = CF_BV + 512
CF_INVF = CF_VALID + 512
CF_SGN = CF_INVF + 1
CF_N = CF_SGN + 1


class Buf:
    __slots__ = ("name", "w", "r")

    def __init__(self, name=""):
        self.name = name
        self.w = None
        self.r = {}


def bufs(n, name=""):
    return [Buf(f"{name}{i}") for i in range(n)]


class KB:
    NDMA = 32

    def __init__(self, nc, es):
        self.nc = nc
        self.engs = {"pe": nc.tensor, "act": nc.scalar, "dve": nc.vector, "pool": nc.gpsimd, "sp": nc.sync}
        self.sems = {}
        for e in ("pe", "act", "dve", "pool"):
            self.sems[e] = es.enter_context(nc.semaphore("sem_" + e))
        self.cnt = {e: 0 for e in ("pe", "act", "dve", "pool")}
        self.seen = {e: {} for e in self.engs}
        for i in range(self.NDMA):
            self.sems[("d", i)] = es.enter_context(nc.semaphore(f"semd{i}"))
        self.dma_val = [0] * self.NDMA
        self.dma_next = 0
        self.n_inst = 0
        self.n_wait = 0

    def _wait(self, e, deps):
        eng = self.engs[e]
        seen = self.seen[e]
        for k, v in deps:
            if seen.get(k, 0) < v:
                eng.wait_ge(self.sems[k], v)
                seen[k] = v
                self.n_wait += 1

    def op(self, e, fn, reads=(), writes=()):
        deps = []
        for b in reads:
            if b.w is not None:
                deps.append(b.w)
        for b in writes:
            if b.w is not None and b.w[0] != e:
                deps.append(b.w)
            for k, v in b.r.items():
                if k != e:
                    deps.append((k, v))
        self._wait(e, deps)
        ins = fn()
        self.cnt[e] += 1
        t = self.cnt[e]
        ins.then_inc(self.sems[e], 1)
        self.n_inst += 1
        for b in writes:
            b.w = (e, t)
            b.r = {}
        for b in reads:
            if b.r.get(e, 0) < t:
                b.r[e] = t
        return ins

    def dma(self, q, out, in_, reads=(), writes=()):
        i = self.dma_next
        self.dma_next = (i + 1) % self.NDMA
        key = ("d", i)
        deps = []
        if self.dma_val[i] > 0:
            deps.append((key, self.dma_val[i]))
        for b in reads:
            if b.w is not None:
                deps.append(b.w)
        for b in writes:
            if b.w is not None:
                deps.append(b.w)
            deps.extend(b.r.items())
        self._wait(q, deps)
        ins = self.engs[q].dma_start(out=out, in_=in_)
        self.dma_val[i] += 16
        v = self.dma_val[i]
        ins.then_inc(self.sems[key], 16)
        self.n_inst += 1
        for b in writes:
            b.w = (key, v)
            b.r = {}
        for b in reads:
            b.r[key] = v

    def barrier(self):
        deps = [(e, c) for e, c in self.cnt.items() if c > 0]
        deps += [(("d", i), v) for i, v in enumerate(self.dma_val) if v > 0]
        for e in self.engs:
            self._wait(e, [d for d in deps if d[0] != e])

    def mm(self, out, lhsT, rhs, start, stop, reads, writes):
        return self.op("pe", lambda: self.nc.tensor.matmul(out, lhsT=lhsT, rhs=rhs, start=start, stop=stop),
                       reads, writes)

    def tr(self, out, in_, ident, reads, writes):
        return self.op("pe", lambda: self.nc.tensor.transpose(out, in_, ident), reads, writes)

    def act(self, out, in_, func, reads, writes, **kw):
        return self.op("act", lambda: self.nc.scalar.activation(out=out, in_=in_, func=func, **kw), reads, writes)

    def tt(self, e, out, in0, in1, op, reads, writes):
        return self.op(e, lambda: self.engs[e].tensor_tensor(out=out, in0=in0, in1=in1, op=op), reads, writes)

    def ts(self, e, out, in0, s1, s2, op0, op1, reads, writes):
        if op1 is None:
            return self.op(e, lambda: self.engs[e].tensor_scalar(out=out, in0=in0, scalar1=s1, scalar2=None, op0=op0),
                           reads, writes)
        return self.op(e, lambda: self.engs[e].tensor_scalar(out=out, in0=in0, scalar1=s1, scalar2=s2, op0=op0, op1=op1),
                       reads, writes)

    def stt(self, e, out, in0, scalar, in1, op0, op1, reads, writes):
        return self.op(e, lambda: self.engs[e].scalar_tensor_tensor(out=out, in0=in0, scalar=scalar, in1=in1,
                                                                      op0=op0, op1=op1), reads, writes)

    def copy(self, e, out, in_, reads, writes):
        if e == "act":
            return self.op(e, lambda: self.nc.scalar.copy(out=out, in_=in_), reads, writes)
        return self.op(e, lambda: self.engs[e].tensor_copy(out=out, in_=in_), reads, writes)

    def memset(self, e, ap, val, writes):
        return self.op(e, lambda: self.engs[e].memset(ap, val), (), writes)


def run_pipelined(gens, depth, stagger=0):
    active = []
    gens = iter(gens)
    exhausted = False
    first = True
    while True:
        while not exhausted and len(active) < depth:
            try:
                g = next(gens)
                active.append(g)
                if first and stagger:
                    first = False
                    for _ in range(stagger):
                        next(g)
            except StopIteration:
                exhausted = True
        if not active:
            break
        for g in list(active):
            try:
                next(g)
            except StopIteration:
                active.remove(g)


def _swap_halves(cols):
    return cols.reshape(-1, 2, 32)[:, ::-1, :].reshape(-1)


def w_in_columns():
    def qn(h):
        return np.arange(64 * h, 64 * h + 64)

    def kv(i, h):
        return 512 + i * 128 + h * 64 + np.arange(64)

    def sb(j, h):
        return 1304 + j * 512 + h * 64 + np.arange(64)

    fm = []
    for p in range(4):
        c = np.concatenate([qn(2 * p), qn(2 * p + 1)])
        fm += [c, _swap_halves(c)]
    for i in (0, 2, 4):
        c = np.concatenate([kv(i, 0), kv(i, 1)])
        fm += [c, _swap_halves(c)]
    fm.append(np.concatenate([kv(1, 0), kv(1, 1)]))
    for j in (0, 1):
        for p in range(4):
            fm.append(np.concatenate([sb(j, 2 * p), sb(j, 2 * p + 1)]))
    fm.append(np.full(128, -1))
    assert len(fm) == 24
    cols = np.concatenate(fm)
    g0 = np.concatenate([kv(3, 0), kv(3, 1), kv(5, 0), kv(5, 1), 1280 + np.arange(24), np.full(512 - 280, -1)])
    g1 = 1304 + 2 * 512 + np.arange(512)
    g25 = 2840 + np.arange(2048)
    cols = np.concatenate([cols, g0, g1, g25])
    assert cols.shape[0] == 12 * 512
    return cols


def prep_weights(inp):
    L = DEPTH
    f = np.float32
    out = {}
    cols = w_in_columns()
    valid = cols >= 0
    win = np.zeros((L, 1024, 12 * 512), f)
    for l in range(L):
        win[l][:, valid] = inp["w_in"][l][:, cols[valid]]
    out["win"] = np.ascontiguousarray(win.reshape(L, 8, 128, 12, 512).transpose(0, 3, 2, 1, 4))
    for nm, src in (("wck1", "ck_w1"), ("wcv1", "cv_w1")):
        out[nm] = np.ascontiguousarray(inp[src].reshape(L, 32, 64, 64).transpose(0, 2, 1, 3).reshape(L, 64, 2048))
    out["peT"] = np.ascontiguousarray(np.stack([inp["ck_pe"], inp["cv_pe"]], 1).transpose(0, 1, 3, 2))
    out["w2"] = np.ascontiguousarray(np.stack([inp["ck_w2"], inp["cv_w2"]], 1))
    out["b1"] = np.ascontiguousarray(np.stack([inp["ck_b1"], inp["cv_b1"]], 2))
    out["wpn"] = np.ascontiguousarray(inp["w_proj_nsa"].reshape(L, 4, 128, 1024).transpose(0, 2, 1, 3))
    out["wps"] = np.ascontiguousarray(inp["w_proj_sb"].reshape(L, 4, 128, 1024).transpose(0, 2, 1, 3))
    out["wout"] = np.ascontiguousarray(inp["w_out"].reshape(L, 8, 128, 1024).transpose(0, 2, 1, 3))
    wup = np.zeros((L, NCP, 128, 8, 256), f)
    for cp in range(NCP):
        c = np.concatenate([cp * 128 + np.arange(128), DFF + cp * 128 + np.arange(128)])
        wup[:, cp] = inp["w_up"][:, :, c].reshape(L, 8, 128, 256).transpose(0, 2, 1, 3)
    out["wup"] = wup
    out["wdn"] = np.ascontiguousarray(inp["w_down"].reshape(L, NCP, 128, 1024).transpose(0, 2, 1, 3))
    out["cw"] = np.ascontiguousarray(inp["conv_w"].transpose(0, 2, 1).reshape(L, 44, 128, 3).transpose(0, 2, 1, 3))
    out["cb"] = np.ascontiguousarray(inp["conv_b"].reshape(L, 44, 128).transpose(0, 2, 1))
    out["gfm"] = np.ascontiguousarray(
        np.stack([inp["g_pre_mix"], inp["g_pre_ffn"]], 1).reshape(L, 2, 8, 128).transpose(0, 1, 3, 2))
    out["gbc"] = np.ascontiguousarray(np.stack([inp["g_post_mix"], inp["g_post_ffn"]], 1))
    return {k: np.ascontiguousarray(v, dtype=f) for k, v in out.items()}


def const_tables():
    p = np.arange(128)
    cb = np.zeros((128, CB_N), np.float32)
    cb[:, CB_IDENT:CB_IDENT + 128] = np.eye(128)
    cb[:, CB_TRI:CB_TRI + 128] = -1.0 * (p[:, None] >= p[None, :])
    cb[:, CB_ONES:CB_ONES + 128] = -1.0
    cb[:, CB_MSTRICT:CB_MSTRICT + 128] = p[:, None] < p[None, :]
    cb[:, CB_MCAUSAL:CB_MCAUSAL + 128] = p[:, None] <= p[None, :]
    cb[:, CB_MWINLOW:CB_MWINLOW + 128] = p[:, None] > p[None, :]
    s = np.arange(T)
    cb[:32, CB_E:CB_E + T] = (s[None, :] // 64) == np.arange(32)[:, None]
    n = np.arange(NCMP)
    cb[:NCMP, CB_CMPM:CB_CMPM + T] = (16 * n[:, None] + 31) <= s[None, :]
    cf = np.zeros((128, CF_N), np.float32)
    ci = n[:, None] * 16
    sj = np.arange(32)[None, :] * 64
    cf[:NCMP, CF_OV:CF_OV + 32] = (ci < sj + 64) & (sj < ci + 32)
    t = (np.arange(NT)[None, :, None] * 128 + p[:, None, None])
    j = np.arange(32)[None, None, :]
    valid = (j * 64 <= t)
    cur = t // 64
    forced = (j == 0) | (j == cur) | (j == cur - 1)
    bv = np.where(forced, 1e4, 0.0) + (valid.astype(np.float32) - 1.0)
    cf[:, CF_BV:CF_BV + 512] = bv.reshape(128, 512)
    cf[:, CF_VALID:CF_VALID + 512] = valid.reshape(128, 512)
    half = 32
    inv_freq = (10000.0 ** (-np.arange(half, dtype=np.float32) / half)).astype(np.float32)
    cf[:, CF_INVF] = inv_freq[p % 32]
    cf[:, CF_SGN] = np.where((p % 64) < 32, -1.0, 1.0)
    return cb, cf


class Prog:
    def __init__(self, nb=NB, layers=(0, 1), stop_after=None, dbg=False):
        self.nb = nb
        self.layers = layers
        self.stop_after = stop_after
        self.dbg = dbg
        self.nc = bass.Bass("TRN2", target_bir_lowering=False)
        self.uid = 0
        self.norm_tmp = {}
        self.build()

    def S(self, name, shape, dt):
        self.uid += 1
        return self.nc.sbuf_tensor(f"{name}_u{self.uid}", list(shape), dt)

    def P(self, name, shape, dt):
        self.uid += 1
        return self.nc.psum_tensor(f"{name}_u{self.uid}", list(shape), dt)

    def dram(self, name, shape, dt, kind="ExternalInput"):
        return self.nc.dram_tensor(name, list(shape), dt, kind=kind).ap()

    def build(self):
        nc = self.nc
        L = DEPTH
        nb = self.nb
        skind = "ExternalOutput" if self.dbg else "Internal"
        d = {}
        d["x"] = self.dram("x", [nb, T, D], F32)
        d["pos"] = self.dram("pos", [nb, T], I32)
        d["win"] = self.dram("win", [L, 12, 128, 8, 512], F32)
        d["wck1"] = self.dram("wck1", [L, 64, 2048], F32)
        d["wcv1"] = self.dram("wcv1", [L, 64, 2048], F32)
        d["peT"] = self.dram("peT", [L, 2, 64, 32], F32)
        d["w2"] = self.dram("w2", [L, 2, 64, 64], F32)
        d["b1"] = self.dram("b1", [L, 64, 2], F32)
        d["wpn"] = self.dram("wpn", [L, 128, 4, 1024], F32)
        d["wps"] = self.dram("wps", [L, 128, 4, 1024], F32)
        d["wout"] = self.dram("wout", [L, 128, 8, 1024], F32)
        d["wup"] = self.dram("wup", [L, NCP, 128, 8, 256], F32)
        d["wdn"] = self.dram("wdn", [L, 128, NCP, 1024], F32)
        d["cw"] = self.dram("cw", [L, 128, 44, 3], F32)
        d["cb"] = self.dram("cb", [L, 128, 44], F32)
        d["gfm"] = self.dram("gfm", [L, 2, 128, 8], F32)
        d["gbc"] = self.dram("gbc", [L, 2, 1024], F32)
        d["cstb"] = self.dram("cstb", [128, CB_N], F32)
        d["cstf"] = self.dram("cstf", [128, CF_N], F32)
        d["y"] = self.dram("y", [nb, T, D], F32, kind="ExternalOutput")
        d["qTn"] = self.dram("s_qTn", [8, 64, T], BF16, kind=skind)
        d["kTc"] = self.dram("s_kTc", [2, 64, T], BF16, kind=skind)
        d["vTc"] = self.dram("s_vTc", [2, 64, T], BF16, kind=skind)
        d["kTs"] = self.dram("s_kTs", [2, 64, T], BF16, kind=skind)
        d["kTw"] = self.dram("s_kTw", [2, 64, T], BF16, kind=skind)
        d["qTsb"] = self.dram("s_qTsb", [8, 64, T], BF16, kind=skind)
        d["kTsb"] = self.dram("s_kTsb", [8, 64, T], BF16, kind=skind)
        d["vS"] = self.dram("s_vS", [T, 128], BF16, kind=skind)
        d["vW"] = self.dram("s_vW", [T, 128], BF16, kind=skind)
        d["vSB"] = self.dram("s_vSB", [T, 512], BF16, kind=skind)
        d["gm"] = self.dram("s_gm", [T, 2048], BF16, kind=skind)
        self.d = d
        sb = {}
        sb["qTn"] = bufs(8, "qTn")
        for k in ("kTc", "vTc", "kTs", "kTw"):
            sb[k] = bufs(2, k)
        sb["qTsb"] = bufs(8, "qTsb")
        sb["kTsb"] = bufs(8, "kTsb")
        for k in ("vS", "vW", "vSB"):
            sb[k] = bufs(NT, k)
        sb["gm"] = [bufs(4, f"gm{i}_") for i in range(NT)]
        self.sbuf = sb
        if self.dbg:
            d["dbg_hT"] = self.dram("dbg_hT", [128, 8, T], BF16, kind="ExternalOutput")
            d["dbg_gn"] = self.dram("dbg_gn", [128, NT, 24], F32, kind="ExternalOutput")
            d["dbg_onsa"] = self.dram("dbg_onsa", [128, NT, 512], F32, kind="ExternalOutput")
            d["dbg_osb"] = self.dram("dbg_osb", [128, NT, 512], BF16, kind="ExternalOutput")
            d["dbg_selT"] = self.dram("dbg_selT", [32, 2, T], BF16, kind="ExternalOutput")

        with ExitStack() as es:
            self.es = es
            kb = KB(nc, es)
            self.kb = kb
            self.x = es.enter_context(self.S("x_sb", [128, NT, D], F32))
            self.xb = bufs(NT, "x")
            self.cstb = es.enter_context(self.S("cstb_sb", [128, CB_E], BF16))
            self.cstf = es.enter_context(self.S("cstf_sb", [128, CF_N], F32))
            self.cb_buf = Buf("cstb")
            self.cf_buf = Buf("cstf")
            kb.dma("pool", self.cstb[:], d["cstb"][:, 0:CB_E], (), [self.cb_buf])
            kb.dma("sp", self.cstf[:], d["cstf"][:, :], (), [self.cf_buf])
            self.epsc = es.enter_context(self.S("epsc", [128, 1], F32))
            kb.memset("pool", self.epsc[:], EPS, [Buf("eps")])
            for b in range(nb):
                for i in range(NT):
                    kb.dma("sp", self.x[:, i, :], d["x"][b, i * 128:(i + 1) * 128, :], (), [self.xb[i]])
                for l in self.layers:
                    self.layer(b, l)
                for i in range(NT):
                    kb.dma("sp", d["y"][b, i * 128:(i + 1) * 128, :], self.x[:, i, :], [self.xb[i]], ())
            kb.barrier()

    def cbs(self, off, n=128, rows=128):
        return self.cstb[0:rows, off:off + n]

    def stop(self, name):
        return self.stop_after == name

    def layer(self, b, l):
        kb = self.kb
        nc = self.nc
        done = False
        with ExitStack() as ls:
            self.gn = ls.enter_context(self.S("gn", [128, NT, 24], F32))
            self.gnb = bufs(NT, "gn")
            self.phase_proj(b, l)
            if self.dbg:
                kb.dma("sp", self.d["dbg_gn"][:, :, :], self.gn[:], self.gnb, ())
            if not self.stop("proj"):
                self.phase_attn(b, l, ls)
                if not self.stop("attn") and not self.stop("nsa"):
                    self.phase_merge(b, l)
                    done = not self.stop("merge")
            kb.barrier()
        if done:
            self.phase_ffn(b, l)

    def norm_tiles(self, ps, tiles, gfm_ap, hT, hT_bufs, psum_t, psum_bufs, tag, joff=0):
        kb = self.kb
        nc = self.nc
        n = len(tiles)
        if tag not in self.norm_tmp:
            self.norm_tmp[tag] = dict(
                ss=ps.enter_context(self.S(f"ss_{tag}", [128, NT], F32)),
                rstd=ps.enter_context(self.S(f"rstd_{tag}", [128, NT], F32)),
                junk=ps.enter_context(self.S(f"junk_{tag}", [128, D], BF16)),
                hb=ps.enter_context(self.S(f"hb_{tag}", [128, 2, D], BF16)),
                gfm=ps.enter_context(self.S(f"gfm_{tag}", [128, 8], F32)),
                bufs=(Buf(), Buf(), Buf(), Buf(), bufs(2)))
        tm = self.norm_tmp[tag]
        ss, rstd, junk, hb, gfm = tm["ss"], tm["rstd"], tm["junk"], tm["hb"], tm["gfm"]
        b_ss, b_rstd, b_junk, b_g, b_hb = tm["bufs"]
        kb.dma("sp", gfm[:], gfm_ap, (), [b_g])
        kb.memset("dve", ss[:], 0.0, [b_ss])
        for j, i in enumerate(tiles):
            kb.act(junk[:], self.x[:, i, :], AF.Square, [self.xb[i]], [b_junk, b_ss], accum_out=ss[:, j:j + 1])
        kb.ts("dve", rstd[:, 0:n], ss[:, 0:n], 1.0 / D, EPS, ALU.mult, ALU.add, [b_ss], [b_rstd])
        kb.act(rstd[:, 0:n], rstd[:, 0:n], AF.Sqrt, [b_rstd], [b_rstd])
        kb.op("dve", lambda: nc.vector.reciprocal(out=rstd[:, 0:n], in_=rstd[:, 0:n]), [b_rstd], [b_rstd])
        ident = self.cbs(CB_IDENT)
        for j, i in enumerate(tiles):
            s = j % 2
            kb.act(hb[:, s, :], self.x[:, i, :], AF.Copy, [self.xb[i], b_rstd], [b_hb[s]], scale=rstd[:, j:j + 1])
            pt = psum_t[s]
            for c in range(8):
                kb.tr(pt[:, c * 128:(c + 1) * 128], hb[:, s, c * 128:(c + 1) * 128], ident,
                      [b_hb[s], self.cb_buf], [psum_bufs[s]])
            kb.tt("dve", hT[:, :, (joff + j) * 128:(joff + j + 1) * 128], pt[:].rearrange("p (c t) -> p c t", c=8),
                  gfm[:].unsqueeze(2).broadcast_to([128, 8, 128]), ALU.mult,
                  [psum_bufs[s], b_g], [hT_bufs[joff + j]])

    def phase_proj(self, b, l):
        kb = self.kb
        nc = self.nc
        d = self.d
        sbf = self.sbuf
        with ExitStack() as ps:
            hT = ps.enter_context(self.S("hT", [128, 8, T], BF16))
            hTb = bufs(NT, "hT")
            cos2 = ps.enter_context(self.S("cos2", [128, T], F32))
            sinpm = ps.enter_context(self.S("sinpm", [128, T], F32))
            b_cos, b_sin = Buf("cos"), Buf("sin")
            pst = [ps.enter_context(self.P(f"pst{i}", [128, 1024], BF16)) for i in range(2)]
            pstb = bufs(2, "pst")
            psm = [ps.enter_context(self.P(f"psm{i}", [128, 512], F32)) for i in range(6)]
            psmb = bufs(6, "psm")
            with ExitStack() as ps2:
                posi = ps2.enter_context(self.S("posi", [128, T], I32))
                ang = ps2.enter_context(self.S("ang", [128, T], F32))
                tmpf = ps2.enter_context(self.S("tmpf", [128, T], F32))
                tmpi = ps2.enter_context(self.S("tmpi", [128, T], I32))
                b_posi, b_ang, b_tf, b_ti = Buf(), Buf(), Buf(), Buf()
                kb.dma("sp", posi[:], d["pos"][b:b + 1, :].partition_broadcast(128), (), [b_posi])
                kb.copy("dve", ang[:], posi[:], [b_posi], [b_ang])
                kb.ts("dve", ang[:], ang[:], self.cstf[:, CF_INVF:CF_INVF + 1], None, ALU.mult, None,
                      [b_ang, self.cf_buf], [b_ang])
                for which in ("sin", "cos"):
                    dst, db = (sinpm, b_sin) if which == "sin" else (cos2, b_cos)
                    if which == "cos":
                        kb.ts("dve", ang[:], ang[:], PI / 2, None, ALU.add, None, [b_ang], [b_ang])
                    kb.ts("dve", tmpi[:], ang[:], 1.0 / (2 * PI), None, ALU.mult, None, [b_ang], [b_ti])
                    kb.copy("dve", tmpf[:], tmpi[:], [b_ti], [b_tf])
                    kb.stt("dve", tmpf[:], tmpf[:], -2 * PI, ang[:], ALU.mult, ALU.add, [b_tf, b_ang], [b_tf])
                    kb.ts("dve", tmpf[:], tmpf[:], -3.1415925, 3.1415925, ALU.max, ALU.min, [b_tf], [b_tf])
                    kb.act(dst[:], tmpf[:], AF.Sin, [b_tf], [db])
                kb.ts("dve", sinpm[:], sinpm[:], self.cstf[:, CF_SGN:CF_SGN + 1], None, ALU.mult, None,
                      [b_sin, self.cf_buf], [b_sin])
                kb.barrier()
            ntag = f"p{self.uid}"
            wbuf = [ps.enter_context(self.S(f"wbuf{i}", [128, 8, 512], BF16)) for i in range(3)]
            wb = bufs(3, "wbuf")
            stg = [ps.enter_context(self.S(f"stg{i}", [128, T], BF16)) for i in range(3)]
            stgb = bufs(3, "stg")
            stt_ = [ps.enter_context(self.S(f"stt{i}", [128, 512], BF16)) for i in range(4)]
            sttb = bufs(4, "stt")
            rt = [ps.enter_context(self.S(f"rt{i}", [128, 512], F32)) for i in range(4)]
            rtb = bufs(4, "rt")
            def fm_dest(ch):
                if ch < 8:
                    p = ch // 2
                    return [(d["qTn"][2 * p], sbf["qTn"][2 * p]), (d["qTn"][2 * p + 1], sbf["qTn"][2 * p + 1])]
                if ch < 14:
                    nm = ("kTc", "kTs", "kTw")[(ch - 8) // 2]
                    return [(d[nm][0], sbf[nm][0]), (d[nm][1], sbf[nm][1])]
                if ch == 14:
                    return [(d["vTc"][0], sbf["vTc"][0]), (d["vTc"][1], sbf["vTc"][1])]
                if ch < 19:
                    p = ch - 15
                    return [(d["qTsb"][2 * p], sbf["qTsb"][2 * p]), (d["qTsb"][2 * p + 1], sbf["qTsb"][2 * p + 1])]
                p = ch - 19
                return [(d["kTsb"][2 * p], sbf["kTsb"][2 * p]), (d["kTsb"][2 * p + 1], sbf["kTsb"][2 * p + 1])]

            pi = 0
            si = 0
            ri = 0
            def issue_w(g):
                if g < 12:
                    kb.dma("pool", wbuf[g % 3][:], d["win"][l, g], (), [wb[g % 3]])
            issue_w(0)
            issue_w(1)
            for g in range(6):
                w = wbuf[g % 3]
                wbb = wb[g % 3]
                issue_w(g + 2)
                chunks = [g * 4 + c for c in range(4) if g * 4 + c < 23]
                ci = 0
                while ci < len(chunks):
                    ch = chunks[ci]
                    rope = ch < 14
                    st = stg[si % 3]
                    stb = stgb[si % 3]
                    si += 1
                    for tc in range(4):
                        if g == 0 and ci == 0:
                            self.norm_tiles(ps, [4 * tc + k for k in range(4)], d["gfm"][l, 0], hT, hTb, pst, pstb,
                                            ntag, joff=4 * tc)
                        rd = [hTb[4 * tc + k] for k in range(4)] + [wbb]
                        pa, pab = psm[pi % 6], psmb[pi % 6]
                        pi += 1
                        for k in range(8):
                            kb.mm(pa[:], w[:, k, (ch % 4) * 128:(ch % 4 + 1) * 128], hT[:, k, tc * 512:(tc + 1) * 512],
                                  k == 0, k == 7, rd, [pab])
                        if rope:
                            pb_, pbb = psm[pi % 6], psmb[pi % 6]
                            pi += 1
                            for k in range(8):
                                kb.mm(pb_[:], w[:, k, (ch % 4 + 1) * 128:(ch % 4 + 2) * 128],
                                      hT[:, k, tc * 512:(tc + 1) * 512], k == 0, k == 7, rd, [pbb])
                            r1, r1b = rt[ri % 4], rtb[ri % 4]
                            r2, r2b = rt[(ri + 1) % 4], rtb[(ri + 1) % 4]
                            ri += 2
                            kb.tt("dve", r1[:], pa[:], cos2[:, tc * 512:(tc + 1) * 512], ALU.mult, [pab, b_cos], [r1b])
                            kb.tt("dve", r2[:], pb_[:], sinpm[:, tc * 512:(tc + 1) * 512], ALU.mult, [pbb, b_sin], [r2b])
                            kb.tt("pool", st[:, tc * 512:(tc + 1) * 512], r1[:], r2[:], ALU.add, [r1b, r2b], [stb])
                        elif 15 <= ch < 19:
                            kb.act(st[:, tc * 512:(tc + 1) * 512], pa[:], AF.Copy, [pab], [stb], scale=SCALE)
                        else:
                            kb.copy("act", st[:, tc * 512:(tc + 1) * 512], pa[:], [pab], [stb])
                    for half, (dap, dbuf) in enumerate(fm_dest(ch)):
                        kb.dma("sp", dap, st[half * 64:(half + 1) * 64, :], [stb], [dbuf])
                    ci += 2 if rope else 1
            for g in range(6, 12):
                w = wbuf[g % 3]
                wbb = wb[g % 3]
                issue_w(g + 2)
                for i in range(NT):
                    pa, pab = psm[pi % 6], psmb[pi % 6]
                    pi += 1
                    for k in range(8):
                        kb.mm(pa[:], hT[:, k, i * 128:(i + 1) * 128], w[:, k, :], k == 0, k == 7, [hTb[i], wbb], [pab])
                    s4 = (g * NT + i) % 4
                    st, stb = stt_[s4], sttb[s4]
                    rows = slice(i * 128, (i + 1) * 128)
                    if g == 6:
                        kb.copy("dve", st[:, 0:256], pa[:, 0:256], [pab], [stb])
                        kb.act(self.gn[:, i, :], pa[:, 256:280], AF.Sigmoid, [pab], [self.gnb[i]])
                        kb.dma("sp", d["vS"][rows, :], st[:, 0:128], [stb], [sbf["vS"][i]])
                        kb.dma("sp", d["vW"][rows, :], st[:, 128:256], [stb], [sbf["vW"][i]])
                    elif g == 7:
                        kb.copy("dve", st[:], pa[:], [pab], [stb])
                        kb.dma("sp", d["vSB"][rows, :], st[:], [stb], [sbf["vSB"][i]])
                    else:
                        kb.act(st[:], pa[:], AF.Sigmoid, [pab], [stb])
                        kb.dma("sp", d["gm"][rows, (g - 8) * 512:(g - 7) * 512], st[:], [stb], [sbf["gm"][i][g - 8]])
            kb.barrier()

    def phase_attn(self, b, l, ls):
        kb = self.kb
        nc = self.nc
        d = self.d
        sbf = self.sbuf
        self.o_acc = ls.enter_context(self.S("o_acc", [128, NT, 512], F32))
        self.o_accb = [bufs(8, f"oacc{i}_") for i in range(NT)]
        self.o_sb = ls.enter_context(self.S("o_sb", [128, NT, 512], BF16))
        self.o_sbb = [bufs(8, f"osb{i}_") for i in range(NT)]
        self.nsa(b, l)
        if self.dbg:
            kb.dma("sp", d["dbg_onsa"][:, :, :], self.o_acc[:], [x for r in self.o_accb for x in r], ())
        if self.stop("nsa"):
            return
        mw = {}
        for nm, shp in (("wpn", [128, 4, 1024]), ("wps", [128, 4, 1024]), ("wout", [128, 8, 1024])):
            t_ = ls.enter_context(self.S("mw_" + nm, shp, BF16))
            b_ = Buf(nm)
            kb.dma("pool", t_[:], d[nm][l], (), [b_])
            mw[nm] = (t_, b_)
        self.mw = mw
        self.sbattn(b, l)
        if self.dbg:
            kb.dma("sp", d["dbg_osb"][:, :, :], self.o_sb[:], [x for r in self.o_sbb for x in r], ())

    def nsa(self, b, l):
        kb = self.kb
        nc = self.nc
        d = self.d
        sbf = self.sbuf
        cstb, cstf = self.cstb, self.cstf
        CB, CFb = self.cb_buf, self.cf_buf
        with ExitStack() as ps:
            def sbt(name, shape, dt):
                return ps.enter_context(self.S(name, shape, dt))
            qT = sbt("qT", [96, NT, 4, 128], BF16); qTb = bufs(4, "qT")
            kbuf = sbt("kbuf", [64, 4, T], BF16); kbb = bufs(4, "kbuf")
            ksel = sbt("ksel", [96, T], BF16); kselb = Buf("ksel"); kselEb = Buf("kselE")
            cmpm = sbt("cmpm", [128, T], BF16); cmpmb = Buf("cmpm")
            kb.dma("pool", cmpm[:], d["cstb"][:, CB_CMPM:CB_CMPM + T], (), [cmpmb])
            kb.dma("pool", ksel[64:96, :], d["cstb"][0:32, CB_E:CB_E + T], (), [kselEb])
            Vs = sbt("Vs", [128, NT, 65], BF16); Vsb = Buf("Vs")
            Vw = sbt("Vw", [128, NT, 65], BF16); Vwb = Buf("Vw")
            selTb = bufs(NT, "selT")
            w1 = [sbt("w1k", [64, 2048], BF16), sbt("w1v", [64, 2048], BF16)]; w1b = bufs(2, "w1")
            peT = sbt("peT", [64, 2, 32], BF16); peTb = Buf()
            w2 = sbt("w2", [64, 2, 64], BF16); w2b = Buf()
            b1 = sbt("b1", [64, 2], F32); b1b = Buf()
            biasb = sbt("biasb", [64, 2], F32); biasbb = Buf()
            hc = sbt("hc", [64, 128], BF16); hcb = Buf()
            kcT = sbt("kcT", [64, 128], BF16); kcTb = Buf()
            vcx = sbt("vcx", [128, 97], F32); vcxb = Buf()
            em = [sbt(f"em{i}", [128, 512], F32) for i in range(2)]; emb = bufs(2, "em")
            rden = sbt("rden", [128, 8], F32); rdenb = Buf()
            scg = sbt("scg", [128, 8], F32); scgb = Buf()
            sc = [sbt(f"sc{i}", [128, 32], F32) for i in range(2)]; scb = bufs(2, "sc")
            big = sbt("big", [128, 1024], F32); bigb = Buf()
            rank = sbt("rank", [128, 32], F32); rankb = Buf()
            selb = sbt("selb", [128, 32], BF16); selbb = Buf()
            Pt = [sbt(f"Pt{i}", [128, 512], BF16) for i in range(7)]; Ptb = bufs(7, "Pt")
            rden2 = [sbt(f"rden2_{i}", [128, 4], F32) for i in range(6)]; rden2b = bufs(6, "rden2")
            pa = [ps.enter_context(self.P(f"pa{i}", [128, 512], F32)) for i in range(8)]
            pab = bufs(8, "pa")
            pstv = pa[3][0:32, 400:464].bitcast(BF16)
            pstb = pab[3]

            kb.dma("pool", w1[0][:], d["wck1"][l], (), [w1b[0]])
            kb.dma("pool", w1[1][:], d["wcv1"][l], (), [w1b[1]])
            kb.dma("pool", peT[:], d["peT"][l].rearrange("k d l -> d k l"), (), [peTb])
            kb.dma("pool", w2[:], d["w2"][l].rearrange("k a b -> a k b"), (), [w2b])
            kb.dma("sp", b1[:], d["b1"][l], (), [b1b])
            kb.memset("pool", Vs[:, :, 64:65], 1.0, [Vsb])
            kb.memset("pool", Vw[:, :, 64:65], 1.0, [Vwb])
            kb.memset("dve", vcx[:], 0.0, [vcxb])
            kb.memset("dve", vcx[:, 64:65], 1.0, [vcxb])
            kb.copy("dve", vcx[:, 65:97], cstf[:, CF_OV:CF_OV + 32], [CFb], [vcxb])
            ident = self.cbs(CB_IDENT)
            mcausal = self.cbs(CB_MCAUSAL)
            mwinlow = self.cbs(CB_MWINLOW)

            for h in range(2):
                for g in range(4):
                    kb.dma("sp", qT[0:64, :, g, :], d["qTn"][4 * h + g].rearrange("d (i t) -> d i t", t=128),
                           [sbf["qTn"][4 * h + g]], [qTb[g]])
                for slot, nm in enumerate(("kTc", "vTc", "kTs", "kTw")):
                    if slot == 2:
                        kb.dma("sp", ksel[0:64, :], d[nm][h], [sbf[nm][h]], [kselb])
                    else:
                        kb.dma("sp", kbuf[:, slot, :], d[nm][h], [sbf[nm][h]], [kbb[slot]])
                kb.dma("sp", Vs[:, :, 0:64], d["vS"].rearrange("(i p) c -> p i c", p=128)[:, :, h * 64:(h + 1) * 64],
                       sbf["vS"], [Vsb])
                kb.dma("sp", Vw[:, :, 0:64], d["vW"].rearrange("(i p) c -> p i c", p=128)[:, :, h * 64:(h + 1) * 64],
                       sbf["vW"], [Vwb])
                for kv in range(2):
                    A = pa[0][0:64, 0:NCMP]
                    for li in range(32):
                        kb.mm(A, w1[kv][:, li * 64:(li + 1) * 64], kbuf[:, kv, li:li + 2017:16], li == 0, li == 31,
                              [w1b[kv], kbb[kv]], [pab[0]])
                    Bv = pa[1][0:64, 0:1]
                    for li in range(32):
                        kb.mm(Bv, w1[kv][:, li * 64:(li + 1) * 64], peT[:, kv, li:li + 1], li == 0, li == 31,
                              [w1b[kv], peTb], [pab[1]])
                    kb.tt("dve", biasb[:, kv:kv + 1], Bv, b1[:, kv:kv + 1], ALU.add, [pab[1], b1b], [biasbb])
                    kb.act(hc[:, 0:NCMP], A, AF.Gelu_apprx_tanh, [pab[0], biasbb], [hcb], bias=biasb[:, kv:kv + 1])
                    if kv == 0:
                        o2 = pa[1][0:64, 128:128 + NCMP]
                        kb.mm(o2, w2[:, 0, :], hc[:, 0:NCMP], True, True, [w2b, hcb], [pab[1]])
                        kb.copy("dve", kcT[:, 0:NCMP], o2, [pab[1]], [kcTb])
                    else:
                        o2 = pa[1][0:NCMP, 256:320]
                        kb.mm(o2, hc[:, 0:NCMP], w2[:, 1, :], True, True, [w2b, hcb], [pab[1]])
                        kb.copy("dve", vcx[0:NCMP, 0:64], o2, [pab[1]], [vcxb])
                state = {"s": 0, "p": 0, "r": 0}
                SB_ = [0, 1, 2, 6, 7]

                def next_S():
                    j = SB_[state["s"] % 5]
                    state["s"] += 1
                    return pa[j], pab[j]

                def cmp_tile(i):
                    tsl = slice(i * 128, (i + 1) * 128)
                    S, Sb = next_S()
                    S3 = S[0:NCMP, :].rearrange("p (g t) -> p g t", g=4)
                    kb.mm(S3, kcT[:, 0:NCMP], qT[0:64, i, :, :], True, True, [kcTb] + qTb, [Sb])
                    yield
                    e_, eb_ = em[i % 2], emb[i % 2]
                    kb.act(e_[0:NCMP, :], S[0:NCMP, :], AF.Exp, [Sb], [eb_], scale=SCALE)
                    e3 = e_[0:NCMP, :].rearrange("p (g t) -> p g t", g=4)
                    kb.tt("dve", e3, e3, cmpm[0:NCMP, i * 128:(i + 1) * 128].unsqueeze(1).broadcast_to([NCMP, 4, 128]),
                          ALU.mult, [eb_, cmpmb], [eb_])
                    yield
                    PV, PVb = pa[3], pab[3]
                    PV3 = PV[:, 0:388].rearrange("p (g c) -> p g c", g=4)
                    for g in range(4):
                        kb.mm(PV3[:, g, :], e_[0:NCMP, g * 128:(g + 1) * 128], vcx[0:NCMP, :], g == 0, g == 3,
                              [eb_, vcxb], [PVb])
                    yield
                    r4 = rden[:, 0:4]
                    kb.ts("dve", r4, PV3[:, :, 64], 1e-30, None, ALU.max, None, [PVb], [rdenb])
                    yield
                    kb.op("dve", lambda: nc.vector.reciprocal(out=r4, in_=r4), [rdenb], [rdenb])
                    yield
                    prev = cstf[:, CF_BV + i * 32:CF_BV + (i + 1) * 32]
                    prevb = CFb
                    for g in range(4):
                        hh = 4 * h + g
                        s_, sb_ = sc[g % 2], scb[g % 2]
                        kb.stt("dve", s_[:], PV3[:, g, 65:97], rden[:, g:g + 1], prev, ALU.mult, ALU.add,
                               [PVb, rdenb, prevb], [sb_])
                        prev, prevb = s_[:], sb_
                        kb.stt("dve", self.o_acc[:, i, hh * 64:(hh + 1) * 64], PV3[:, g, 0:64], rden[:, g:g + 1],
                               self.gn[:, i, 3 * hh:3 * hh + 1].broadcast_to([128, 64]), ALU.mult, ALU.mult,
                               [PVb, rdenb, self.gnb[i]], [self.o_accb[i][hh]])
                    yield
                    big3 = big[:].rearrange("p (a c) -> p a c", a=32)
                    kb.tt("dve", big3, prev.unsqueeze(1).broadcast_to([128, 32, 32]),
                          prev.unsqueeze(2).broadcast_to([128, 32, 32]), ALU.is_gt, [prevb], [bigb])
                    yield
                    kb.op("dve", lambda: nc.vector.reduce_sum(out=rank[:], in_=big3, axis=AX.X), [bigb], [rankb])
                    yield
                    kb.ts("dve", selb[:], rank[:], 16.0, NEG, ALU.is_ge, ALU.mult, [rankb], [selbb])
                    yield
                    kb.tr(pstv, selb[:], ident, [selbb, CB], [pstb])
                    kb.copy("act", qT[64:96, i, :, :], pstv.unsqueeze(1).broadcast_to([32, 4, 128]),
                            [pstb], [selTb[i]])

                def branch_tile(i, ks, kslot, V, Vb, O, Ob, first, last, masks, gate, use_sel):
                    tsl = slice(i * 128, (i + 1) * 128)
                    S, Sb = next_S()
                    S3 = S[:].rearrange("p (g t) -> p g t", g=4)
                    if use_sel:
                        kb.mm(S3, ksel[:, ks * 128:(ks + 1) * 128], qT[:, i, :, :], True, True,
                              [kselb, kselEb, selTb[i]] + qTb, [Sb])
                    else:
                        kb.mm(S3, kbuf[:, kslot, ks * 128:(ks + 1) * 128], qT[0:64, i, :, :], True, True,
                              [kbb[kslot]] + qTb, [Sb])
                    yield
                    P, Pb = Pt[state["p"] % 7], Ptb[state["p"] % 7]
                    state["p"] += 1
                    kb.act(P[:], S[:], AF.Exp, [Sb], [Pb], scale=SCALE)
                    P3 = P[:].rearrange("p (g t) -> p g t", g=4)
                    for m in masks:
                        kb.tt("pool", P3, P3, m.unsqueeze(1).broadcast_to([128, 4, 128]), ALU.mult, [Pb, CB], [Pb])
                    yield
                    O3 = O[:, 0:260].rearrange("p (g c) -> p g c", g=4)
                    for g in range(4):
                        kb.mm(O3[:, g, :], P[:, g * 128:(g + 1) * 128], V[:, ks, :], first and g == 0, last and g == 3,
                              [Pb, Vb], [Ob])
                    if last:
                        yield
                        r_, rb_ = rden2[state["r"] % 6], rden2b[state["r"] % 6]
                        state["r"] += 1
                        kb.op("dve", lambda: nc.vector.reciprocal(out=r_[:], in_=O3[:, :, 64]), [Ob], [rb_])
                        yield
                        kb.tt("dve", r_[:], r_[:], self.gn[:, i, 12 * h + gate:12 * h + 12:3], ALU.mult,
                              [rb_, self.gnb[i]], [rb_])
                        yield
                        for g in range(4):
                            hh = 4 * h + g
                            oa = self.o_acc[:, i, hh * 64:(hh + 1) * 64]
                            kb.stt("dve", oa, O3[:, g, 0:64], r_[:, g:g + 1], oa, ALU.mult, ALU.add,
                                   [Ob, rb_, self.o_accb[i][hh]], [self.o_accb[i][hh]])

                def group_tiles(i):
                    Os, Osb = pa[4], pab[4]
                    Ow, Owb = pa[5], pab[5]
                    for ks in range(i + 1):
                        yield branch_tile(i, ks, 2, Vs, Vsb, Os, Osb, ks == 0, ks == i,
                                          [mcausal] if ks == i else [], 1, True)
                    lo = max(0, i - 4)
                    for ks in range(lo, i + 1):
                        masks = []
                        if ks == i:
                            masks.append(mcausal)
                        if ks == i - 4:
                            masks.append(mwinlow)
                        yield branch_tile(i, ks, 3, Vw, Vwb, Ow, Owb, ks == lo, ks == i, masks, 2, False)

                DEPTH_ = 5
                active = []

                def step_all():
                    for g_ in list(active):
                        try:
                            next(g_)
                        except StopIteration:
                            active.remove(g_)

                def start(g_):
                    while len(active) >= DEPTH_:
                        step_all()
                    active.append(g_)

                cmp_g = {i: cmp_tile(i) for i in range(NT)}
                start(cmp_g[0])
                for i in range(NT):
                    while any(g_ is cmp_g[i] for g_ in active):
                        step_all()
                    if i + 1 < NT:
                        start(cmp_g[i + 1])
                    for g_ in group_tiles(i):
                        start(g_)
                while active:
                    step_all()
            kb.barrier()

    def sbattn(self, b, l):
        kb = self.kb
        nc = self.nc
        d = self.d
        sbf = self.sbuf
        cstb = self.cstb
        CB = self.cb_buf
        NCH = 4
        with ExitStack() as ps:
            def sbt(name, shape, dt):
                return ps.enter_context(self.S(name, shape, dt))
            qs = [sbt(f"qs{i}", [128, T], BF16) for i in range(2)]; qsb = [bufs(2, f"qs{i}_") for i in range(2)]
            ksb_ = [sbt(f"ks{i}", [128, T], BF16) for i in range(2)]; ksbb = [bufs(2, f"ks{i}_") for i in range(2)]
            vsb = [sbt(f"vsb{i}", [128, NT, 128], BF16) for i in range(2)]; vsbb = bufs(2, "vsb")
            et = [sbt(f"e{i}", [128, 512], F32) for i in range(NCH)]; etb = bufs(NCH, "e")
            Lt = [sbt(f"L{i}", [128, 512], BF16) for i in range(NCH)]; Ltb = bufs(NCH, "L")
            Ls = [sbt(f"Ls{i}", [128, 512], BF16) for i in range(NCH)]; Lsb = bufs(NCH, "Ls")
            wt = [sbt(f"w{i}", [128, 512], BF16) for i in range(NCH)]; wtb = bufs(NCH, "w")
            Ob_ = [ps.enter_context(self.P(f"sbO{i}", [128, 512], F32)) for i in range(NCH)]; Obb = bufs(NCH, "sbO")
            zb_ = [ps.enter_context(self.P(f"sbz{i}", [128, 512], F32)) for i in range(NCH)]
            zbb = bufs(NCH, "sbz")
            tri = self.cbs(CB_TRI)
            onesn = self.cbs(CB_ONES)
            mstrict = self.cbs(CB_MSTRICT)
            state = {"z": 0}

            def load_pair(hp):
                s = hp % 2
                for a in range(2):
                    kb.dma("sp", qs[s][a * 64:(a + 1) * 64, :], d["qTsb"][2 * hp + a], [sbf["qTsb"][2 * hp + a]], [qsb[s][a]])
                    kb.dma("sp", ksb_[s][a * 64:(a + 1) * 64, :], d["kTsb"][2 * hp + a], [sbf["kTsb"][2 * hp + a]], [ksbb[s][a]])
                kb.dma("sp", vsb[s][:], d["vSB"].rearrange("(i p) c -> p i c", p=128)[:, :, hp * 128:(hp + 1) * 128],
                       sbf["vSB"], [vsbb[s]])

            def chain(slot, hp, a, tq):
                s = hp % 2
                base = a * 64
                head = 2 * hp + a
                O, Ob = Ob_[slot], Obb[slot]
                O3 = O[:, 0:256].rearrange("p (c e) -> p c e", c=4)
                Lsum, Lsumb = Ls[slot], Lsb[slot]
                kb.memset("pool", Lsum[:], 0.0, [Lsumb])
                nks = 4 * tq + 4
                firstO = True
                for ks in range(nks - 1, -1, -1):
                    c0 = max(0, ks - 4 * tq) * 128
                    N = 512 - c0
                    diag = ks >= 4 * tq
                    first = ks == nks - 1
                    z, zb = zb_[slot], zbb[slot]
                    kb.mm(z[:, 0:N], ksb_[s][base:base + 64, ks * 128:(ks + 1) * 128],
                          qs[s][base:base + 64, tq * 512 + c0:(tq + 1) * 512], True, False,
                          [ksbb[s][a], qsb[s][a]], [zb])
                    yield
                    e, eb = et[slot], etb[slot]
                    kb.act(e[:, 0:N], z[:, 0:N], AF.Exp, [zb], [eb])
                    if diag:
                        kb.tt("dve", e[:, 0:128], e[:, 0:128], mstrict, ALU.mult, [eb, CB], [eb])
                    yield
                    Lx, Lxb = Lt[slot], Ltb[slot]
                    kb.act(Lx[:, 0:N], e[:, 0:N], AF.Ln, [eb], [Lxb], bias=1.0)
                    yield
                    kb.mm(z[:, 0:N], tri, Lx[:, 0:N], False, first, [CB, Lxb], [zb])
                    if not first:
                        kb.mm(z[:, 0:N], onesn, Lsum[:, c0:512], False, True, [CB, Lsumb], [zb])
                    if ks > 0:
                        kb.tt("dve", Lsum[:, c0:512], Lsum[:, c0:512], Lx[:, 0:N], ALU.add, [Lsumb, Lxb], [Lsumb])
                    yield
                    w, wb = wt[slot], wtb[slot]
                    kb.act(w[:, 0:N], z[:, 0:N], AF.Exp, [zb], [wb])
                    if diag:
                        kb.tt("dve", w[:, 0:128], w[:, 0:128], mstrict, ALU.mult, [wb, CB], [wb])
                    yield
                    for c in range(c0 // 128, 4):
                        kb.mm(O3[:, c, :], w[:, c * 128 - c0:(c + 1) * 128 - c0], vsb[s][:, ks, a * 64:(a + 1) * 64],
                              firstO, ks == 0 and c == 3, [wb, vsbb[s]], [Ob])
                        firstO = False
                    yield
                kb.copy("dve", self.o_sb[:, 4 * tq:4 * tq + 4, head * 64:(head + 1) * 64], O3,
                        [Ob], [self.o_sbb[4 * tq + c][head] for c in range(4)])

            def all_chains():
                for hp in range(4):
                    for tq in (3, 2, 1, 0):
                        for a in range(2):
                            yield (hp, a, tq)

            load_pair(0)
            load_pair(1)
            remaining = {hp: 8 for hp in range(4)}
            free = list(range(NCH))
            active = []
            gen = all_chains()
            done = False
            while True:
                while not done and free:
                    try:
                        hp, a, tq = next(gen)
                    except StopIteration:
                        done = True
                        break
                    sl = free.pop(0)
                    g_new = chain(sl, hp, a, tq)
                    for _ in range((0, 2, 3, 5)[sl]):
                        next(g_new)
                    active.append((sl, hp, g_new))
                if not active:
                    break
                for item in list(active):
                    sl, hp, g = item
                    try:
                        next(g)
                    except StopIteration:
                        active.remove(item)
                        free.append(sl)
                        remaining[hp] -= 1
                        if remaining[hp] == 0 and hp + 2 < 4:
                            load_pair(hp + 2)
            kb.barrier()

    def post_norm_add(self, i, m, mb, gbc, gbcb, tmp, add_eng="pool"):
        kb = self.kb
        nc = self.nc
        ssq, ssqb, rs, rsb, junk, junkb, tn, tnb = tmp
        kb.memset("dve", ssq[:], 0.0, [ssqb])
        for hf in range(2):
            kb.act(junk[:], m[hf][:], AF.Square, [mb[hf]], [junkb, ssqb], accum_out=ssq[:, hf:hf + 1],
                   scale=float(D) ** -0.5)
        kb.tt("dve", rs[:], ssq[:, 0:1], ssq[:, 1:2], ALU.add, [ssqb], [rsb])
        kb.act(rs[:], rs[:], AF.Sqrt, [rsb], [rsb], bias=self.epsc[:, 0:1])
        kb.op("dve", lambda: nc.vector.reciprocal(out=rs[:], in_=rs[:]), [rsb], [rsb])
        for hf in range(2):
            kb.stt("dve", tn[:, hf * 512:(hf + 1) * 512], m[hf][:], rs[:, 0:1], gbc[:, hf * 512:(hf + 1) * 512],
                   ALU.mult, ALU.mult, [mb[hf], rsb, gbcb], [tnb])
        kb.tt(add_eng, self.x[:, i, :], self.x[:, i, :], tn[:], ALU.add, [self.xb[i], tnb], [self.xb[i]])

    def phase_merge(self, b, l):
        kb = self.kb
        nc = self.nc
        d = self.d
        sbf = self.sbuf
        CB = self.cb_buf
        ident = self.cbs(CB_IDENT)
        with ExitStack() as ps:
            def sbt(name, shape, dt):
                return ps.enter_context(self.S(name, shape, dt))
            wpn, wpnb = self.mw["wpn"]
            wps, wpsb = self.mw["wps"]
            wout, woutb = self.mw["wout"]
            gbc = sbt("m_gbc", [128, 1024], F32); gbcb = Buf()
            kb.dma("sp", gbc[:], d["gbc"][l, 0:1, :].partition_broadcast(128), (), [gbcb])
            gmt = [sbt(f"m_gmt{j}", [128, 2048], BF16) for j in range(2)]; gmtb = bufs(2)
            onb = [sbt(f"m_onb{j}", [128, 512], BF16) for j in range(2)]; onbb = bufs(2)
            oT = [sbt(f"m_oT{j}", [128, 8, 128], BF16) for j in range(2)]; oTb = bufs(2)
            t1 = [sbt(f"m_t1{j}", [128, 1024], F32) for j in range(2)]; t1b = bufs(2)
            t2 = [sbt(f"m_t2{j}", [128, 1024], F32) for j in range(2)]; t2b = bufs(2)
            yb = [sbt(f"m_yb{j}", [128, 1024], BF16) for j in range(2)]; ybb = bufs(2)
            yT = [sbt(f"m_yT{j}", [128, 8, 128], BF16) for j in range(2)]; yTb = bufs(2)
            tmp = (sbt("m_ssq", [128, 2], F32), Buf(), sbt("m_rs", [128, 1], F32), Buf(),
                   sbt("m_junk", [128, 512], BF16), Buf(), sbt("m_tn", [128, 1024], F32), Buf())
            pT = [ps.enter_context(self.P(f"m_pT{j}", [128, 1024], BF16)) for j in range(2)]; pTb = bufs(2)
            yn = [ps.enter_context(self.P(f"m_yn{j}", [128, 512], F32)) for j in range(2)]; ynb = bufs(2)
            ys = [ps.enter_context(self.P(f"m_ys{j}", [128, 512], F32)) for j in range(2)]; ysb = bufs(2)
            mm_ = [ps.enter_context(self.P(f"m_m{j}", [128, 512], F32)) for j in range(2)]; mmb = bufs(2)
            def merge_tile(i):
                j = i % 2
                pTj, pTjb = pT[j], pTb[j]
                kb.dma("sp", gmt[j][:], d["gm"][i * 128:(i + 1) * 128, :], sbf["gm"][i], [gmtb[j]])
                kb.copy("act", onb[j][:], self.o_acc[:, i, :], self.o_accb[i], [onbb[j]])
                for c in range(4):
                    kb.tr(pTj[:, c * 128:(c + 1) * 128], onb[j][:, c * 128:(c + 1) * 128], ident, [onbb[j], CB], [pTjb])
                for c in range(4):
                    kb.tr(pTj[:, (4 + c) * 128:(5 + c) * 128], self.o_sb[:, i, c * 128:(c + 1) * 128], ident,
                          self.o_sbb[i] + [CB], [pTjb])
                yield
                kb.copy("dve", oT[j][:].rearrange("p c t -> p (c t)"), pTj[:], [pTjb], [oTb[j]])
                yield
                for hf in range(2):
                    for k in range(4):
                        kb.mm(yn[hf][:], oT[j][:, k, :], wpn[:, k, hf * 512:(hf + 1) * 512], k == 0, k == 3,
                              [oTb[j], wpnb], [ynb[hf]])
                    for k in range(4):
                        kb.mm(ys[hf][:], oT[j][:, 4 + k, :], wps[:, k, hf * 512:(hf + 1) * 512], k == 0, k == 3,
                              [oTb[j], wpsb], [ysb[hf]])
                    sl = slice(hf * 512, (hf + 1) * 512)
                    kb.tt("dve", t1[j][:, sl], yn[hf][:], gmt[j][:, hf * 512:(hf + 1) * 512], ALU.mult, [ynb[hf], gmtb[j]], [t1b[j]])
                    kb.tt("dve", t2[j][:, sl], ys[hf][:], gmt[j][:, 1024 + hf * 512:1024 + (hf + 1) * 512], ALU.mult,
                          [ysb[hf], gmtb[j]], [t2b[j]])
                    yield
                kb.tt("pool", yb[j][:], t1[j][:], t2[j][:], ALU.add, [t1b[j], t2b[j]], [ybb[j]])
                yield
                for c in range(8):
                    kb.tr(pTj[:, c * 128:(c + 1) * 128], yb[j][:, c * 128:(c + 1) * 128], ident, [ybb[j], CB], [pTjb])
                yield
                kb.copy("act", yT[j][:].rearrange("p c t -> p (c t)"), pTj[:], [pTjb], [yTb[j]])
                yield
                for hf in range(2):
                    for k in range(8):
                        kb.mm(mm_[hf][:], yT[j][:, k, :], wout[:, k, hf * 512:(hf + 1) * 512], k == 0, k == 7,
                              [yTb[j], woutb], [mmb[hf]])
                self.post_norm_add(i, mm_, mmb, gbc, gbcb, tmp)

            run_pipelined((merge_tile(i) for i in range(NT)), 2)
            kb.barrier()

    def phase_ffn(self, b, l):
        kb = self.kb
        nc = self.nc
        d = self.d
        CB = self.cb_buf
        with ExitStack() as ps:
            def sbt(name, shape, dt):
                return ps.enter_context(self.S(name, shape, dt))
            wd = sbt("f_wd", [128, NCP, 1024], BF16); wdb = bufs(2)
            kb.dma("pool", wd[:, 0:11, :], d["wdn"][l][:, 0:11, :], (), [wdb[0]])
            kb.dma("pool", wd[:, 11:22, :], d["wdn"][l][:, 11:22, :], (), [wdb[1]])
            cw = sbt("f_cw", [128, 44, 3], F32); cwb = Buf()
            cbs_ = sbt("f_cb", [128, 44], F32); cbb = Buf()
            gbc = sbt("f_gbc", [128, 1024], F32); gbcb = Buf()
            kb.dma("sp", cw[:], d["cw"][l], (), [cwb])
            kb.dma("sp", cbs_[:], d["cb"][l], (), [cbb])
            kb.dma("sp", gbc[:], d["gbc"][l, 1:2, :].partition_broadcast(128), (), [gbcb])
            Xe = [sbt(f"f_Xe{j}", [128, 44, 4], F32) for j in range(2)]; Xeb = bufs(2, "Xe")
            kb.memset("pool", Xe[0][:], 0.0, [Xeb[0]])
            et = [sbt(f"f_et{j}", [128, 44, 2], F32) for j in range(3)]; etb = bufs(3, "et")
            hT2 = sbt("f_hT2", [128, 8, 512], BF16); hT2b = bufs(4)
            aT = sbt("f_aT", [128, NCP, 512], BF16); aTb = bufs(NCP)
            wub = [sbt(f"f_wub{j}", [128, 8, 256], BF16) for j in range(3)]; wubb = bufs(3)
            cv = [[sbt(f"f_c{p}{j}", [128, 512], F32) for j in range(3)] for p in range(2)]
            cvb = [bufs(3), bufs(3)]
            gl = [sbt(f"f_gl{j}", [128, 512], F32) for j in range(2)]; glb = bufs(2)
            tmp = (sbt("f_ssq", [128, 2], F32), Buf(), sbt("f_rs", [128, 1], F32), Buf(),
                   sbt("f_junk", [128, 512], BF16), Buf(), sbt("f_tn", [128, 1024], F32), Buf())
            pst = [ps.enter_context(self.P(f"f_pst{j}", [128, 1024], BF16)) for j in range(2)]; pstb = bufs(2)
            pu = [ps.enter_context(self.P(f"f_pu{j}", [128, 512], F32)) for j in range(4)]; pub = bufs(4)
            pf = [ps.enter_context(self.P(f"f_pf{j}", [128, 512], F32)) for j in range(2)]; pfb = bufs(2)
            tagn = f"f{self.uid}"
            pi = 0
            nld = 4 * NCP

            def issue_wu(j):
                if j < nld:
                    kb.dma("pool", wub[j % 3][:], d["wup"][l, j % NCP], (), [wubb[j % 3]])
            issue_wu(0)
            issue_wu(1)
            self.norm_tiles(ps, [0, 1, 2, 3], d["gfm"][l, 1], hT2, hT2b, pst, pstb, tagn)
            for qt in range(4):
                for cp in range(NCP):
                    jj = qt * NCP + cp
                    issue_wu(jj + 2)
                    w, wb = wub[jj % 3], wubb[jj % 3]
                    parts = []
                    for part in range(2):
                        ci = part * NCP + cp
                        p_, pb_ = pu[pi % 4], pub[pi % 4]
                        pi += 1
                        for k in range(8):
                            kb.mm(p_[:], w[:, k, part * 128:(part + 1) * 128], hT2[:, k, :], k == 0, k == 7,
                                  hT2b + [wb], [pb_])
                        Xc, Xn = Xe[qt % 2], Xe[(qt + 1) % 2]
                        Xcb, Xnb = Xeb[qt % 2], Xeb[(qt + 1) % 2]
                        c, cb_ = cv[part][cp % 3], cvb[part][cp % 3]
                        kb.act(c[:], p_[:], AF.Identity, [pb_, cwb, cbb], [cb_], scale=cw[:, ci, 2:3],
                               bias=cbs_[:, ci:ci + 1])
                        kb.copy("act", Xc[:, ci, 2:4], p_[:, 0:2], [pb_], [Xcb])
                        kb.copy("act", Xn[:, ci, 0:2], p_[:, 510:512], [pb_], [Xnb])
                        parts.append((ci, p_, pb_, c, cb_))
                    for (ci, p_, pb_, c, cb_) in parts:
                        kb.stt("dve", c[:, 2:512], p_[:, 1:511], cw[:, ci, 1:2], c[:, 2:512], ALU.mult, ALU.add,
                               [pb_, cwb, cb_], [cb_])
                    for (ci, p_, pb_, c, cb_) in parts:
                        kb.stt("dve", c[:, 2:512], p_[:, 0:510], cw[:, ci, 0:1], c[:, 2:512], ALU.mult, ALU.add,
                               [pb_, cwb, cb_], [cb_])

                    def finish(cq):
                        g_, gb_ = gl[cq % 2], glb[cq % 2]
                        kb.act(g_[:, 2:512], cv[0][cq % 3][:, 2:512], AF.Gelu_apprx_tanh, [cvb[0][cq % 3]], [gb_])
                        kb.tt("dve", aT[:, cq, 2:512], g_[:, 2:512], cv[1][cq % 3][:, 2:512], ALU.mult,
                              [gb_, cvb[1][cq % 3]], [aTb[cq]])
                    if cp > 0:
                        finish(cp - 1)
                    if cp == NCP - 1:
                        finish(cp)
                Xc, Xcb = Xe[qt % 2], Xeb[qt % 2]
                kb.tt("dve", et[0][:], Xc[:, :, 2:4], cw[:, :, 2:3].broadcast_to([128, 44, 2]), ALU.mult, [Xcb, cwb], [etb[0]])
                kb.tt("dve", et[1][:], Xc[:, :, 1:3], cw[:, :, 1:2].broadcast_to([128, 44, 2]), ALU.mult, [Xcb, cwb], [etb[1]])
                kb.tt("dve", et[2][:], Xc[:, :, 0:2], cw[:, :, 0:1].broadcast_to([128, 44, 2]), ALU.mult, [Xcb, cwb], [etb[2]])
                kb.tt("dve", et[0][:], et[0][:], et[1][:], ALU.add, [etb[0], etb[1]], [etb[0]])
                kb.tt("dve", et[2][:], et[2][:], cbs_[:].unsqueeze(2).broadcast_to([128, 44, 2]), ALU.add, [etb[2], cbb], [etb[2]])
                kb.tt("dve", et[0][:], et[0][:], et[2][:], ALU.add, [etb[0], etb[2]], [etb[0]])
                kb.act(et[1][:, 0:NCP, :], et[0][:, 0:NCP, :], AF.Gelu_apprx_tanh, [etb[0]], [etb[1]])
                kb.tt("dve", aT[:, :, 0:2], et[1][:, 0:NCP, :], et[0][:, NCP:2 * NCP, :], ALU.mult, [etb[0], etb[1]], aTb)
                if qt + 1 < 4:
                    self.norm_tiles(ps, [4 * (qt + 1) + j for j in range(4)], d["gfm"][l, 1], hT2, hT2b, pst, pstb, tagn)
                for tt_ in range(4):
                    for hf in range(2):
                        for cp in range(NCP):
                            kb.mm(pf[hf][:], aT[:, cp, tt_ * 128:(tt_ + 1) * 128], wd[:, cp, hf * 512:(hf + 1) * 512],
                                  cp == 0, cp == NCP - 1, [aTb[cp], wdb[cp // 11]], [pfb[hf]])
                    self.post_norm_add(4 * qt + tt_, pf, pfb, gbc, gbcb, tmp, "dve")
            kb.barrier()


_PROG_CACHE = {}


def kernel(**inputs):
    inp = {k: np.asarray(v) for k, v in inputs.items()}
    W = prep_weights(inp)
    cb, cf = const_tables()
    if "prog" not in _PROG_CACHE:
        _PROG_CACHE["prog"] = Prog(nb=NB, layers=tuple(range(DEPTH)))
    prog = _PROG_CACHE["prog"]
    x = np.ascontiguousarray(inp["x"], dtype=np.float32)
    pos = np.ascontiguousarray(inp["positions"]).astype(np.int32)
    in_maps = []
    for c in range(N_CORES):
        m = dict(W)
        m["cstb"] = cb
        m["cstf"] = cf
        m["x"] = np.ascontiguousarray(x[c * NB:(c + 1) * NB])
        m["pos"] = np.ascontiguousarray(pos[c * NB:(c + 1) * NB])
        in_maps.append(m)
    res = run_bass_kernel_spmd(prog.nc, in_maps, core_ids=list(range(N_CORES)))
    out = np.concatenate([np.asarray(r["y"]) for r in res.results], axis=0)
    return out.astype(np.float32)
```
